# Optimizing a Trainium2 kernel written in Bass

```python
import jax, jax.numpy as jnp
from jax import lax
import numpy as np

D_MODEL = 1024
BATCH = 32
SEQ = 2048
DEPTH = 4

GRID_W = 64
CTX_LEN = 256
N_MIXERS = 3
N_GDN = (DEPTH + 2) // 3
N_S5 = (DEPTH + 1) // 3
N_RWKV = DEPTH // 3
N_MOD = 6
FFN_HIDDEN = ((8 * D_MODEL // 3 + 255) // 256) * 256
NORM_EPS = 1e-6

GDN_HEAD_DIM = 128
GDN_QK_HEADS = D_MODEL // 128
GDN_V_HEADS = 2 * GDN_QK_HEADS
GDN_KEY_DIM = GDN_QK_HEADS * GDN_HEAD_DIM
GDN_VAL_DIM = GDN_V_HEADS * GDN_HEAD_DIM
GDN_QKV_DIM = 2 * GDN_KEY_DIM + GDN_VAL_DIM
GDN_CONV = 5
GDN_CHUNK = 64

S5_GROUP = 16
S5_GROUPS = D_MODEL // S5_GROUP
S5_STATE = 64

RWKV_HEAD = 64
RWKV_HEADS = D_MODEL // RWKV_HEAD
RWKV_DECAY_LORA = 64
RWKV_A_LORA = 64
RWKV_GATE_LORA = 128
RWKV_GN_EPS = 64e-5

kernel_name = 'hybrid_gdn_s5_rwkv7_prefix_dit'

f32 = jnp.float32


def rmsnorm(x, w):
    xf = x.astype(f32)
    y = xf * lax.rsqrt(jnp.mean(xf * xf, axis=-1, keepdims=True) + NORM_EPS)
    return (y * w.astype(f32)).astype(x.dtype)


def l2norm(x):
    xf = x.astype(f32)
    return (xf * lax.rsqrt(jnp.sum(xf * xf, axis=-1, keepdims=True) + 1e-6)).astype(x.dtype)


def rev(t):
    return jnp.flip(t, axis=1)


def ident(t):
    return t


def swiglu(h, w_in, w_out):
    gu = h @ w_in
    return (jax.nn.silu(gu[..., :FFN_HIDDEN]) * gu[..., FFN_HIDDEN:]) @ w_out


def centred_depthwise_conv(x, w):
    pad = w.shape[0] // 2
    return lax.conv_general_dilated(x, w[:, None, :].astype(x.dtype), window_strides=(1,),
                                    padding=[(pad, pad)], dimension_numbers=('NWC', 'WIO', 'NWC'),
                                    feature_group_count=x.shape[-1])


def gated_delta_chunked(q, k, v, g, beta, s0):
    B, T, H, DK = q.shape
    DV = v.shape[-1]
    C = GDN_CHUNK
    n = T // C

    def blocks(t):
        t = t.astype(f32).reshape((B, n, C, H) + t.shape[3:])
        return jnp.moveaxis(t, 3, 1)

    q, k, v, g, beta = blocks(q), blocks(k), blocks(v), blocks(g), blocks(beta)
    gcum = jnp.cumsum(g, axis=-1)
    incl = jnp.tril(jnp.ones((C, C), bool))
    strict = jnp.tril(jnp.ones((C, C), bool), -1)
    diff = gcum[..., :, None] - gcum[..., None, :]
    decay = jnp.where(incl, jnp.exp(jnp.where(incl, diff, 0.0)), 0.0)
    k_beta = k * beta[..., None]
    v_beta = v * beta[..., None]
    m = jnp.where(strict, jnp.einsum('bhnik,bhnjk->bhnij', k_beta, k) * decay, 0.0)
    eye = jnp.eye(C, dtype=f32)
    t_inv = lax.linalg.triangular_solve(eye + m, jnp.broadcast_to(eye, m.shape),
                                        left_side=True, lower=True, unit_diagonal=True)
    u = jnp.einsum('bhnij,bhnjd->bhnid', t_inv, v_beta)
    w = jnp.einsum('bhnij,bhnjk->bhnik', t_inv, k_beta * jnp.exp(gcum)[..., None])
    attn = jnp.where(incl, jnp.einsum('bhnik,bhnjk->bhnij', q, k) * decay, 0.0)
    q_dec = q * jnp.exp(gcum)[..., None]
    k_dec = k * jnp.exp(gcum[..., -1:] - gcum)[..., None]
    g_tot = jnp.exp(gcum[..., -1])

    def step(S, xs):
        q_c, k_c, u_c, w_c, a_c, gt = xs
        v_new = u_c - jnp.einsum('bhck,bhkd->bhcd', w_c, S)
        o_c = jnp.einsum('bhck,bhkd->bhcd', q_c, S) + jnp.einsum('bhij,bhjd->bhid', a_c, v_new)
        S = S * gt[..., None, None] + jnp.einsum('bhck,bhcd->bhkd', k_c, v_new)
        return S, o_c

    xs = tuple(jnp.moveaxis(t, 2, 0) for t in (q_dec, k_dec, u, w, attn, g_tot))
    s_fin, o = lax.scan(step, s0.astype(f32), xs)
    o = jnp.transpose(o, (1, 0, 3, 2, 4)).reshape(B, T, H, DV)
    return o, s_fin


def gdn_mixer(hc, hx, w_qkvz, conv_w, w_ab, a_log, dt_bias, norm_w, w_out):
    rep = GDN_V_HEADS // GDN_QK_HEADS

    def feats(h):
        B, T, _ = h.shape
        y = h @ w_qkvz
        qkv = jax.nn.silu(centred_depthwise_conv(y[..., :GDN_QKV_DIM], conv_w))
        z = y[..., GDN_QKV_DIM:].reshape(B, T, GDN_V_HEADS, GDN_HEAD_DIM)
        q = l2norm(qkv[..., :GDN_KEY_DIM].reshape(B, T, GDN_QK_HEADS, GDN_HEAD_DIM)) * GDN_HEAD_DIM ** -0.5
        k = l2norm(qkv[..., GDN_KEY_DIM:2 * GDN_KEY_DIM].reshape(B, T, GDN_QK_HEADS, GDN_HEAD_DIM))
        v = qkv[..., 2 * GDN_KEY_DIM:].reshape(B, T, GDN_V_HEADS, GDN_HEAD_DIM)
        q = jnp.repeat(q, rep, axis=2)
        k = jnp.repeat(k, rep, axis=2)
        ab = jnp.einsum('btd,rdh->rbth', h, w_ab)
        g = -jnp.exp(a_log.astype(f32))[:, None, None, :] * jax.nn.softplus(
            (ab[..., :GDN_V_HEADS] + dt_bias[:, None, None, :]).astype(f32))
        beta = jax.nn.sigmoid(ab[..., GDN_V_HEADS:].astype(f32))
        return q, k, v, z, g, beta

    qc, kc, vc, zc, gc, bc = feats(hc)
    qx, kx, vx, zx, gx, bx = feats(hx)
    B = hx.shape[0]
    s_zero = jnp.zeros((B, GDN_V_HEADS, GDN_HEAD_DIM, GDN_HEAD_DIM), f32)
    oc_dirs, ox_dirs = [], []
    for d in range(2):
        tr = rev if d == 1 else ident
        oc, sc = gated_delta_chunked(tr(qc), tr(kc), tr(vc), tr(gc[d]), tr(bc[d]), s_zero)
        ox, _ = gated_delta_chunked(tr(qx), tr(kx), tr(vx), tr(gx[d]), tr(bx[d]), sc)
        oc_dirs.append(tr(oc))
        ox_dirs.append(tr(ox))

    def finish(o, z, h):
        B_, T_, _ = h.shape
        y = rmsnorm(o.astype(h.dtype), norm_w) * jax.nn.silu(z)
        return y.reshape(B_, T_, GDN_VAL_DIM) @ w_out

    return (finish(oc_dirs[0] + oc_dirs[1], zc, hc), finish(ox_dirs[0] + ox_dirs[1], zx, hx))


def s5_discretize(lam_re, lam_im, log_dt, b_re, b_im):
    dt = jnp.exp(log_dt.astype(f32))[:, None]
    lr, li = lam_re.astype(f32), lam_im.astype(f32)
    mag = jnp.exp(dt * lr)
    ar, ai = mag * jnp.cos(dt * li), mag * jnp.sin(dt * li)
    den = lr * lr + li * li
    fr = ((ar - 1.0) * lr + ai * li) / den
    fi = (ai * lr - (ar - 1.0) * li) / den
    br, bi = b_re.astype(f32), b_im.astype(f32)
    bbr = fr[..., None] * br - fi[..., None] * bi
    bbi = fr[..., None] * bi + fi[..., None] * br
    return ar, ai, bbr, bbi


def _complex_linear_combine(e1, e2):
    a1r, a1i, b1r, b1i = e1
    a2r, a2i, b2r, b2i = e2
    return (a1r * a2r - a1i * a2i, a1r * a2i + a1i * a2r,
            a2r * b1r - a2i * b1i + b2r, a2r * b1i + a2i * b1r + b2i)


def s5_scan(u, ar, ai, bbr, bbi, s0r, s0i):
    T = u.shape[1]
    bur = jnp.einsum('gpc,btgc->tbgp', bbr, u)
    bui = jnp.einsum('gpc,btgc->tbgp', bbi, u)
    bur = bur.at[0].add(ar * s0r - ai * s0i)
    bui = bui.at[0].add(ar * s0i + ai * s0r)
    a_r = jnp.broadcast_to(ar, (T, 1) + ar.shape)
    a_i = jnp.broadcast_to(ai, (T, 1) + ai.shape)
    _, _, sr, si = lax.associative_scan(_complex_linear_combine, (a_r, a_i, bur, bui), axis=0)
    return sr, si


def s5_mixer(hc, hx, lam_re, lam_im, log_dt, b_re, b_im, c_re, c_im, d_skip, w_glu, b_glu):
    def group(h):
        B_, T_, _ = h.shape
        return h.astype(f32).reshape(B_, T_, S5_GROUPS, S5_GROUP)

    uc, ux = group(hc), group(hx)
    B = hx.shape[0]
    s_zero = jnp.zeros((B, S5_GROUPS, S5_STATE), f32)
    yc_dirs, yx_dirs = [], []
    for d in range(2):
        tr = rev if d == 1 else ident
        ar, ai, bbr, bbi = s5_discretize(lam_re[d], lam_im[d], log_dt[d], b_re[d], b_im[d])
        cr, ci = c_re[d].astype(f32), c_im[d].astype(f32)
        scr, sci = s5_scan(tr(uc), ar, ai, bbr, bbi, s_zero, s_zero)
        sxr, sxi = s5_scan(tr(ux), ar, ai, bbr, bbi, scr[-1], sci[-1])
        yc_dirs.append(tr(jnp.einsum('gcp,tbgp->btgc', cr, scr) - jnp.einsum('gcp,tbgp->btgc', ci, sci)))
        yx_dirs.append(tr(jnp.einsum('gcp,tbgp->btgc', cr, sxr) - jnp.einsum('gcp,tbgp->btgc', ci, sxi)))
    dk = d_skip.astype(f32).reshape(S5_GROUPS, S5_GROUP)

    def glu(y, u, h):
        y = jax.nn.gelu((y + dk * u).reshape(h.shape).astype(h.dtype))
        vg = y @ w_glu + b_glu
        return vg[..., :D_MODEL] * jax.nn.sigmoid(vg[..., D_MODEL:])

    return (glu(yc_dirs[0] + yc_dirs[1], uc, hc), glu(yx_dirs[0] + yx_dirs[1], ux, hx))


def grid_quad_shift(h):
    B, T, D = h.shape
    rows = T // GRID_W
    q = D // 4
    g = h.reshape(B, rows, GRID_W, D)
    left = jnp.pad(g[:, :, :-1, :q], ((0, 0), (0, 0), (1, 0), (0, 0)))
    right = jnp.pad(g[:, :, 1:, q:2 * q], ((0, 0), (0, 0), (0, 1), (0, 0)))
    up = jnp.pad(g[:, :-1, :, 2 * q:3 * q], ((0, 0), (1, 0), (0, 0), (0, 0)))
    down = jnp.pad(g[:, 1:, :, 3 * q:], ((0, 0), (0, 1), (0, 0), (0, 0)))
    return jnp.concatenate([left, right, up, down], axis=-1).reshape(B, T, D)


def seq_bi_shift(h):
    half = h.shape[-1] // 2
    prev = jnp.pad(h[:, :-1, :half], ((0, 0), (1, 0), (0, 0)))
    nxt = jnp.pad(h[:, 1:, half:], ((0, 0), (0, 1), (0, 0)))
    return jnp.concatenate([prev, nxt], axis=-1)


def rwkv7_scan(r, w, k, v, kk, a, s0):
    def step(S, xs):
        r_t, w_t, k_t, v_t, kk_t, a_t = xs
        sa = jnp.einsum('bhvk,bhk->bhv', S, kk_t)
        S = (S * w_t[:, :, None, :] - sa[..., None] * (kk_t * a_t)[:, :, None, :]
             + v_t[..., None] * k_t[:, :, None, :])
        return S, jnp.einsum('bhvk,bhk->bhv', S, r_t)

    xs = tuple(jnp.moveaxis(t, 1, 0) for t in (r, w, k, v, kk, a))
    s_fin, y = lax.scan(step, s0, xs)
    return jnp.moveaxis(y, 0, 1), s_fin


def rwkv7_mixer(hc, hx, mu, w_rkv, w0, w1, w2, a0, a1, a2, g1, g2, k_k, k_a, r_k, ln_w, ln_b, w_out):
    def feats(h, shifted):
        B_, T_, _ = h.shape

        def heads(t):
            return t.astype(f32).reshape(B_, T_, RWKV_HEADS, RWKV_HEAD)

        dx = shifted - h
        xr, xw, xk, xv, xa, xg = [h + dx * mu[j] for j in range(6)]
        r = xr @ w_rkv[0]
        k = xk @ w_rkv[1]
        v = xv @ w_rkv[2]
        g = jax.nn.sigmoid(xg @ g1) @ g2
        kk = l2norm(heads(k * k_k))
        dirs = []
        for d in range(2):
            logw = -jax.nn.softplus(-(w0[d] + jnp.tanh(xw @ w1[d]) @ w2[d])) - 0.5
            decay = jnp.exp(-jnp.exp(logw.astype(f32)))
            a = jax.nn.sigmoid(a0[d] + (xa @ a1[d]) @ a2[d])
            k_d = k * (1.0 + (a - 1.0) * k_a)
            dirs.append((heads(decay), heads(k_d), heads(a)))
        return heads(r), heads(v), kk, g, dirs

    rc, vc, kkc, gc, dirs_c = feats(hc, seq_bi_shift(hc))
    rx, vx, kkx, gx, dirs_x = feats(hx, grid_quad_shift(hx))
    B = hx.shape[0]
    s_zero = jnp.zeros((B, RWKV_HEADS, RWKV_HEAD, RWKV_HEAD), f32)
    rk = r_k.astype(f32)
    yc_list, yx_list, bc_list, bx_list = [], [], [], []
    for d in range(2):
        tr = rev if d == 1 else ident
        wc_, kc_, ac_ = dirs_c[d]
        wx_, kx_, ax_ = dirs_x[d]
        yc_d, sc = rwkv7_scan(tr(rc), tr(wc_), tr(kc_), tr(vc), tr(kkc), tr(ac_), s_zero)
        yx_d, _ = rwkv7_scan(tr(rx), tr(wx_), tr(kx_), tr(vx), tr(kkx), tr(ax_), sc)
        yc_list.append(tr(yc_d))
        yx_list.append(tr(yx_d))
        bc_list.append(jnp.sum(rc * kc_ * rk, axis=-1, keepdims=True) * vc)
        bx_list.append(jnp.sum(rx * kx_ * rk, axis=-1, keepdims=True) * vx)
    lw = ln_w.astype(f32).reshape(RWKV_HEADS, RWKV_HEAD)
    lb = ln_b.astype(f32).reshape(RWKV_HEADS, RWKV_HEAD)

    def finish(y, bonus, g, h):
        mean = jnp.mean(y, axis=-1, keepdims=True)
        var = jnp.mean(jnp.square(y - mean), axis=-1, keepdims=True)
        gn = (y - mean) * lax.rsqrt(var + RWKV_GN_EPS) * lw + lb
        out = (gn + bonus).reshape(h.shape).astype(h.dtype)
        return (out * g) @ w_out

    return (finish(yc_list[0] + yc_list[1], bc_list[0] + bc_list[1], gc, hc),
            finish(yx_list[0] + yx_list[1], bx_list[0] + bx_list[1], gx, hx))


def setup_inputs(seed: int = 0) -> dict:
    key = jax.random.key(seed)
    subkeys = jax.random.split(key, 64)
    counter = iter(range(64))
    D = D_MODEL

    def nrm(shape, scale):
        return scale * jax.random.normal(subkeys[next(counter)], shape, f32)

    def unif(shape, lo, hi):
        return jax.random.uniform(subkeys[next(counter)], shape, f32, lo, hi)

    dt = jnp.exp(unif((N_GDN, 2, GDN_V_HEADS), float(np.log(1e-3)), float(np.log(1e-1))))
    return {
        'x': nrm((BATCH, SEQ, D), 1.0),
        'c': nrm((BATCH, D), 1.0),
        'ctx': nrm((BATCH, CTX_LEN, D), 1.0),
        'c_ctx': nrm((D,), 0.5),
        'mod_w': nrm((DEPTH, D, N_MOD * D), 0.5 * D ** -0.5),
        'mod_b': nrm((DEPTH, N_MOD * D), 0.02),
        'norm_mix': 1.0 + nrm((DEPTH, D), 0.02),
        'norm_ffn': 1.0 + nrm((DEPTH, D), 0.02),
        'ffn_w_in': nrm((DEPTH, D, 2 * FFN_HIDDEN), D ** -0.5),
        'ffn_w_out': nrm((DEPTH, FFN_HIDDEN, D), FFN_HIDDEN ** -0.5),
        'final_norm': 1.0 + nrm((D,), 0.02),
        'gdn_w_qkvz': nrm((N_GDN, D, GDN_QKV_DIM + GDN_VAL_DIM), D ** -0.5),
        'gdn_conv': nrm((N_GDN, GDN_CONV, GDN_QKV_DIM), GDN_CONV ** -0.5),
        'gdn_w_ab': nrm((N_GDN, 2, D, 2 * GDN_V_HEADS), 0.5 * D ** -0.5),
        'gdn_a_log': jnp.log(unif((N_GDN, 2, GDN_V_HEADS), 1.0, 16.0)),
        'gdn_dt_bias': dt + jnp.log(-jnp.expm1(-dt)),
        'gdn_norm': 1.0 + nrm((N_GDN, GDN_HEAD_DIM), 0.02),
        'gdn_w_out': nrm((N_GDN, GDN_VAL_DIM, D), GDN_VAL_DIM ** -0.5),
        's5_lambda_re': -0.5 + nrm((N_S5, 2, S5_GROUPS, S5_STATE), 0.01),
        's5_lambda_im': jnp.pi * jnp.arange(S5_STATE, dtype=f32) + nrm((N_S5, 2, S5_GROUPS, S5_STATE), 0.01),
        's5_log_dt': unif((N_S5, 2, S5_GROUPS), float(np.log(1e-3)), float(np.log(1e-1))),
        's5_b_re': nrm((N_S5, 2, S5_GROUPS, S5_STATE, S5_GROUP), (2 * S5_GROUP) ** -0.5),
        's5_b_im': nrm((N_S5, 2, S5_GROUPS, S5_STATE, S5_GROUP), (2 * S5_GROUP) ** -0.5),
        's5_c_re': nrm((N_S5, 2, S5_GROUPS, S5_GROUP, S5_STATE), S5_STATE ** -0.5),
        's5_c_im': nrm((N_S5, 2, S5_GROUPS, S5_GROUP, S5_STATE), S5_STATE ** -0.5),
        's5_d': nrm((N_S5, D), 1.0),
        's5_w_glu': nrm((N_S5, D, 2 * D), D ** -0.5),
        's5_b_glu': nrm((N_S5, 2 * D), 0.02),
        'rwkv_mu': unif((N_RWKV, 6, D), 0.0, 1.0),
        'rwkv_w_rkv': nrm((N_RWKV, 3, D, D), D ** -0.5),
        'rwkv_w0': unif((N_RWKV, 2, D), -5.0, 1.0),
        'rwkv_w1': nrm((N_RWKV, 2, D, RWKV_DECAY_LORA), D ** -0.5),
        'rwkv_w2': nrm((N_RWKV, 2, RWKV_DECAY_LORA, D), 0.5 * RWKV_DECAY_LORA ** -0.5),
        'rwkv_a0': nrm((N_RWKV, 2, D), 0.5),
        'rwkv_a1': nrm((N_RWKV, 2, D, RWKV_A_LORA), D ** -0.5),
        'rwkv_a2': nrm((N_RWKV, 2, RWKV_A_LORA, D), 0.5 * RWKV_A_LORA ** -0.5),
        'rwkv_g1': nrm((N_RWKV, D, RWKV_GATE_LORA), D ** -0.5),
        'rwkv_g2': nrm((N_RWKV, RWKV_GATE_LORA, D), RWKV_GATE_LORA ** -0.5),
        'rwkv_k_k': 0.85 + nrm((N_RWKV, D), 0.05),
        'rwkv_k_a': 1.0 + nrm((N_RWKV, D), 0.05),
        'rwkv_r_k': nrm((N_RWKV, RWKV_HEADS, RWKV_HEAD), 0.1),
        'rwkv_ln_w': 1.0 + nrm((N_RWKV, D), 0.02),
        'rwkv_ln_b': nrm((N_RWKV, D), 0.02),
        'rwkv_w_out': nrm((N_RWKV, D, D), D ** -0.5),
    }


def reference(x, c, ctx, c_ctx, mod_w, mod_b, norm_mix, norm_ffn, ffn_w_in, ffn_w_out, final_norm,
              gdn_w_qkvz, gdn_conv, gdn_w_ab, gdn_a_log, gdn_dt_bias, gdn_norm, gdn_w_out,
              s5_lambda_re, s5_lambda_im, s5_log_dt, s5_b_re, s5_b_im, s5_c_re, s5_c_im, s5_d,
              s5_w_glu, s5_b_glu,
              rwkv_mu, rwkv_w_rkv, rwkv_w0, rwkv_w1, rwkv_w2, rwkv_a0, rwkv_a1, rwkv_a2, rwkv_g1,
              rwkv_g2, rwkv_k_k, rwkv_k_a, rwkv_r_k, rwkv_ln_w, rwkv_ln_b, rwkv_w_out):
    xc = ctx
    silu_c = jax.nn.silu(c)
    silu_cc = jax.nn.silu(c_ctx)
    for i in range(DEPTH):
        mod_x = jnp.split((silu_c @ mod_w[i] + mod_b[i])[:, None, :], N_MOD, axis=-1)
        mod_c = jnp.split(silu_cc @ mod_w[i] + mod_b[i], N_MOD, axis=-1)
        hx = rmsnorm(x, norm_mix[i]) * (1.0 + mod_x[1]) + mod_x[0]
        hc = rmsnorm(xc, norm_mix[i]) * (1.0 + mod_c[1]) + mod_c[0]
        kind, j = i % N_MIXERS, i // N_MIXERS
        if kind == 0:
            yc, yx = gdn_mixer(hc, hx, gdn_w_qkvz[j], gdn_conv[j], gdn_w_ab[j], gdn_a_log[j],
                               gdn_dt_bias[j], gdn_norm[j], gdn_w_out[j])
        elif kind == 1:
            yc, yx = s5_mixer(hc, hx, s5_lambda_re[j], s5_lambda_im[j], s5_log_dt[j], s5_b_re[j],
                              s5_b_im[j], s5_c_re[j], s5_c_im[j], s5_d[j], s5_w_glu[j], s5_b_glu[j])
        else:
            yc, yx = rwkv7_mixer(hc, hx, rwkv_mu[j], rwkv_w_rkv[j], rwkv_w0[j], rwkv_w1[j], rwkv_w2[j],
                                 rwkv_a0[j], rwkv_a1[j], rwkv_a2[j], rwkv_g1[j], rwkv_g2[j],
                                 rwkv_k_k[j], rwkv_k_a[j], rwkv_r_k[j], rwkv_ln_w[j], rwkv_ln_b[j],
                                 rwkv_w_out[j])
        x = x + mod_x[2] * yx
        hx = rmsnorm(x, norm_ffn[i]) * (1.0 + mod_x[4]) + mod_x[3]
        x = x + mod_x[5] * swiglu(hx, ffn_w_in[i], ffn_w_out[i])
        if i < DEPTH - 1:
            xc = xc + mod_c[2] * yc
            hc = rmsnorm(xc, norm_ffn[i]) * (1.0 + mod_c[4]) + mod_c[3]
            xc = xc + mod_c[5] * swiglu(hc, ffn_w_in[i], ffn_w_out[i])
    return rmsnorm(x, final_norm)
```

```python
import numpy as np
import concourse.bass as bass
import concourse.mybir as mybir
from concourse.bass_utils import run_bass_kernel_spmd

F32 = mybir.dt.float32
BF16 = mybir.dt.bfloat16
AF = mybir.ActivationFunctionType
ALU = mybir.AluOpType

D = 1024
KT = 8
T_CTX = 256
T_X = 2048
T = T_CTX + T_X
DEPTH = 4
FFN_H = 2816
FT = FFN_H // 128
NCORES = 8
NSEQ = 4
EPS = 1e-6
CHUNKS = [(0, 256, True), (256, 512, False), (768, 512, False), (1280, 512, False), (1792, 512, False)]
HALVES = [[0, 1, 2], [3, 4]]


class Buf:
    __slots__ = ("name", "w", "r", "excl")

    def __init__(self, name="", excl=False):
        self.name = name
        self.w = None
        self.r = {}
        self.excl = excl


class Sched:
    ENG = ("pe", "act", "dve", "pool", "sp")

    def __init__(self, nc, ndma=10):
        self.nc = nc
        self.eng = dict(pe=nc.tensor, act=nc.scalar, dve=nc.vector, pool=nc.gpsimd, sp=nc.sync)
        self.sems = {}
        self.cnt = {}
        for e in self.ENG:
            self.sems[e] = nc.alloc_semaphore("s_" + e)
            self.cnt[e] = 0
        self.seen = {e: {} for e in self.ENG}
        self.dq = {}
        for q in ("sp", "pool", "act"):
            keys = []
            for i in range(ndma):
                k = "d_%s%d" % (q, i)
                self.sems[k] = nc.alloc_semaphore(k)
                self.cnt[k] = 0
                keys.append(k)
            self.dq[q] = [keys, 0]
        self.nops = 0

    def _wait(self, e, key, val):
        if val <= 0 or self.seen[e].get(key, 0) >= val:
            return
        self.eng[e].wait_ge(self.sems[key], val)
        self.seen[e][key] = val

    def _deps(self, e, reads, writes):
        need = {}
        for b in reads:
            if b.w is not None and b.w[1] > need.get(b.w[0], 0):
                need[b.w[0]] = b.w[1]
        for b in writes:
            if b.w is not None and b.w[1] > need.get(b.w[0], 0):
                need[b.w[0]] = b.w[1]
            for k, v in b.r.items():
                if v > need.get(k, 0):
                    need[k] = v
        for k, v in need.items():
            self._wait(e, k, v)

    def op(self, e, fn, reads=(), writes=()):
        xr = [b for b in reads if b.excl]
        if xr:
            writes = list(writes) + xr
            reads = [b for b in reads if not b.excl]
        self._deps(e, reads, writes)
        ins = fn(self.eng[e])
        self.cnt[e] += 1
        ins.then_inc(self.sems[e], 1)
        v = self.cnt[e]
        for b in reads:
            b.r[e] = v
        for b in writes:
            b.w = (e, v)
            b.r = {}
        self.nops += 1

    def dma(self, q, out, in_, reads=(), writes=()):
        keys, idx = self.dq[q]
        k = keys[idx % len(keys)]
        self.dq[q][1] += 1
        self._deps(q, reads, writes)
        self._wait(q, k, self.cnt[k])
        self.eng[q].dma_start(out=out, in_=in_).then_inc(self.sems[k], 16)
        self.cnt[k] += 16
        v = self.cnt[k]
        for b in reads:
            b.r[k] = v
        for b in writes:
            b.w = (k, v)
            b.r = {}
        self.nops += 1

    def barrier(self, engines=None):
        for e in (engines or self.ENG):
            for k, v in self.cnt.items():
                self._wait(e, k, v)

    def final_wait(self, e="sp"):
        for k, v in self.cnt.items():
            self._wait(e, k, v)


class Ctx:
    pass


_UID = [0]


def un(name):
    _UID[0] += 1
    return "%s_%d" % (name, _UID[0])


def mm(S, out_ap, pairs, reads, writes):
    n = len(pairs)

    def fn(pe):
        ins = None
        for i, (l, r) in enumerate(pairs):
            ins = pe.matmul(out_ap, l, r, start=(i == 0), stop=(i == n - 1))
        return ins
    S.op("pe", fn, reads, writes)


def build_program(nseq=NSEQ, depth=DEPTH, mixers=True, debug=None, stop=None):
    nc = bass.Bass("TRN2", target_bir_lowering=False)
    C = Ctx()
    C.stop = stop
    import os as _os
    C.stop2 = int(_os.environ.get('STOP2', '0'))
    C.x0eng = _os.environ.get('X0ENG', 'pool')
    C.nh = int(_os.environ.get('NH', '2'))
    C.nlev = int(_os.environ.get('NLEV', '6'))
    C.nc = nc
    C.nseq = nseq
    S = Sched(nc)
    C.S = S

    def din(name, shape, dt=F32):
        return nc.dram_tensor(name, list(shape), dt, kind="ExternalInput").ap()

    C.xT = din("xT", [nseq, KT, 128, T])
    C.cT = din("cT", [128, KT, NSEQ + 1])
    C.mod_w = din("mod_w_t", [DEPTH, 48, 128, KT, 128])
    C.mod_b = din("mod_b_t", [128, DEPTH, 48])
    C.norm_mix = din("norm_mix_t", [128, DEPTH, KT])
    C.norm_ffn = din("norm_ffn_t", [128, DEPTH, KT])
    C.final_norm = din("final_norm_t", [128, KT])
    C.ffn_w_in = din("ffn_w_in_t", [DEPTH, 2 * FT, 128, KT, 128])
    C.ffn_w_out = din("ffn_w_out_t", [DEPTH, KT, 128, FT, 128])
    C.outT = nc.dram_tensor("outT", [nseq, KT, 128, T_X], F32, kind="ExternalOutput").ap()
    if debug:
        C.dbg = nc.dram_tensor("dbg", [KT, 128, T], F32, kind="ExternalOutput").ap()
        C.dbgY = nc.dram_tensor("dbgY", [16, 128, T], BF16, kind="ExternalOutput").ap()

    sb = nc.alloc_sbuf_tensor
    C.H = sb("H", [128, KT, T], BF16)
    C.Hb = [[Buf("H%d_%d" % (k, c)) for c in range(5)] for k in range(KT)]
    C.ones = sb("ones", [128, 128], F32)
    C.ones_b = Buf("ones")
    C.MODS = sb("MODS", [128, DEPTH, 48, NSEQ + 1], F32)
    C.MODS_b = Buf("MODS")
    C.AM = sb("AM", [128, DEPTH, 2, KT, NSEQ + 1], F32)
    C.AM_b = Buf("AM")
    C.nrm = sb("nrm", [128, 2, DEPTH, KT], F32)
    C.fnrm = sb("fnrm", [128, KT], F32)
    C.nrm_b = Buf("nrm")
    C.modb = sb("modb", [128, DEPTH, 48], F32)
    C.scT = sb("scT", [128, KT, NSEQ + 1], F32)
    C.scT_b = Buf("scT")
    C.tmpf = [sb("tmpf%d" % i, [128, 512], F32) for i in range(3)]
    C.tmpf_b = [Buf("tmpf%d" % i) for i in range(3)]
    C.tmpf_i = 0
    C.rs = sb("rs", [128, 512], F32)
    C.rs_b = Buf("rs")
    C.ps = [nc.alloc_psum_tensor("ps%d" % i, [128, 512], F32) for i in range(8)]
    C.psbf = [p[:].bitcast(BF16) for p in C.ps]
    C.ps_b = [Buf("ps%d" % i, excl=True) for i in range(8)]
    gdn_declare(C)
    s5_declare(C)
    rwkv_declare(C)

    prologue(C)
    consts_load(C)
    if mixers and depth > 1:
        s5_setup(C)
    for s in range(nseq):
        src = C.xT[s]
        for l in range(depth):
            kind, j = l % 3, l // 3
            with r_scope(C, src):
                norm_modulate(C, l, 0, s)
            src = C.Rscr
            if mixers:
                if kind == 0:
                    gdn_mixer(C, l, j, s)
                    if debug and debug[1] == l and s == 0:
                        S.barrier()
                        for f in range(16):
                            S.dma("sp", C.dbgY[f], C.Yscr[f], (), ())
                        S.barrier()
                elif kind == 1:
                    s5_mixer(C, l, s)
                else:
                    rwkv_mixer(C, l, s)
            with r_scope(C, src):
                if mixers:
                    if kind == 0:
                        gdn_out_proj(C, l, j, s)
                    elif kind == 1:
                        apply_mixer_out(C, l, s)
                    else:
                        rwkv_out_proj(C, l, s)
                if debug == ("xmix", l) and s == 0:
                    dump_R(C)
                norm_modulate(C, l, 1, s)
                ffn(C, l, s)
                if debug == ("x", l) and s == 0:
                    dump_R(C)
                if l == depth - 1:
                    final_norm_store(C, s)
    S.final_wait("sp")
    return nc


class r_scope:
    def __init__(self, C, src):
        self.C = C
        self.src = src

    def __enter__(self):
        C = self.C
        C.S.barrier()
        self.g = C.nc.sbuf_tensor(un("R"), [128, KT, T], F32)
        C.R = self.g.__enter__()
        C.Rb = [[Buf("R%d_%d" % (k, c)) for c in range(5)] for k in range(KT)]
        for k in range(KT):
            C.S.dma("sp" if k % 2 == 0 else "act", C.R[:, k, :], self.src[k], (), C.Rb[k])
        return self

    def __exit__(self, *a):
        C = self.C
        for k in range(KT):
            C.S.dma("sp" if k % 2 == 0 else "act", C.Rscr[k], C.R[:, k, :], C.Rb[k], ())
        C.S.barrier()
        self.g.__exit__(*a)
        C.R = None
        return False


def next_tmpf(C):
    i = C.tmpf_i % len(C.tmpf)
    C.tmpf_i += 1
    return C.tmpf[i], C.tmpf_b[i]


def dump_R(C):
    S = C.S
    for k in range(KT):
        S.dma("sp", C.dbg[k], C.R[:, k, :], reads=C.Rb[k], writes=())


def prologue(C):
    nc, S = C.nc, C.S
    S.op("dve", lambda e: e.memset(C.ones[:], 1.0), (), [C.ones_b])
    S.dma("sp", C.scT[:], C.cT, (), [C.scT_b])
    S.dma("sp", C.modb[:], C.mod_b, (), [C.nrm_b])
    S.dma("sp", C.nrm[:, 0], C.norm_mix, (), [C.nrm_b])
    S.dma("sp", C.nrm[:, 1], C.norm_ffn, (), [C.nrm_b])
    S.dma("sp", C.fnrm[:], C.final_norm, (), [C.nrm_b])
    S.op("act", lambda e: e.activation(out=C.scT[:], in_=C.scT[:], func=AF.Silu), [C.scT_b], [C.scT_b])
    NW = 3
    with (nc.sbuf_tensor(un("wm0"), [128, KT, 128], F32) as wm0,
          nc.sbuf_tensor(un("wm1"), [128, KT, 128], F32) as wm1,
          nc.sbuf_tensor(un("wm2"), [128, KT, 128], F32) as wm2):
        wm = [wm0, wm1, wm2]
        wm_b = [Buf("wm%d" % i) for i in range(NW)]
        i = 0
        for l in range(DEPTH):
            for f in range(48):
                w, wb = wm[i % NW], wm_b[i % NW]
                S.dma("sp" if i % 2 == 0 else "act", w[:], C.mod_w[l, f], (), [wb])
                pb = i % 4
                mm(S, C.ps[pb][:, 0:NSEQ + 1],
                   [(w[:, k, :], C.scT[:, k, :]) for k in range(KT)],
                   [wb, C.scT_b], [C.ps_b[pb]])
                S.op("dve", lambda e, l=l, f=f, pb=pb: e.tensor_scalar(
                    out=C.MODS[:, l, f, :], in0=C.ps[pb][:, 0:NSEQ + 1], scalar1=C.modb[:, l, f:f + 1],
                    scalar2=None, op0=ALU.add), [C.ps_b[pb], C.nrm_b], [C.MODS_b])
                i += 1
        S.barrier()
    for l in range(DEPTH):
        for sub, j in ((0, 1), (1, 4)):
            S.op("dve", lambda e, l=l, sub=sub, j=j: e.scalar_tensor_tensor(
                out=C.AM[:, l, sub], in0=C.MODS[:, l, j * 8:(j + 1) * 8, :], scalar=1.0,
                in1=C.nrm[:, sub, l, :].unsqueeze(2).to_broadcast([128, KT, NSEQ + 1]),
                op0=ALU.add, op1=ALU.mult), [C.MODS_b, C.nrm_b], [C.AM_b])
    S.barrier()


def load_seq(C, s):
    S = C.S
    for k in range(KT):
        S.dma("sp" if k % 2 == 0 else "act", C.R[:, k, :], C.xT[s, k], (), C.Rb[k])


def rstd_chunk(C, c):
    S = C.S
    t0, n, _ = CHUNKS[c]
    pb = 7
    for k in range(KT):
        tf, tb = next_tmpf(C)
        S.op("act", lambda e, k=k, tf=tf: e.activation(out=tf[:, :n], in_=C.R[:, k, t0:t0 + n], func=AF.Square),
             [C.Rb[k][c]], [tb])
        S.op("pe", lambda e, k=k, tf=tf: e.matmul(C.ps[pb][:, :n], C.ones[:], tf[:, :n], start=(k == 0), stop=(k == KT - 1)),
             [tb, C.ones_b], [C.ps_b[pb]])
    S.op("act", lambda e: e.activation(out=C.rs[:, :n], in_=C.ps[pb][:, :n], func=AF.Sqrt, scale=1.0 / D, bias=EPS),
         [C.ps_b[pb]], [C.rs_b])
    S.op("dve", lambda e: e.reciprocal(out=C.rs[:, :n], in_=C.rs[:, :n]), [C.rs_b], [C.rs_b])


def norm_modulate(C, l, sub, s):
    S = C.S
    jshift = 0 if sub == 0 else 3
    for c, (t0, n, is_ctx) in enumerate(CHUNKS):
        col = NSEQ if is_ctx else s
        rstd_chunk(C, c)
        for k in range(KT):
            tf, tb = next_tmpf(C)
            S.op("dve", lambda e, k=k, tf=tf: e.tensor_tensor(out=tf[:, :n], in0=C.R[:, k, t0:t0 + n], in1=C.rs[:, :n], op=ALU.mult),
                 [C.Rb[k][c], C.rs_b], [tb])
            S.op("act", lambda e, k=k, tf=tf: e.activation(
                out=C.H[:, k, t0:t0 + n], in_=tf[:, :n], func=AF.Identity,
                scale=C.AM[:, l, sub, k, col:col + 1], bias=C.MODS[:, l, jshift * 8 + k, col:col + 1]),
                [tb, C.AM_b, C.MODS_b], [C.Hb[k][c]])


def ffn(C, l, s):
    nc, S = C.nc, C.S
    jg = 5
    S.barrier()
    with (
        nc.sbuf_tensor(un("ACTB"), [128, FT, 1280], BF16) as ACTB,
        nc.sbuf_tensor(un("wi0"), [128, 2, KT, 128], BF16) as wi0,
        nc.sbuf_tensor(un("wi1"), [128, 2, KT, 128], BF16) as wi1,
        nc.sbuf_tensor(un("wo0"), [128, FT, 128], BF16) as wo0,
        nc.sbuf_tensor(un("wo1"), [128, FT, 128], BF16) as wo1,
        nc.sbuf_tensor(un("sg0"), [128, 512], F32) as sg0,
        nc.sbuf_tensor(un("sg1"), [128, 512], F32) as sg1,
    ):
        wi, wi_b = [wi0, wi1], [Buf("wi0"), Buf("wi1")]
        wo, wo_b = [wo0, wo1], [Buf("wo0"), Buf("wo1")]
        sg, sg_b = [sg0, sg1], [Buf("sg0"), Buf("sg1")]
        it = 0
        for half in HALVES:
            hb = CHUNKS[half[0]][0]
            AB = [[Buf("AB") for _ in half] for _ in range(FT)]
            for f in range(FT):
                w, wb = wi[f % 2], wi_b[f % 2]
                S.dma("pool", w[:, 0], C.ffn_w_in[l, f], (), [wb])
                S.dma("pool", w[:, 1], C.ffn_w_in[l, FT + f], (), [wb])
                for ci, c in enumerate(half):
                    t0, n, is_ctx = CHUNKS[c]
                    pg, pu = (it % 3) * 2, (it % 3) * 2 + 1
                    hreads = [C.Hb[k][c] for k in range(KT)]
                    mm(S, C.ps[pg][:, :n], [(w[:, 0, k, :], C.H[:, k, t0:t0 + n]) for k in range(KT)],
                       [wb] + hreads, [C.ps_b[pg]])
                    mm(S, C.ps[pu][:, :n], [(w[:, 1, k, :], C.H[:, k, t0:t0 + n]) for k in range(KT)],
                       [wb] + hreads, [C.ps_b[pu]])
                    g_, gb = sg[it % 2], sg_b[it % 2]
                    S.op("act", lambda e, pg=pg, g_=g_, n=n: e.activation(out=g_[:, :n], in_=C.ps[pg][:, :n], func=AF.Silu),
                         [C.ps_b[pg]], [gb])
                    S.op("dve", lambda e, pu=pu, g_=g_, n=n, f=f, t0=t0: e.tensor_tensor(
                        out=ACTB[:, f, t0 - hb:t0 - hb + n], in0=g_[:, :n], in1=C.ps[pu][:, :n], op=ALU.mult),
                        [gb, C.ps_b[pu]], [AB[f][ci]])
                    it += 1
            for o in range(KT):
                w, wb = wo[o % 2], wo_b[o % 2]
                S.dma("pool", w[:], C.ffn_w_out[l, o], (), [wb])
                for ci, c in enumerate(half):
                    t0, n, is_ctx = CHUNKS[c]
                    col = NSEQ if is_ctx else s
                    po = 6 + (it % 2)
                    it += 1
                    mm(S, C.ps[po][:, :n], [(w[:, f, :], ACTB[:, f, t0 - hb:t0 - hb + n]) for f in range(FT)],
                       [wb] + [AB[f][ci] for f in range(FT)], [C.ps_b[po]])
                    S.op("dve", lambda e, po=po, n=n, o=o, t0=t0, col=col: e.scalar_tensor_tensor(
                        out=C.R[:, o, t0:t0 + n], in0=C.ps[po][:, :n], scalar=C.MODS[:, l, jg * 8 + o, col:col + 1],
                        in1=C.R[:, o, t0:t0 + n], op0=ALU.mult, op1=ALU.add),
                        [C.ps_b[po], C.MODS_b, C.Rb[o][c]], [C.Rb[o][c]])
        S.barrier()


def final_norm_store(C, s):
    S = C.S
    for c, (t0, n, is_ctx) in enumerate(CHUNKS):
        if is_ctx:
            continue
        rstd_chunk(C, c)
        for k in range(KT):
            tf, tb = next_tmpf(C)
            S.op("dve", lambda e, k=k, tf=tf: e.scalar_tensor_tensor(
                out=tf[:, :n], in0=C.R[:, k, t0:t0 + n], scalar=C.fnrm[:, k:k + 1], in1=C.rs[:, :n],
                op0=ALU.mult, op1=ALU.mult), [C.Rb[k][c], C.rs_b, C.nrm_b], [tb])
            S.dma("sp", C.outT[s, k, :, t0 - T_CTX:t0 - T_CTX + n], tf[:, :n], [tb], ())
    S.barrier()


def tile_tokens(tile):
    return tile * 128


def gdn_declare(C):
    nc = C.nc

    def din(name, shape, dt=F32):
        return nc.dram_tensor(name, list(shape), dt, kind="ExternalInput").ap()
    C.g_wqkvz = din("g_wqkvz_t", [2, 48, 128, KT, 128])
    C.g_conv = din("g_conv_t", [2, 128, 32, 5])
    C.g_wab = din("g_wab_t", [2, 128, KT, 128])
    C.g_par = din("g_par", [2, 128, 8])
    C.g_norm = din("g_norm_t", [2, 128, 1])
    C.g_wout = din("g_wout_t", [2, KT, 128, 16, 128])
    C.c_ident = din("c_ident", [128, 128])
    C.c_mask = din("c_mask", [2, 128, 256])
    C.Yscr = nc.dram_tensor("Yscr", [16, 128, T], BF16, kind="Internal").ap()
    sb = nc.alloc_sbuf_tensor
    C.ident = sb("ident", [128, 128], F32)
    C.identb = sb("identb", [128, 128], BF16)
    C.ident_b = Buf("ident")
    C.Yscr_b = Buf("Yscr")
    C.Rscr = nc.dram_tensor("Rscr", [KT, 128, T], F32, kind="Internal").ap()


def consts_load(C):
    S = C.S
    S.dma("sp", C.ident[:], C.c_ident, (), [C.ident_b])
    S.op("dve", lambda e: e.tensor_copy(out=C.identb[:], in_=C.ident[:]), [C.ident_b], [C.ident_b])


def gdn_mixer(C, l, j, s):
    nc, S = C.nc, C.S
    from contextlib import ExitStack
    S.barrier()
    ps = C.ps
    psb = C.ps_b
    with ExitStack() as st:
        def sbt(name, shape, dt):
            return st.enter_context(nc.sbuf_tensor(un(name), shape, dt))
        GCALL = sbt("GCALL", [128, T], F32)
        TOKP = sbt("TOKP", [128, 18, 6, 32], F32)
        GPAR = sbt("GPAR", [128, 8], F32)
        MASK = sbt("MASK", [128, 2, 256], F32)
        GNW = sbt("GNW", [128, 1], F32)
        CONVW = sbt("CONVW", [128, 32, 5], F32)
        b_GC, b_TOKP, b_par = Buf("GCALL"), Buf("TOKP"), Buf("gpar")
        S.dma("sp", GPAR[:], C.g_par[j], (), [b_par])
        S.dma("sp", MASK[:, 0, :], C.c_mask[0], (), [b_par])
        S.dma("sp", MASK[:, 1, :], C.c_mask[1], (), [b_par])
        S.dma("sp", GNW[:], C.g_norm[j], (), [b_par])
        S.dma("sp", CONVW[:], C.g_conv[j], (), [b_par])
        S.op("act", lambda e: e.activation(out=GPAR[:, 5:6], in_=GPAR[:, 2:3], func=AF.Exp), [b_par], [b_par])
        S.op("dve", lambda e: e.tensor_scalar(out=GPAR[:, 5:6], in0=GPAR[:, 5:6], scalar1=-1.0, scalar2=None, op0=ALU.mult), [b_par], [b_par])

        with ExitStack() as sa:
            def sba(name, shape, dt):
                return sa.enter_context(nc.sbuf_tensor(un(name), shape, dt))
            GL = sba("GL", [128, T], F32)
            PF = sba("PF", [128, T], F32)
            TMP = sba("TMPA", [128, T], F32)
            WAB = sba("WAB", [128, KT, 128], BF16)
            GLt = sba("GLt", [128, 128], F32)
            b_GL, b_PF, b_TMP, b_WAB, b_GLt = Buf(), Buf(), Buf(), Buf(), Buf()
            S.dma("pool", WAB[:], C.g_wab[j], (), [b_WAB])
            for c, (t0, n, _) in enumerate(CHUNKS):
                pb = c % 2
                mm(S, ps[pb][:, :n], [(WAB[:, k, :], C.H[:, k, t0:t0 + n]) for k in range(KT)],
                   [b_WAB] + [C.Hb[k][c] for k in range(KT)], [psb[pb]])
                tf, tb = next_tmpf(C)
                S.op("act", lambda e, pb=pb, tf=tf, n=n: e.activation(out=tf[:, :n], in_=ps[pb][:, :n], func=AF.Exp,
                                                                  scale=GPAR[:, 0:1], bias=GPAR[:, 1:2]), [psb[pb], b_par], [tb])
                S.op("act", lambda e, tf=tf, n=n: e.activation(out=tf[:, :n], in_=tf[:, :n], func=AF.Ln, bias=1.0), [tb], [tb])
                S.op("dve", lambda e, tf=tf, n=n, t0=t0: e.tensor_scalar(out=GL[:, t0:t0 + n], in0=tf[:, :n], scalar1=GPAR[:, 5:6],
                                                                       scalar2=None, op0=ALU.mult), [tb, b_par], [b_GL])
            for n_ in range(T // 64):
                S.op("dve", lambda e, n_=n_: e.tensor_tensor_scan(
                    out=PF[:, n_ * 64:(n_ + 1) * 64], data0=C.ones[:, 0:64], data1=GL[:, n_ * 64:(n_ + 1) * 64],
                    initial=0.0, op0=ALU.mult, op1=ALU.add), [b_GL, C.ones_b], [b_PF])
            PF3 = PF[:].rearrange("p (n c) -> p n c", c=64)
            TB = PF3[:, :, 63:64].to_broadcast([128, T // 64, 64])
            TMP3 = TMP[:].rearrange("p (n c) -> p n c", c=64)
            GL3 = GL[:].rearrange("p (n c) -> p n c", c=64)
            GC3 = GCALL[:].rearrange("p (n c) -> p n c", c=64)
            S.op("dve", lambda e: e.tensor_tensor(out=TMP3, in0=TB, in1=PF3, op=ALU.subtract), [b_PF], [b_TMP])
            S.op("dve", lambda e: e.tensor_tensor(out=TMP[:], in0=TMP[:], in1=PF[:], op=ALU.subtract), [b_TMP, b_PF], [b_TMP])
            S.op("dve", lambda e: e.tensor_tensor(out=TMP[:], in0=TMP[:], in1=GL[:], op=ALU.add), [b_TMP, b_GL], [b_TMP])
            S.op("dve", lambda e: e.scalar_tensor_tensor(out=TMP[:], in0=TMP[:], scalar=GPAR[:, 3:4], in1=PF[:],
                                                         op0=ALU.mult, op1=ALU.add), [b_TMP, b_PF, b_par], [b_TMP])
            S.op("dve", lambda e: e.scalar_tensor_tensor(out=GCALL[:], in0=GL[:], scalar=GPAR[:, 4:5], in1=TMP[:],
                                                         op0=ALU.mult, op1=ALU.add), [b_TMP, b_GL, b_par], [b_GC])
            S.op("dve", lambda e: e.tensor_tensor(out=TMP3, in0=TB, in1=GC3, op=ALU.subtract), [b_PF, b_GC, b_TMP], [b_TMP])
            for tl in range(18):
                tt = slice(tl * 128, (tl + 1) * 128)
                pb = 2 + (tl % 2)

                def ftr(pe, tt=tt, pb=pb):
                    pe.transpose(ps[pb][:, 0:128], GCALL[:, tt], C.ident[:])
                    pe.transpose(ps[pb][:, 128:256], TMP[:, tt], C.ident[:])
                    return pe.transpose(ps[pb][:, 256:384], GL[:, tt], C.ident[:])
                S.op("pe", ftr, [b_GC, b_TMP, b_GL, C.ident_b], [psb[pb]])
                S.op("act", lambda e, pb=pb: e.activation(out=GLt[:], in_=ps[pb][:, 256:384], func=AF.Copy), [psb[pb]], [b_GLt])
                def rows(ap_, base):
                    return ap_[:, base:base + 64].rearrange("p (d r) -> p d r", d=2)[:, :, 0:16]
                logb = rows(GLt[:], 16)
                gc_ = rows(ps[pb][:, 0:128], 0)
                gcx_ = rows(ps[pb][:, 0:128], 64)
                dk_ = rows(ps[pb][:, 128:256], 0)
                da_ = rows(ps[pb][:, 128:256], 64)

                def outv(w):
                    return TOKP[:, tl, w, :].rearrange("p (d r) -> p d r", d=2)
                for w, src, op_ in ((0, gcx_, ALU.subtract), (1, gc_, ALU.subtract), (2, gc_, ALU.subtract), (3, gcx_, ALU.subtract),
                                    (4, dk_, ALU.add), (5, da_, ALU.add)):
                    S.op("dve", lambda e, w=w, src=src, op_=op_, outv=outv, logb=logb: e.tensor_tensor(out=outv(w), in0=src, in1=logb, op=op_),
                         [psb[pb], b_GLt], [b_TOKP])
            S.op("act", lambda e: e.activation(out=TOKP[:, :, 4:6, :], in_=TOKP[:, :, 4:6, :], func=AF.Exp), [b_TOKP], [b_TOKP])
            S.barrier()

        for hg in range(8 if not C.stop else (0 if C.stop == 'A' else 1)):
            gdn_head_group(C, l, j, s, hg, GCALL, b_GC, TOKP, b_TOKP, MASK, GNW, CONVW, b_par)
        S.barrier()


def gdn_head_group(C, l, j, s, hg, GCALL, b_GC, TOKP, b_TOKP, MASK, GNW, CONVW, b_par):
    nc, S = C.nc, C.S
    from contextlib import ExitStack
    ps, psb = C.ps, C.ps_b
    with ExitStack() as st:
        def sbt(name, shape, dt):
            return st.enter_context(nc.sbuf_tensor(un(name), shape, dt))
        QT = sbt("QT", [128, T], BF16)
        KTt = sbt("KTt", [128, T], BF16)
        Vtok = sbt("Vtok", [128, 18, 2, 128], BF16)
        Ktok = sbt("Ktok", [128, 18, 128], BF16)
        SZ = sbt("SZ", [128, 2, T], BF16)
        b_QT, b_KT, b_Vtok, b_Ktok, b_SZ = Buf(), Buf(), Buf(), Buf(), Buf()
        with ExitStack() as sp_:
            def sbp(name, shape, dt):
                return sp_.enter_context(nc.sbuf_tensor(un(name), shape, dt))
            PRE = sbp("PRE", [128, T + 8], F32)
            CONVO = sbp("CONVO", [128, T + 4], F32)
            VT = sbp("VT", [128, T], BF16)
            W0 = sbp("W0", [128, KT, 128], BF16)
            W1 = sbp("W1", [128, KT, 128], BF16)
            Wb = [W0, W1]
            b_W = [Buf(), Buf()]
            b_PRE, b_CONVO, b_VT = Buf(), Buf(), Buf()
            S.op("pool", lambda e: e.memset(PRE[:], 0.0), (), [b_PRE])
            feats = [("q", hg), ("k", 8 + hg), ("v0", 16 + 2 * hg), ("v1", 16 + 2 * hg + 1),
                     ("z0", 32 + 2 * hg), ("z1", 32 + 2 * hg + 1)]
            for fi, (kind, f) in enumerate(feats):
                W, bW = Wb[fi % 2], b_W[fi % 2]
                S.dma("pool", W[:], C.g_wqkvz[j, f], (), [bW])
                for c, (t0, n, is_ctx) in enumerate(CHUNKS):
                    pb = c % 2
                    mm(S, ps[pb][:, :n], [(W[:, k, :], C.H[:, k, t0:t0 + n]) for k in range(KT)],
                       [bW] + [C.Hb[k][c] for k in range(KT)], [psb[pb]])
                    if kind[0] == "z":
                        S.op("act", lambda e, pb=pb, n=n, t0=t0, kind=kind: e.activation(
                            out=SZ[:, int(kind[1]), t0:t0 + n], in_=ps[pb][:, :n], func=AF.Silu), [psb[pb]], [b_SZ])
                    else:
                        off = t0 + 2 if is_ctx else t0 + 6
                        S.op("act", lambda e, pb=pb, n=n, off=off: e.activation(
                            out=PRE[:, off:off + n], in_=ps[pb][:, :n], func=AF.Copy), [psb[pb]], [b_PRE])
                if kind[0] == "z":
                    continue
                NW_ = T + 4
                S.op("dve", lambda e, f=f: e.tensor_scalar(out=CONVO[:, 0:NW_], in0=PRE[:, 0:NW_], scalar1=CONVW[:, f, 0:1],
                                                          scalar2=None, op0=ALU.mult), [b_PRE, b_par], [b_CONVO])
                for tap in range(1, 5):
                    S.op("dve", lambda e, f=f, tap=tap: e.scalar_tensor_tensor(
                        out=CONVO[:, 0:NW_], in0=PRE[:, tap:tap + NW_], scalar=CONVW[:, f, tap:tap + 1], in1=CONVO[:, 0:NW_],
                        op0=ALU.mult, op1=ALU.add), [b_PRE, b_CONVO, b_par], [b_CONVO])
                if kind[0] == "v":
                    hvl = int(kind[1])
                    S.op("act", lambda e: e.activation(out=VT[:, 0:T_CTX], in_=CONVO[:, 0:T_CTX], func=AF.Silu), [b_CONVO], [b_VT])
                    S.op("act", lambda e: e.activation(out=VT[:, T_CTX:T], in_=CONVO[:, T_CTX + 4:T + 4], func=AF.Silu), [b_CONVO], [b_VT])
                    for g4 in range(0, 18, 6):
                        pb = 2 + ((g4 // 6) % 2)

                        def ftr(pe, g4=g4, pb=pb):
                            ins = None
                            for i in range(6):
                                ins = pe.transpose(C.psbf[pb][:, i * 128:(i + 1) * 128], VT[:, (g4 + i) * 128:(g4 + i + 1) * 128], C.identb[:])
                            return ins
                        S.op("pe", ftr, [b_VT, C.ident_b], [psb[pb]])
                        S.op("dve", lambda e, g4=g4, pb=pb, hvl=hvl: e.tensor_copy(
                            out=Vtok[:, g4:g4 + 6, hvl, :], in_=C.psbf[pb][:, 0:768].rearrange("p (a b) -> p a b", a=6)), [psb[pb]], [b_Vtok])
                else:
                    S.op("act", lambda e: e.activation(out=CONVO[:, 0:T_CTX], in_=CONVO[:, 0:T_CTX], func=AF.Silu), [b_CONVO], [b_CONVO])
                    S.op("act", lambda e: e.activation(out=CONVO[:, T_CTX + 4:T + 4], in_=CONVO[:, T_CTX + 4:T + 4], func=AF.Silu), [b_CONVO], [b_CONVO])
                    dst, bdst = (QT, b_QT) if kind == "q" else (KTt, b_KT)
                    qscale = 128.0 ** -0.5 if kind == "q" else 1.0
                    for c, (t0, n, is_ctx) in enumerate(CHUNKS):
                        off = t0 if is_ctx else t0 + 4
                        tf, tb = next_tmpf(C)
                        S.op("act", lambda e, tf=tf, n=n, off=off: e.activation(out=tf[:, :n], in_=CONVO[:, off:off + n], func=AF.Square), [b_CONVO], [tb])
                        S.op("pe", lambda e, tf=tf, n=n: e.matmul(ps[7][:, :n], C.ones[:], tf[:, :n], start=True, stop=True), [tb, C.ones_b], [psb[7]])
                        S.op("act", lambda e, n=n: e.activation(out=C.rs[:, :n], in_=ps[7][:, :n], func=AF.Sqrt, bias=1e-6), [psb[7]], [C.rs_b])
                        S.op("dve", lambda e, n=n: e.reciprocal(out=C.rs[:, :n], in_=C.rs[:, :n]), [C.rs_b], [C.rs_b])
                        S.op("dve", lambda e, n=n, off=off, t0=t0, dst=dst, qscale=qscale: e.scalar_tensor_tensor(
                            out=dst[:, t0:t0 + n], in0=CONVO[:, off:off + n], scalar=qscale, in1=C.rs[:, :n], op0=ALU.mult, op1=ALU.mult),
                            [b_CONVO, C.rs_b], [bdst])
                    if kind == "k":
                        for g4 in range(0, 18, 6):
                            pb = 2 + ((g4 // 6) % 2)

                            def ftr(pe, g4=g4, pb=pb):
                                ins = None
                                for i in range(6):
                                    ins = pe.transpose(C.psbf[pb][:, i * 128:(i + 1) * 128], KTt[:, (g4 + i) * 128:(g4 + i + 1) * 128], C.identb[:])
                                return ins
                            S.op("pe", ftr, [b_KT, C.ident_b], [psb[pb]])
                            S.op("dve", lambda e, g4=g4, pb=pb: e.tensor_copy(
                                out=Ktok[:, g4:g4 + 6, :], in_=C.psbf[pb][:, 0:768].rearrange("p (a b) -> p a b", a=6)), [psb[pb]], [b_Ktok])
            S.barrier()
        if C.stop == 'P':
            return
        gdn_recurrence(C, l, j, s, hg, GCALL, b_GC, TOKP, b_TOKP, MASK, GNW, b_par, QT, KTt, Vtok, Ktok, SZ,
                       [b_QT, b_KT, b_Vtok, b_Ktok, b_SZ])
        S.barrier()


def gdn_recurrence(C, l, j, s, hg, GCALL, b_GC, TOKP, b_TOKP, MASK, GNW, b_par, QT, KTt, Vtok, Ktok, SZ, inb):
    nc, S = C.nc, C.S
    from contextlib import ExitStack
    ps, psb = C.ps, C.ps_b
    b_QT, b_KT, b_Vtok, b_Ktok, b_SZ = inb
    with ExitStack() as st:
        def sbt(name, shape, dt):
            return st.enter_context(nc.sbuf_tensor(un(name), shape, dt))
        OACC = sbt("OACC", [128, 18, 2, 128], F32)
        b_OACC = [[Buf() for _ in range(2)] for _ in range(18)]
        GM = sbt("GM", [128, 256], F32)
        b_GM = Buf()
        Hf = [sbt("Hf%d" % i, [128, 128], F32) for i in range(2)]
        Hb = [sbt("Hb%d" % i, [128, 128], BF16) for i in range(2)]
        b_H = [Buf(), Buf()]
        NR = 2
        def rot(name, shape, dt):
            return [[sbt("%s%d_%d" % (name, h, i), shape, dt) for i in range(NR)] for h in range(2)], [[Buf() for i in range(NR)] for h in range(2)]
        ERB, b_ERB = rot("ERB", [128, 256], F32)
        BR, b_BR = rot("BR", [128, 2, 128], BF16)
        KA, b_KA = rot("KA", [128, 2, 128], BF16)
        ARG, b_ARG = rot("ARG", [128, 512], F32)
        DD, b_DD = rot("DD", [128, 512], BF16)
        UT, b_UT = rot("UT", [128, 128], F32)
        PP, b_PP = rot("PP", [128, 2, 256], F32)
        XX, b_XX = rot("XX", [128, 2, 128], F32)
        U32, b_U32 = rot("U32", [128, 128], F32)
        TTB, b_TTB = rot("TTB", [128, 128], BF16)
        X1, b_X1 = rot("X1", [128, 128], BF16)
        NE, b_NE = rot("NE", [128, 128], BF16)
        OS_, b_OS = rot("OS", [128, 128], F32)
        ON, b_ON = rot("ON", [128, 128], BF16)
        YT, b_YT = rot("YT", [128, 128], BF16)
        SQ, b_SQ = rot("SQ", [128, 4], F32)
        it = 0
        for d in range(2):
            for h in range(2):
                S.op("pool", lambda e, h=h: e.memset(Hf[h][:], 0.0), (), [b_H[h]])
                S.op("pool", lambda e, h=h: e.memset(Hb[h][:], 0.0), (), [b_H[h]])
            order = list(range(18)) if d == 0 else [1, 0] + list(range(17, 1, -1))
            if C.stop and C.stop.startswith('T'):
                order = order[:int(C.stop[1:])]
            for tl in order:
                tt = slice(tl * 128, (tl + 1) * 128)
                def fg(pe, tt=tt):
                    pe.matmul(ps[2][:, 0:128], KTt[:, tt], KTt[:, tt], start=True, stop=True)
                    return pe.matmul(ps[2][:, 128:256], KTt[:, tt], QT[:, tt], start=True, stop=True)
                S.op("pe", fg, [b_KT, b_QT], [psb[2]])
                S.op("dve", lambda e, d=d: e.tensor_tensor(out=GM[:], in0=ps[2][:, 0:256], in1=MASK[:, d, :], op=ALU.mult), [psb[2], b_par], [b_GM])
                for h in range(C.nh):
                    hv = 2 * hg + h
                    ri = it % NR
                    it += 1
                    idx = d * 16 + hv
                    rc = d * 32 + hv
                    rx = 64 + rc
                    e_, be_ = ERB[h][ri], b_ERB[h][ri]
                    pR = 3
                    def frb(pe, tt=tt, rx=rx, rc=rc, pR=pR):
                        pe.matmul(ps[pR][:, 0:128], C.ident[:, rx:rx + 1].to_broadcast([128, 128]), GCALL[:, tt], start=True, stop=True)
                        return pe.matmul(ps[pR][:, 128:256], C.ident[:, rc:rc + 1].to_broadcast([128, 128]), GCALL[:, tt], start=True, stop=True)
                    S.op("pe", frb, [b_GC, C.ident_b], [psb[pR]])
                    S.op("act", lambda e, e_=e_, pR=pR: e.activation(out=e_[:], in_=ps[pR][:, 0:256], func=AF.Exp), [psb[pR]], [be_])
                    br, bbr = BR[h][ri], b_BR[h][ri]
                    S.op("pool", lambda e, br=br, e_=e_, tt=tt: e.tensor_tensor(out=br[:, 0, :], in0=KTt[:, tt], in1=e_[:, 0:128], op=ALU.mult), [b_KT, be_], [bbr])
                    S.op("pool", lambda e, br=br, e_=e_, tt=tt: e.tensor_tensor(out=br[:, 1, :], in0=QT[:, tt], in1=e_[:, 128:256], op=ALU.mult), [b_QT, be_], [bbr])
                    ka, bka = KA[h][ri], b_KA[h][ri]
                    S.op("pool", lambda e, ka=ka, tl=tl, idx=idx: e.tensor_scalar(out=ka[:, 0, :], in0=Ktok[:, tl, :], scalar1=TOKP[:, tl, 4, idx:idx + 1], scalar2=None, op0=ALU.mult), [b_Ktok, b_TOKP], [bka])
                    S.op("pool", lambda e, ka=ka, tl=tl, idx=idx: e.tensor_scalar(out=ka[:, 1, :], in0=Ktok[:, tl, :], scalar1=TOKP[:, tl, 5, idx:idx + 1], scalar2=None, op0=ALU.mult), [b_Ktok, b_TOKP], [bka])
                    if C.stop2 == 1:
                        continue
                    ar, bar = ARG[h][ri], b_ARG[h][ri]
                    in0 = ps[pR][:, 0:256].rearrange("p (a t) -> p a t", a=2).unsqueeze(2).to_broadcast([128, 2, 2, 128])
                    in1 = TOKP[:, tl, 0:4, idx:idx + 1].rearrange("p (a b) o -> p a b o", a=2).to_broadcast([128, 2, 2, 128])
                    ar4 = ar[:].rearrange("p (a b t) -> p a b t", a=2, b=2)
                    S.op("dve", lambda e, ar4=ar4, in0=in0, in1=in1: e.tensor_tensor(out=ar4, in0=in0, in1=in1, op=ALU.subtract), [psb[pR], b_TOKP], [bar])
                    S.op("pool", lambda e, ar=ar: e.tensor_scalar(out=ar[:], in0=ar[:], scalar1=0.0, scalar2=None, op0=ALU.min), [bar], [bar])
                    S.op("act", lambda e, ar=ar: e.activation(out=ar[:], in_=ar[:], func=AF.Exp), [bar], [bar])
                    dd, bdd = DD[h][ri], b_DD[h][ri]
                    gm4 = GM[:].rearrange("p (a t) -> p a t", a=2).unsqueeze(2).to_broadcast([128, 2, 2, 128])
                    dd4 = dd[:].rearrange("p (a b t) -> p a b t", a=2, b=2)
                    S.op("dve", lambda e, dd4=dd4, ar4=ar4, gm4=gm4: e.tensor_tensor(out=dd4, in0=ar4, in1=gm4, op=ALU.mult), [bar, b_GM], [bdd])
                    U = dd[:, 0:128]
                    if C.stop2 == 2:
                        continue
                    u32, bu32 = U32[h][ri], b_U32[h][ri]
                    S.op("dve", lambda e, u32=u32, ar=ar: e.tensor_tensor(out=u32[:], in0=ar[:, 0:128], in1=GM[:, 0:128], op=ALU.mult), [bar, b_GM], [bu32])
                    ut, but = UT[h][ri], b_UT[h][ri]
                    pS = 4 + h
                    xx, bxx = XX[h][ri], b_XX[h][ri]
                    pp, bpp = PP[h][ri], b_PP[h][ri]
                    ttb, bttb = TTB[h][ri], b_TTB[h][ri]
                    solve_fp32(C, u32[:], bu32, pS, ut, but, pp, bpp, xx, bxx, ttb, bttb)
                    bxx = bttb
                    TT = ttb[:]
                    x1, bx1 = X1[h][ri], b_X1[h][ri]
                    ne, bne = NE[h][ri], b_NE[h][ri]
                    pX, pO, pH = 6, 0 + h, 7
                    for cb in ((0, 1) if d == 0 else (1, 0)):
                        pp_ = slice(cb * 64, cb * 64 + 64)
                        cc = slice(cb * 64, cb * 64 + 64)
                        pccol = 128 + (cb * 64 + 63 if d == 0 else cb * 64)

                        def fx1(pe, pp_=pp_, cc=cc, br=br, dd=dd, h=h, tl=tl):
                            pe.matmul(ps[pX][pp_, 0:128], br[:, 0, cc], Hb[h][:], start=True, stop=False)
                            return pe.matmul(ps[pX][pp_, 0:128], dd[pp_, 128 + cc.start:128 + cc.start + 64], Vtok[pp_, tl, h, :], start=False, stop=True)
                        S.op("pe", fx1, [bbr, bdd, b_H[h], b_Vtok], [psb[pX]])
                        S.op("act", lambda e, pp_=pp_, x1=x1: e.activation(out=x1[pp_, :], in_=ps[pX][pp_, 0:128], func=AF.Copy), [psb[pX]], [bx1])
                        S.op("pe", lambda e, pp_=pp_, cc=cc, TT=TT, x1=x1: e.matmul(ps[pX][pp_, 128:256], TT[pp_, cc], x1[pp_, :], start=True, stop=True), [bxx, bx1], [psb[pX]])
                        S.op("dve", lambda e, pp_=pp_, ne=ne: e.tensor_scalar(out=ne[pp_, :], in0=ps[pX][pp_, 128:256], scalar1=-1.0, scalar2=None, op0=ALU.mult), [psb[pX]], [bne])

                        def fo(pe, pp_=pp_, cc=cc, br=br, dd=dd, ne=ne, h=h, tl=tl):
                            pe.matmul(ps[pO][pp_, 0:128], br[:, 1, cc], Hb[h][:], start=True, stop=False)
                            pe.matmul(ps[pO][pp_, 0:128], dd[pp_, 256 + cc.start:256 + cc.start + 64], Vtok[pp_, tl, h, :], start=False, stop=False)
                            return pe.matmul(ps[pO][pp_, 0:128], dd[pp_, 384 + cc.start:384 + cc.start + 64], ne[pp_, :], start=False, stop=True)
                        S.op("pe", fo, [bbr, bdd, bne, b_H[h], b_Vtok], [psb[pO]])

                        def fh(pe, pp_=pp_, ka=ka, ne=ne, h=h, tl=tl):
                            pe.matmul(ps[pH][:, 0:128], ka[pp_, 0, :], Vtok[pp_, tl, h, :], start=True, stop=False)
                            return pe.matmul(ps[pH][:, 0:128], ka[pp_, 1, :], ne[pp_, :], start=False, stop=True)
                        S.op("pe", fh, [bka, bne, b_Vtok], [psb[pH]])
                        S.op("dve", lambda e, h=h, e_=e_, pccol=pccol: e.scalar_tensor_tensor(
                            out=Hf[h][:], in0=Hf[h][:], scalar=e_[:, pccol:pccol + 1], in1=ps[pH][:, 0:128], op0=ALU.mult, op1=ALU.add),
                            [psb[pH], be_, b_H[h]], [b_H[h]])
                        S.op("act", lambda e, h=h: e.activation(out=Hb[h][:], in_=Hf[h][:], func=AF.Copy), [b_H[h]], [b_H[h]])
                    if C.stop2 == 4:
                        continue
                    if d == 0:
                        S.op("act", lambda e, tl=tl, h=h, pO=pO: e.activation(out=OACC[:, tl, h, :], in_=ps[pO][:, 0:128], func=AF.Copy), [psb[pO]], [b_OACC[tl][h]])
                    else:
                        os_, bos = OS_[h][ri], b_OS[h][ri]
                        on, bon = ON[h][ri], b_ON[h][ri]
                        yt, byt = YT[h][ri], b_YT[h][ri]
                        sq, bsq = SQ[h][ri], b_SQ[h][ri]
                        S.op("dve", lambda e, os_=os_, tl=tl, h=h, pO=pO: e.tensor_tensor(out=os_[:], in0=ps[pO][:, 0:128], in1=OACC[:, tl, h, :], op=ALU.add), [psb[pO], b_OACC[tl][h]], [bos])
                        tf, tb = next_tmpf(C)
                        S.op("act", lambda e, tf=tf, os_=os_, sq=sq: e.activation(out=tf[:, 0:128], in_=os_[:], func=AF.Square, accum_out=sq[:, 0:1]), [bos], [tb, bsq])
                        S.op("act", lambda e, sq=sq: e.activation(out=sq[:, 1:2], in_=sq[:, 0:1], func=AF.Sqrt, scale=1.0 / 128, bias=EPS), [bsq], [bsq])
                        S.op("dve", lambda e, sq=sq: e.reciprocal(out=sq[:, 2:3], in_=sq[:, 1:2]), [bsq], [bsq])
                        S.op("dve", lambda e, on=on, os_=os_, sq=sq: e.tensor_scalar(out=on[:], in0=os_[:], scalar1=sq[:, 2:3], scalar2=None, op0=ALU.mult), [bos, bsq], [bon])
                        S.op("pe", lambda e, on=on, pS=pS: e.transpose(C.psbf[pS][:, 896:1024], on[:], C.identb[:]), [bon, C.ident_b], [psb[pS]])
                        S.op("dve", lambda e, yt=yt, pS=pS, h=h, tt=tt: e.scalar_tensor_tensor(
                            out=yt[:], in0=C.psbf[pS][:, 896:1024], scalar=GNW[:, 0:1], in1=SZ[:, h, tt], op0=ALU.mult, op1=ALU.mult),
                            [psb[pS], b_par, b_SZ], [byt])
                        S.dma("sp", C.Yscr[hv, :, tt], yt[:], [byt], [C.Yscr_b])


def solve_fp32(C, U32, bU, pS, UT, bUT, PP, bPP, XX, bXX, TTb, bTT):
    S = C.S
    ps, psb = C.ps, C.ps_b
    S.op("pe", lambda e: e.transpose(ps[pS][:, 384:512], U32, C.ident[:]), [bU, C.ident_b], [psb[pS]])
    S.op("act", lambda e: e.activation(out=UT[:], in_=ps[pS][:, 384:512], func=AF.Copy), [psb[pS]], [bUT])
    S.op("pool", lambda e: e.tensor_tensor(out=XX[:, 0, :], in0=C.ident[:], in1=U32, op=ALU.subtract), [bU, C.ident_b], [bXX])
    Pk, PTk = U32, UT[:]
    pdeps = [bU, bUT]
    for k in range(1, 6):
        cur = k % 2
        last = (k == 5)

        def fp(pe, Pk=Pk, PTk=PTk, last=last):
            if not last:
                pe.matmul(ps[pS][:, 0:128], PTk, Pk, start=True, stop=True)
            return pe.matmul(ps[pS][:, 128:256], Pk, PTk, start=True, stop=True)
        S.op("pe", fp, pdeps, [psb[pS]])
        S.op("act", lambda e, cur=cur: e.activation(out=PP[:, cur, :], in_=ps[pS][:, 0:256], func=AF.Copy), [psb[pS]], [bPP])
        Pk, PTk = PP[:, cur, 0:128], PP[:, cur, 128:256]
        pdeps = [bPP]
        xprev = XX[:, (k - 1) % 2, :]
        S.op("pe", lambda e, PTk=PTk, xprev=xprev: e.matmul(ps[pS][:, 256:384], PTk, xprev, start=True, stop=True), [bPP, bXX], [psb[pS]])
        if last:
            S.op("dve", lambda e, xprev=xprev: e.tensor_tensor(out=TTb[:], in0=ps[pS][:, 256:384], in1=xprev, op=ALU.add), [psb[pS], bXX], [bTT])
        else:
            xcur = XX[:, k % 2, :]
            S.op("dve", lambda e, xcur=xcur, xprev=xprev: e.tensor_tensor(out=xcur, in0=ps[pS][:, 256:384], in1=xprev, op=ALU.add), [psb[pS], bXX], [bXX])


def gdn_out_proj(C, l, j, s):
    nc, S = C.nc, C.S
    from contextlib import ExitStack
    ps, psb = C.ps, C.ps_b
    S.barrier()
    with ExitStack() as st:
        def sbt(name, shape, dt):
            return st.enter_context(nc.sbuf_tensor(un(name), shape, dt))
        WO = sbt("WO", [128, KT, 16, 128], BF16)
        YB = [sbt("YB%d" % i, [128, 16, 512], BF16) for i in range(2)]
        b_WO = Buf()
        b_YB = [Buf(), Buf()]
        for o in range(KT):
            S.dma("pool", WO[:, o], C.g_wout[j, o], (), [b_WO])
        it = 0
        for c, (t0, n, is_ctx) in enumerate(CHUNKS):
            col = NSEQ if is_ctx else s
            yb, byb = YB[c % 2], b_YB[c % 2]
            S.dma("sp", yb[:, :, 0:n], C.Yscr[:, :, t0:t0 + n].rearrange("f p t -> p f t"), [C.Yscr_b], [byb])
            for o in range(KT):
                po = it % 2
                it += 1
                mm(S, ps[po][:, :n], [(WO[:, o, f, :], yb[:, f, 0:n]) for f in range(16)], [b_WO, byb], [psb[po]])
                S.op("dve", lambda e, po=po, n=n, o=o, t0=t0, col=col: e.scalar_tensor_tensor(
                    out=C.R[:, o, t0:t0 + n], in0=ps[po][:, :n], scalar=C.MODS[:, l, 2 * 8 + o, col:col + 1],
                    in1=C.R[:, o, t0:t0 + n], op0=ALU.mult, op1=ALU.add),
                    [psb[po], C.MODS_b, C.Rb[o][c]], [C.Rb[o][c]])
        S.barrier()


NCH = T // 8
NCC = T_CTX // 8
I32 = mybir.dt.int32


def s5_declare(C):
    nc = C.nc

    def din(name, shape, dt=F32):
        return nc.dram_tensor(name, list(shape), dt, kind="ExternalInput").ap()
    C.s5_lam = din("s5_lam", [64, 2, 128])
    C.s5_dt = din("s5_dt", [64, 128])
    C.s5_B = din("s5_B", [64, 2, 128, 16])
    C.s5_C = din("s5_C", [64, 2, 128, 16])
    C.s5_dskip = din("s5_dskip", [128, KT])
    C.s5_wglu = din("s5_wglu_t", [16, 128, KT, 128])
    C.s5_bglu = din("s5_bglu_t", [128, 16])
    C.c_sel4 = din("c_sel4", [128, 4, 8, 128])
    C.c_maskz = din("c_maskz", [2, 128, 128])
    C.s5_Tz = nc.dram_tensor("s5_Tz", [128, 128, 128], BF16, kind="Internal").ap()
    C.s5_W = nc.dram_tensor("s5_W", [128, 128, 2, 64], BF16, kind="Internal").ap()
    C.s5_Vi = nc.dram_tensor("s5_Vi", [128, 64, 2, 128], BF16, kind="Internal").ap()
    C.A8 = nc.alloc_sbuf_tensor("A8", [64, 2, 128], F32)
    C.A8_b = Buf("A8")
    C.MOscr = nc.dram_tensor("MOscr", [KT, 128, T], F32, kind="Internal").ap()


def s5_setup(C):
    nc, S = C.nc, C.S
    from contextlib import ExitStack
    ps, psb = C.ps, C.ps_b
    S.barrier()
    with ExitStack() as st:
        def sbt(name, shape, dt=F32):
            return st.enter_context(nc.sbuf_tensor(un(name), shape, dt))
        LAM = sbt("LAM", [64, 2, 128]); DT = sbt("DT", [64, 128])
        Bt = sbt("Bt", [64, 2, 128, 16]); Ct = sbt("Ct", [64, 2, 128, 16])
        MZ = sbt("MZ", [128, 2, 128])
        AA = sbt("AA", [64, 2, 128]); AI = sbt("AI", [64, 2, 128]); FF = sbt("FF", [64, 2, 128])
        ZT = sbt("ZT", [64, 9, 2, 128]); QT_ = sbt("QT", [64, 15, 2, 128])
        W1 = [sbt("s5w%d" % i, [64, 128]) for i in range(6)]
        KI = sbt("KI", [64, 128], I32)
        b = Buf("s5setup")
        S.dma("sp", LAM[:], C.s5_lam, (), [b]); S.dma("sp", DT[:], C.s5_dt, (), [b])
        S.dma("sp", Bt[:], C.s5_B, (), [b]); S.dma("sp", Ct[:], C.s5_C, (), [b])
        S.dma("sp", MZ[:, 0, :], C.c_maskz[0], (), [b]); S.dma("sp", MZ[:, 1, :], C.c_maskz[1], (), [b])

        def V(fn):
            S.op("dve", fn, [b], [b])

        def A(fn):
            S.op("act", fn, [b], [b])
        lr, li = LAM[:, 0, :], LAM[:, 1, :]
        t0_, t1_, t2_, t3_, t4_, t5_ = [w[:] for w in W1]
        A(lambda e: e.activation(out=DT[:], in_=DT[:], func=AF.Exp))
        V(lambda e: e.tensor_tensor(out=t0_, in0=DT[:], in1=lr, op=ALU.mult))
        A(lambda e: e.activation(out=t0_, in_=t0_, func=AF.Exp))
        V(lambda e: e.tensor_tensor(out=t1_, in0=DT[:], in1=li, op=ALU.mult))

        def sincos(dst, shift):
            V(lambda e: e.tensor_scalar(out=t2_, in0=t1_, scalar1=1.0 / (2 * np.pi), scalar2=64.0 + shift, op0=ALU.mult, op1=ALU.add))
            V(lambda e: e.tensor_copy(out=KI[:], in_=t2_))
            V(lambda e: e.tensor_copy(out=t3_, in_=KI[:]))
            V(lambda e: e.tensor_tensor(out=t2_, in0=t2_, in1=t3_, op=ALU.subtract))
            A(lambda e: e.activation(out=dst, in_=t2_, func=AF.Sin, scale=2 * np.pi))
        sincos(t4_, 0.0)
        sincos(t5_, 0.25)
        V(lambda e: e.tensor_tensor(out=AA[:, 0, :], in0=t0_, in1=t5_, op=ALU.mult))
        V(lambda e: e.tensor_tensor(out=AA[:, 1, :], in0=t0_, in1=t4_, op=ALU.mult))
        V(lambda e: e.tensor_tensor(out=t1_, in0=t0_, in1=t0_, op=ALU.mult))
        V(lambda e: e.reciprocal(out=t1_, in_=t1_))
        V(lambda e: e.tensor_tensor(out=AI[:, 0, :], in0=AA[:, 0, :], in1=t1_, op=ALU.mult))
        V(lambda e: e.scalar_tensor_tensor(out=AI[:, 1, :], in0=AA[:, 1, :], scalar=-1.0, in1=t1_, op0=ALU.mult, op1=ALU.mult))
        V(lambda e: e.tensor_tensor(out=t1_, in0=lr, in1=lr, op=ALU.mult))
        V(lambda e: e.tensor_tensor(out=t2_, in0=li, in1=li, op=ALU.mult))
        V(lambda e: e.tensor_tensor(out=t1_, in0=t1_, in1=t2_, op=ALU.add))
        V(lambda e: e.reciprocal(out=t1_, in_=t1_))
        V(lambda e: e.tensor_scalar(out=t2_, in0=AA[:, 0, :], scalar1=-1.0, scalar2=None, op0=ALU.add))
        V(lambda e: e.tensor_tensor(out=t3_, in0=t2_, in1=lr, op=ALU.mult))
        V(lambda e: e.tensor_tensor(out=t4_, in0=AA[:, 1, :], in1=li, op=ALU.mult))
        V(lambda e: e.tensor_tensor(out=t3_, in0=t3_, in1=t4_, op=ALU.add))
        V(lambda e: e.tensor_tensor(out=FF[:, 0, :], in0=t3_, in1=t1_, op=ALU.mult))
        V(lambda e: e.tensor_tensor(out=t3_, in0=AA[:, 1, :], in1=lr, op=ALU.mult))
        V(lambda e: e.tensor_tensor(out=t4_, in0=t2_, in1=li, op=ALU.mult))
        V(lambda e: e.tensor_tensor(out=t3_, in0=t3_, in1=t4_, op=ALU.subtract))
        V(lambda e: e.tensor_tensor(out=FF[:, 1, :], in0=t3_, in1=t1_, op=ALU.mult))

        def cmul(dst, x, y):
            V(lambda e: e.tensor_tensor(out=t0_, in0=x[:, 0, :], in1=y[:, 0, :], op=ALU.mult))
            V(lambda e: e.tensor_tensor(out=t1_, in0=x[:, 1, :], in1=y[:, 1, :], op=ALU.mult))
            V(lambda e: e.tensor_tensor(out=dst[:, 0, :], in0=t0_, in1=t1_, op=ALU.subtract))
            V(lambda e: e.tensor_tensor(out=t0_, in0=x[:, 0, :], in1=y[:, 1, :], op=ALU.mult))
            V(lambda e: e.tensor_tensor(out=t1_, in0=x[:, 1, :], in1=y[:, 0, :], op=ALU.mult))
            V(lambda e: e.tensor_tensor(out=dst[:, 1, :], in0=t0_, in1=t1_, op=ALU.add))
        V(lambda e: e.memset(ZT[:, 0, 0, :], 1.0))
        V(lambda e: e.memset(ZT[:, 0, 1, :], 0.0))
        for e_ in range(1, 9):
            cmul(ZT[:, e_], ZT[:, e_ - 1], AA[:])
        V(lambda e: e.tensor_copy(out=QT_[:, 7], in_=FF[:]))
        for e_ in range(1, 8):
            cmul(QT_[:, 7 + e_], QT_[:, 7 + e_ - 1], AA[:])
            cmul(QT_[:, 7 - e_], QT_[:, 7 - e_ + 1], AI[:])
        S.op("dve", lambda e: e.tensor_copy(out=C.A8[:], in_=ZT[:, 8]), [b], [C.A8_b])

        VP = sbt("VP", [64, 16, 2, 8, 16]); VI = sbt("VI", [64, 16, 2, 8, 16])
        WA = sbt("WA", [64, 16, 2, 8, 16]); WB = sbt("WB", [64, 16, 2, 8, 16])
        TM = sbt("TMs5", [64, 16, 16])
        TZs = sbt("TZs", [128, 16, 128], BF16); WSs = sbt("WSs", [128, 16, 2, 64], BF16); VIs = sbt("VIs", [64, 16, 2, 128], BF16)
        b_st = Buf("s5stage")
        for d in range(2):
            ea = (lambda j: 7 - j) if d == 0 else (lambda j: j)
            eb = (lambda j: -j) if d == 0 else (lambda j: j - 7)
            ev = (lambda t: t) if d == 0 else (lambda t: 7 - t)
            ei = (lambda t: t + 1) if d == 0 else (lambda t: 8 - t)
            for gb in range(4):
                dg = slice(d * 64 + gb * 16, d * 64 + gb * 16 + 16)
                Cr, Ci = Ct[:, 0, dg, :], Ct[:, 1, dg, :]
                Br, Bi = Bt[:, 0, dg, :], Bt[:, 1, dg, :]

                def bc(ap_):
                    return ap_.unsqueeze(2).to_broadcast([64, 16, 16])
                for t in range(8):
                    for (tab, ee) in ((VP, ev(t)), (VI, ei(t))):
                        zr, zi = bc(ZT[:, ee, 0, dg]), bc(ZT[:, ee, 1, dg])
                        V(lambda e, tab=tab, t=t, zr=zr: e.tensor_tensor(out=tab[:, :, 0, t, :], in0=Cr, in1=zr, op=ALU.mult))
                        V(lambda e, zi=zi: e.tensor_tensor(out=TM[:], in0=Ci, in1=zi, op=ALU.mult))
                        V(lambda e, tab=tab, t=t: e.tensor_tensor(out=tab[:, :, 0, t, :], in0=tab[:, :, 0, t, :], in1=TM[:], op=ALU.subtract))
                        V(lambda e, tab=tab, t=t, zi=zi: e.tensor_tensor(out=tab[:, :, 1, t, :], in0=Cr, in1=zi, op=ALU.mult))
                        V(lambda e, zr=zr: e.tensor_tensor(out=TM[:], in0=Ci, in1=zr, op=ALU.mult))
                        V(lambda e, tab=tab, t=t: e.scalar_tensor_tensor(out=tab[:, :, 1, t, :], in0=tab[:, :, 1, t, :], scalar=-1.0, in1=TM[:],
                                                                       op0=ALU.mult, op1=ALU.subtract))
                    for (tab, ee) in ((WA, ea(t)), (WB, eb(t))):
                        qr, qi = bc(QT_[:, 7 + ee, 0, dg]), bc(QT_[:, 7 + ee, 1, dg])
                        V(lambda e, tab=tab, t=t, qr=qr: e.tensor_tensor(out=tab[:, :, 0, t, :], in0=Br, in1=qr, op=ALU.mult))
                        V(lambda e, qi=qi: e.tensor_tensor(out=TM[:], in0=Bi, in1=qi, op=ALU.mult))
                        V(lambda e, tab=tab, t=t: e.tensor_tensor(out=tab[:, :, 0, t, :], in0=tab[:, :, 0, t, :], in1=TM[:], op=ALU.subtract))
                        V(lambda e, tab=tab, t=t, qr=qr: e.tensor_tensor(out=tab[:, :, 1, t, :], in0=Bi, in1=qr, op=ALU.mult))
                        V(lambda e, qi=qi: e.tensor_tensor(out=TM[:], in0=Br, in1=qi, op=ALU.mult))
                        V(lambda e, tab=tab, t=t: e.tensor_tensor(out=tab[:, :, 1, t, :], in0=tab[:, :, 1, t, :], in1=TM[:], op=ALU.add))
                for gl in range(16):
                    pb = gl % 2

                    def flat(tab, ri, gl=gl):
                        return tab[:, gl, ri].rearrange("p t c -> p (t c)")
                    mm(S, ps[pb][:, 0:128], [(flat(WB, 0), flat(VP, 0)), (flat(WB, 1), flat(VP, 1))], [b], [psb[pb]])
                    S.op("dve", lambda e, pb=pb, gl=gl: e.tensor_tensor(out=TZs[:, gl, :], in0=ps[pb][:, 0:128], in1=MZ[:, d, :], op=ALU.mult),
                         [psb[pb], b], [b_st])

                    def ftr(pe, pb=pb, flat=flat):
                        pe.transpose(ps[2 + pb][:, 0:64], flat(WA, 0), C.ident[0:64, 0:64])
                        return pe.transpose(ps[2 + pb][:, 64:128], flat(WA, 1), C.ident[0:64, 0:64])
                    S.op("pe", ftr, [b, C.ident_b], [psb[2 + pb]])
                    S.op("act", lambda e, pb=pb, gl=gl: e.activation(out=WSs[:, gl].rearrange("p r q -> p (r q)"), in_=ps[2 + pb][:, 0:128], func=AF.Copy),
                         [psb[2 + pb]], [b_st])
                S.op("act", lambda e: e.activation(out=VIs[:].rearrange("p g r q -> p (g r q)"), in_=VI[:].rearrange("p g r t c -> p (g r t c)"), func=AF.Copy),
                     [b], [b_st])
                S.dma("sp", C.s5_Tz[dg].rearrange("g p q -> p g q"), TZs[:], [b_st], ())
                S.dma("sp", C.s5_W[dg].rearrange("g p r q -> p g r q"), WSs[:], [b_st], ())
                S.dma("sp", C.s5_Vi[dg].rearrange("g p r q -> p g r q"), VIs[:], [b_st], ())
        S.barrier()


def s5_mixer(C, l, s):
    nc, S = C.nc, C.S
    from contextlib import ExitStack
    ps, psb = C.ps, C.ps_b
    GB = 8
    S.barrier()
    with ExitStack() as st:
        def sbt(name, shape, dt=F32):
            return st.enter_context(nc.sbuf_tensor(un(name), shape, dt))
        ZG = sbt("ZG", [128, KT, T], BF16)
        b_ZG = Buf("ZG")
        DSK = sbt("DSK", [128, KT]); BGL = sbt("BGL", [128, 16])
        b_c = Buf("s5c")
        S.dma("sp", DSK[:], C.s5_dskip, (), [b_c])
        S.dma("sp", BGL[:], C.s5_bglu, (), [b_c])
        with ExitStack() as st2:
            def sb2(name, shape, dt=F32):
                return st2.enter_context(nc.sbuf_tensor(un(name), shape, dt))
            SEL4 = sb2("SEL4", [128, 4, 8, 128], BF16)
            S.dma("pool", SEL4[:], C.c_sel4, (), [b_c])
            Ublk = sb2("Ublk", [128, GB, NCH], BF16); b_U = [Buf() for _ in range(GB)]
            Yblk = sb2("Yblk", [128, GB, NCH], BF16); b_Y = [Buf() for _ in range(GB)]
            SS = sb2("SS", [64, 2 * GB, 2, NCH + 1]); b_SS = Buf("SS")
            TZ = sb2("TZ", [128, 2 * GB, 128], BF16); WW = sb2("WW", [128, 2 * GB, 2, 64], BF16); VV = sb2("VV", [64, 2 * GB, 2, 128], BF16)
            b_P = Buf("s5par")
            T1 = sb2("T1", [64, 2 * GB, 2]); T2 = sb2("T2", [64, 2 * GB, 2]); b_T1 = Buf("s5T1"); b_T2 = Buf("s5T2")
            AB_ = sb2("ABlk", [64, 2, 2 * GB]); b_AB = Buf()
            SPB = sb2("SPB", [64, 2, 2, 2, NCH], BF16); b_SPB = [Buf(), Buf()]
            ZT_ = [sb2("zt%d" % i, [128, NCH]) for i in range(2)]; b_zt = [Buf(), Buf()]
            for blk in range(64 // GB):
                tile = blk
                for d in range(2):
                    dg = slice(d * 64 + blk * GB, d * 64 + blk * GB + GB)
                    S.dma("sp", TZ[:, d * GB:(d + 1) * GB, :], C.s5_Tz[dg].rearrange("g p q -> p g q"), (), [b_P])
                    S.dma("sp", WW[:, d * GB:(d + 1) * GB], C.s5_W[dg].rearrange("g p r q -> p g r q"), (), [b_P])
                    S.dma("sp", VV[:, d * GB:(d + 1) * GB], C.s5_Vi[dg].rearrange("g p r q -> p g r q"), (), [b_P])
                    S.op("dve", lambda e, d=d, dg=dg: e.tensor_copy(out=AB_[:, :, d * GB:(d + 1) * GB], in_=C.A8[:, :, dg]), [C.A8_b], [b_AB])
                hreads = [C.Hb[tile][c] for c in range(5)]
                for gl in range(GB):
                    b2, q = gl // 4, gl % 4
                    pb = gl % 2
                    rr = slice(64 * b2, 64 * b2 + 64)
                    mm(S, ps[pb][:, 0:NCH], [(SEL4[rr, q, j, :], C.H[rr, tile, j::8]) for j in range(8)], [b_c] + hreads, [psb[pb]])
                    S.op("act", lambda e, pb=pb, gl=gl: e.activation(out=Ublk[:, gl, :], in_=ps[pb][:, 0:NCH], func=AF.Copy), [psb[pb]], [b_U[gl]])
                S.op("pool", lambda e: e.memset(SS[:, :, :, 0:1], 0.0), (), [b_SS])
                for gl in range(GB):
                    for d in range(2):
                        ix = d * GB + gl
                        for ri in range(2):
                            pb = 2 + ((gl * 4 + d * 2 + ri) % 4)
                            S.op("pe", lambda e, pb=pb, ix=ix, ri=ri, gl=gl: e.matmul(ps[pb][0:64, 0:NCH], WW[:, ix, ri, :], Ublk[:, gl, :], start=True, stop=True),
                                 [b_P, b_U[gl]], [psb[pb]])
                            eng = "act" if ri == 0 else "dve"

                            def fcp(e, pb=pb, ix=ix, ri=ri, eng=eng, d=d):
                                if d == 0:
                                    segs = [(SS[:, ix, ri, 1:NCH + 1], ps[pb][0:64, 0:NCH])]
                                else:
                                    segs = [(SS[:, ix, ri, 1:NCC + 1], ps[pb][0:64, NCC - 1::-1]),
                                            (SS[:, ix, ri, NCC + 1:NCH + 1], ps[pb][0:64, NCH - 1:NCC - 1:-1])]
                                ins = None
                                for (o_, i_) in segs:
                                    ins = e.activation(out=o_, in_=i_, func=AF.Copy) if eng == "act" else e.tensor_copy(out=o_, in_=i_)
                                return ins
                            S.op(eng, fcp, [psb[pb]], [b_SS])
                Ar = AB_[:, 0, :].unsqueeze(2).to_broadcast([64, 2 * GB, 2])
                Ai = AB_[:, 1, :].unsqueeze(2).to_broadcast([64, 2 * GB, 2])
                for k in range(1, NCH):
                    S.op("dve", lambda e, k=k: e.tensor_tensor(out=T1[:], in0=SS[:, :, :, k], in1=Ar, op=ALU.mult), [b_SS, b_AB], [b_T1])
                    S.op("pool", lambda e, k=k: e.tensor_tensor(out=T2[:], in0=SS[:, :, :, k], in1=Ai, op=ALU.mult), [b_SS, b_AB], [b_T2])
                    S.op("dve", lambda e, k=k: e.tensor_tensor(out=SS[:, :, :, k + 1], in0=SS[:, :, :, k + 1], in1=T1[:], op=ALU.add), [b_T1, b_SS], [b_SS])
                    S.op("dve", lambda e, k=k: e.tensor_tensor(out=SS[:, :, 0, k + 1], in0=SS[:, :, 0, k + 1], in1=T2[:, :, 1], op=ALU.subtract), [b_T2, b_SS], [b_SS])
                    S.op("dve", lambda e, k=k: e.tensor_tensor(out=SS[:, :, 1, k + 1], in0=SS[:, :, 1, k + 1], in1=T2[:, :, 0], op=ALU.add), [b_T2, b_SS], [b_SS])
                for gl in range(GB):
                    pb = gl % 2
                    rot = gl % 2
                    S.op("act", lambda e, gl=gl, rot=rot: e.activation(out=SPB[:, rot, 0], in_=SS[:, gl, :, 0:NCH], func=AF.Copy), [b_SS], [b_SPB[rot]])

                    def fsp(e, gl=gl, rot=rot):
                        e.tensor_copy(out=SPB[:, rot, 1, :, 0:NCC], in_=SS[:, GB + gl, :, NCC - 1::-1])
                        return e.tensor_copy(out=SPB[:, rot, 1, :, NCC:NCH], in_=SS[:, GB + gl, :, NCH - 1:NCC - 1:-1])
                    S.op("dve", fsp, [b_SS], [b_SPB[rot]])

                    def fy(pe, pb=pb, gl=gl, rot=rot):
                        ins = None
                        for (c0, c1) in ((0, NCC), (NCC, NCH)):
                            terms = [(TZ[:, gl, :], Ublk[:, gl, c0:c1]), (TZ[:, GB + gl, :], Ublk[:, gl, c0:c1])]
                            for ri in range(2):
                                terms.append((VV[:, gl, ri, :], SPB[:, rot, 0, ri, c0:c1]))
                                terms.append((VV[:, GB + gl, ri, :], SPB[:, rot, 1, ri, c0:c1]))
                            for i, (l_, r_) in enumerate(terms):
                                ins = pe.matmul(ps[pb][:, c0:c1], l_, r_, start=(i == 0), stop=(i == len(terms) - 1))
                        return ins
                    S.op("pe", fy, [b_P, b_U[gl], b_SPB[rot]], [psb[pb]])
                    S.op("act", lambda e, pb=pb, gl=gl: e.activation(out=Yblk[:, gl, :], in_=ps[pb][:, 0:NCH], func=AF.Copy), [psb[pb]], [b_Y[gl]])
                for t in range(8):
                    b2, q = t // 4, t % 4
                    rr = slice(64 * b2, 64 * b2 + 64)
                    pb = 4 + (t % 2)
                    mm(S, ps[pb][:, 0:NCH], [(SEL4[rr, q, g8, :], Yblk[rr, g8, :]) for g8 in range(8)], [b_c] + b_Y, [psb[pb]])
                    z, bz = ZT_[t % 2], b_zt[t % 2]
                    S.op("dve", lambda e, z=z, tile=tile, t=t, pb=pb: e.scalar_tensor_tensor(
                        out=z[:], in0=C.H[:, tile, t::8], scalar=DSK[:, tile:tile + 1], in1=ps[pb][:, 0:NCH], op0=ALU.mult, op1=ALU.add),
                        [psb[pb], b_c] + hreads, [bz])
                    gelu_tanh(C, ZG[:, tile, t::8], z, bz, [b_ZG])
            S.barrier()
        WG = sbt("WG", [128, 16, KT, 128], BF16); b_WG = Buf()
        SGT = [sbt("sgt%d" % i, [128, 512]) for i in range(3)]; b_sg = [Buf(), Buf(), Buf()]
        for f in range(16):
            S.dma("pool", WG[:, f], C.s5_wglu[f], (), [b_WG])
        it = 0
        for c, (t0, n, is_ctx) in enumerate(CHUNKS):
            for o in range(KT):
                pv, pg = (it % 3) * 2, (it % 3) * 2 + 1
                sg, bsg = SGT[it % 3], b_sg[it % 3]
                it += 1
                mm(S, ps[pv][:, :n], [(WG[:, o, k, :], ZG[:, k, t0:t0 + n]) for k in range(KT)], [b_WG, b_ZG], [psb[pv]])
                mm(S, ps[pg][:, :n], [(WG[:, 8 + o, k, :], ZG[:, k, t0:t0 + n]) for k in range(KT)], [b_WG, b_ZG], [psb[pg]])
                S.op("act", lambda e, pg=pg, sg=sg, n=n, o=o: e.activation(out=sg[:, :n], in_=ps[pg][:, :n], func=AF.Sigmoid, bias=BGL[:, 8 + o:9 + o]), [psb[pg], b_c], [bsg])
                S.op("dve", lambda e, pv=pv, sg=sg, n=n, o=o: e.scalar_tensor_tensor(
                    out=sg[:, :n], in0=ps[pv][:, :n], scalar=BGL[:, o:o + 1], in1=sg[:, :n], op0=ALU.add, op1=ALU.mult), [psb[pv], bsg, b_c], [bsg])
                S.dma("sp", C.MOscr[o, :, t0:t0 + n], sg[:, :n], [bsg], ())
        S.barrier()


def apply_mixer_out(C, l, s):
    nc, S = C.nc, C.S
    S.barrier()
    with (nc.sbuf_tensor(un("mo0"), [128, 512], F32) as mo0, nc.sbuf_tensor(un("mo1"), [128, 512], F32) as mo1,
          nc.sbuf_tensor(un("mo2"), [128, 512], F32) as mo2):
        mo, bmo = [mo0, mo1, mo2], [Buf(), Buf(), Buf()]
        it = 0
        for c, (t0, n, is_ctx) in enumerate(CHUNKS):
            col = NSEQ if is_ctx else s
            for o in range(KT):
                m_, bm = mo[it % 3], bmo[it % 3]
                it += 1
                S.dma("sp" if it % 2 == 0 else "act", m_[:, :n], C.MOscr[o, :, t0:t0 + n], (), [bm])
                S.op("dve", lambda e, m_=m_, n=n, o=o, t0=t0, col=col: e.scalar_tensor_tensor(
                    out=C.R[:, o, t0:t0 + n], in0=m_[:, :n], scalar=C.MODS[:, l, 2 * 8 + o, col:col + 1], in1=C.R[:, o, t0:t0 + n],
                    op0=ALU.mult, op1=ALU.add), [bm, C.MODS_b, C.Rb[o][c]], [C.Rb[o][c]])
        S.barrier()


def gelu_tanh(C, out_ap, z, bz, wbufs):
    S = C.S
    tf, tb = next_tmpf(C)
    n = z.shape[1]
    S.op("dve", lambda e: e.tensor_tensor(out=tf[:, :n], in0=z[:], in1=z[:], op=ALU.mult), [bz], [tb])
    S.op("dve", lambda e: e.tensor_scalar(out=tf[:, :n], in0=tf[:, :n], scalar1=0.044715, scalar2=1.0, op0=ALU.mult, op1=ALU.add), [tb], [tb])
    S.op("dve", lambda e: e.tensor_tensor(out=tf[:, :n], in0=tf[:, :n], in1=z[:], op=ALU.mult), [tb, bz], [tb])
    S.op("act", lambda e: e.activation(out=tf[:, :n], in_=tf[:, :n], func=AF.Tanh, scale=0.7978845608028654), [tb], [tb])
    S.op("dve", lambda e: e.tensor_scalar(out=tf[:, :n], in0=tf[:, :n], scalar1=1.0, scalar2=0.5, op0=ALU.add, op1=ALU.mult), [tb], [tb])
    S.op("dve", lambda e: e.tensor_tensor(out=out_ap, in0=tf[:, :n], in1=z[:], op=ALU.mult), [tb, bz], wbufs)


RW_NV = 9


def rwkv_declare(C):
    nc = C.nc

    def din(name, shape, dt=F32):
        return nc.dram_tensor(name, list(shape), dt, kind="ExternalInput").ap()
    C.rw_mu = din("rw_mu", [128, 6, KT])
    C.rw_wrkv = din("rw_wrkv_t", [3, KT, 128, KT, 128])
    C.rw_l1 = din("rw_l1_t", [3, 128, KT, 128])
    C.rw_l2 = din("rw_l2", [3, 128, D])
    C.rw_vecs = din("rw_vecs", [128, RW_NV, KT])
    C.rw_wout = din("rw_wout_t", [KT, 128, KT, 128])
    C.c_mask4 = din("c_mask4", [2, 128, 512])


def rwkv_mixer(C, l, s):
    nc, S = C.nc, C.S
    from contextlib import ExitStack
    ps, psb = C.ps, C.ps_b
    S.barrier()
    with ExitStack() as st:
        def sbt(name, shape, dt=F32):
            return st.enter_context(nc.sbuf_tensor(un(name), shape, dt))
        SH = sbt("SH", [128, KT, T], BF16); b_SH = Buf("SH")
        MU = sbt("MU", [128, 2, 6, KT]); VEC = sbt("VEC", [128, RW_NV, KT]); b_par = Buf("rwpar")
        L2 = sbt("L2", [128, 3, D], BF16)
        LH = sbt("LH", [128, 3, T], BF16); b_LH = Buf("LH")
        BD = sbt("BD", [128, 128]); MASK4 = sbt("MASK4", [128, 2, 512], BF16)
        S.dma("sp", MU[:, 0], C.rw_mu, (), [b_par])
        S.dma("sp", VEC[:], C.rw_vecs, (), [b_par])
        S.dma("pool", MASK4[:, 0, :], C.c_mask4[0], (), [b_par])
        S.dma("pool", MASK4[:, 1, :], C.c_mask4[1], (), [b_par])
        S.dma("pool", L2[:], C.rw_l2.rearrange("a p n -> p a n"), (), [b_par])
        S.op("dve", lambda e: e.tensor_scalar(out=MU[:, 1], in0=MU[:, 0], scalar1=-1.0, scalar2=1.0, op0=ALU.mult, op1=ALU.add), [b_par], [b_par])
        S.op("pool", lambda e: e.memset(BD[:], 0.0), (), [b_par])
        S.op("pool", lambda e: e.memset(BD[0:64, 0:64], 1.0), [b_par], [b_par])
        S.op("pool", lambda e: e.memset(BD[64:128, 64:128], 1.0), [b_par], [b_par])
        S.op("pool", lambda e: e.memset(SH[:], 0.0), (), [b_SH])
        allH = [C.Hb[k][c] for k in range(KT) for c in range(5)]
        for k in range(KT):
            eng = ("dve", "pool", "act")[k % 3]

            def cp(e, o_, i_, eng=eng):
                return e.activation(out=o_, in_=i_, func=AF.Copy) if eng == "act" else e.tensor_copy(out=o_, in_=i_)
            xs = slice(T_CTX, T)
            if k < 4:
                S.op(eng, lambda e, k=k, cp=cp: cp(e, SH[:, k, 1:T_CTX], C.H[:, k, 0:T_CTX - 1]), allH, [b_SH])
            else:
                S.op(eng, lambda e, k=k, cp=cp: cp(e, SH[:, k, 0:T_CTX - 1], C.H[:, k, 1:T_CTX]), allH, [b_SH])
            hx = C.H[:, k, xs].rearrange("p (r c) -> p r c", c=64)
            sx = SH[:, k, xs].rearrange("p (r c) -> p r c", c=64)
            if k < 2:
                S.op(eng, lambda e, cp=cp, sx=sx, hx=hx: cp(e, sx[:, :, 1:64], hx[:, :, 0:63]), allH, [b_SH])
            elif k < 4:
                S.op(eng, lambda e, cp=cp, sx=sx, hx=hx: cp(e, sx[:, :, 0:63], hx[:, :, 1:64]), allH, [b_SH])
            elif k < 6:
                S.op(eng, lambda e, k=k, cp=cp: cp(e, SH[:, k, T_CTX + 64:T], C.H[:, k, T_CTX:T - 64]), allH, [b_SH])
            else:
                S.op(eng, lambda e, k=k, cp=cp: cp(e, SH[:, k, T_CTX:T - 64], C.H[:, k, T_CTX + 64:T]), allH, [b_SH])

        cnt = [0]
        wb = {}

        class wscope:
            def __enter__(self_):
                self_.st = ExitStack()
                wb["WF"] = [self_.st.enter_context(nc.sbuf_tensor(un("rwWF"), [128, KT, 128], F32)) for i in range(2)]
                wb["WP"] = [self_.st.enter_context(nc.sbuf_tensor(un("rwWP"), [128, 2, KT, 128], BF16)) for i in range(2)]
                wb["bWF"] = [Buf(), Buf()]
                wb["bWP"] = [Buf(), Buf()]
                return self_

            def __exit__(self_, *a):
                S.barrier()
                self_.st.close()
                return False

        def proj2(w_ap, mu_idx, out_fn):
            i = cnt[0] % 2
            cnt[0] += 1
            wf, bwf, wp, bwp = wb["WF"][i], wb["bWF"][i], wb["WP"][i], wb["bWP"][i]
            S.dma("sp" if i == 0 else "act", wf[:], w_ap, (), [bwf])
            S.op("dve", lambda e: e.tensor_tensor(out=wp[:, 0], in0=wf[:], in1=MU[:, 1, mu_idx, :].unsqueeze(2).to_broadcast([128, KT, 128]), op=ALU.mult), [bwf, b_par], [bwp])
            S.op("pool", lambda e: e.tensor_tensor(out=wp[:, 1], in0=wf[:], in1=MU[:, 0, mu_idx, :].unsqueeze(2).to_broadcast([128, KT, 128]), op=ALU.mult), [bwf, b_par], [bwp])
            for c, (t0, n, _) in enumerate(CHUNKS):
                pb = c % 2
                pairs = [(wp[:, 0, k, :], C.H[:, k, t0:t0 + n]) for k in range(KT)] + [(wp[:, 1, k, :], SH[:, k, t0:t0 + n]) for k in range(KT)]
                mm(S, ps[pb][:, :n], pairs, [bwp, b_SH] + [C.Hb[k][c] for k in range(KT)], [psb[pb]])
                out_fn(c, t0, n, ps[pb][:, :n], psb[pb])
        ws_ = wscope()
        ws_.__enter__()
        proj2(C.rw_l1[0], 1, lambda c, t0, n, p_, pb_: S.op("act", lambda e: e.activation(out=LH[:, 0, t0:t0 + n], in_=p_, func=AF.Tanh), [pb_], [b_LH]))
        proj2(C.rw_l1[1], 4, lambda c, t0, n, p_, pb_: S.op("act", lambda e: e.activation(out=LH[:, 1, t0:t0 + n], in_=p_, func=AF.Copy), [pb_], [b_LH]))
        proj2(C.rw_l1[2], 5, lambda c, t0, n, p_, pb_: S.op("act", lambda e: e.activation(out=LH[:, 2, t0:t0 + n], in_=p_, func=AF.Sigmoid), [pb_], [b_LH]))

        ws_.__exit__(None, None, None)
        for o in range(KT if not C.stop else 1):
            rwkv_tile(C, l, s, o, SH, b_SH, MU, VEC, b_par, L2, LH, b_LH, BD, MASK4, proj2, wscope)
        S.barrier()


def rwkv_tile(C, l, s, o, SH, b_SH, MU, VEC, b_par, L2, LH, b_LH, BD, MASK4, proj2, wscope):
    nc, S = C.nc, C.S
    from contextlib import ExitStack
    ps, psb = C.ps, C.ps_b
    ocol = slice(o * 128, (o + 1) * 128)
    with ExitStack() as st:
        def sbt(name, shape, dt=F32):
            return st.enter_context(nc.sbuf_tensor(un(name), shape, dt))
        Rr = sbt("Rr", [128, T], BF16); Kk = sbt("Kk", [128, T], BF16); Vv = sbt("Vv", [128, T], BF16)
        KK = sbt("KK", [128, T], BF16); Gg = sbt("Gg", [128, T], BF16); BON = sbt("BON", [128, T], BF16)
        Vtok = sbt("rVtok", [128, 18, 128], BF16)
        OACC = sbt("rOACC", [128, 18, 128]); b_OACC = [Buf() for _ in range(18)]
        b_R, b_K, b_V, b_KK, b_G, b_BON, b_Vtok = [Buf() for _ in range(7)]
        ws_ = wscope()
        ws_.__enter__()
        proj2(C.rw_wrkv[0, o], 0, lambda c, t0, n, p_, pb_: S.op("act", lambda e: e.activation(out=Rr[:, t0:t0 + n], in_=p_, func=AF.Copy), [pb_], [b_R]))
        proj2(C.rw_wrkv[1, o], 2, lambda c, t0, n, p_, pb_: S.op("act", lambda e: e.activation(out=Kk[:, t0:t0 + n], in_=p_, func=AF.Copy), [pb_], [b_K]))
        proj2(C.rw_wrkv[2, o], 3, lambda c, t0, n, p_, pb_: S.op("act", lambda e: e.activation(out=Vv[:, t0:t0 + n], in_=p_, func=AF.Copy), [pb_], [b_V]))
        ws_.__exit__(None, None, None)
        S.op("pool", lambda e: e.memset(BON[:], 0.0), (), [b_BON])
        for c, (t0, n, _) in enumerate(CHUNKS):
            pb = 2 + c % 2
            S.op("pe", lambda e, pb=pb, t0=t0, n=n: e.matmul(ps[pb][:, :n], L2[:, 2, ocol], LH[:, 2, t0:t0 + n], start=True, stop=True), [b_par, b_LH], [psb[pb]])
            S.op("act", lambda e, pb=pb, t0=t0, n=n: e.activation(out=Gg[:, t0:t0 + n], in_=ps[pb][:, :n], func=AF.Copy), [psb[pb]], [b_G])
            tf, tb = next_tmpf(C)
            tg, tgb = next_tmpf(C)
            S.op("dve", lambda e, tf=tf, t0=t0, n=n: e.tensor_scalar(out=tf[:, :n], in0=Kk[:, t0:t0 + n], scalar1=VEC[:, 4, o:o + 1], scalar2=None, op0=ALU.mult), [b_K, b_par], [tb])
            S.op("act", lambda e, tf=tf, tg=tg, n=n: e.activation(out=tg[:, :n], in_=tf[:, :n], func=AF.Square), [tb], [tgb])
            S.op("pe", lambda e, tg=tg, n=n: e.matmul(ps[7][:, :n], BD[:], tg[:, :n], start=True, stop=True), [tgb, b_par], [psb[7]])
            S.op("act", lambda e, n=n: e.activation(out=C.rs[:, :n], in_=ps[7][:, :n], func=AF.Sqrt, bias=1e-6), [psb[7]], [C.rs_b])
            S.op("dve", lambda e, n=n: e.reciprocal(out=C.rs[:, :n], in_=C.rs[:, :n]), [C.rs_b], [C.rs_b])
            S.op("dve", lambda e, tf=tf, t0=t0, n=n: e.tensor_tensor(out=KK[:, t0:t0 + n], in0=tf[:, :n], in1=C.rs[:, :n], op=ALU.mult), [tb, C.rs_b], [b_KK])
        for g6 in range(0, 18, 6):
            pb = 2 + ((g6 // 6) % 2)

            def ftr(pe, g6=g6, pb=pb):
                ins = None
                for i in range(6):
                    ins = pe.transpose(C.psbf[pb][:, i * 128:(i + 1) * 128], Vv[:, (g6 + i) * 128:(g6 + i + 1) * 128], C.identb[:])
                return ins
            S.op("pe", ftr, [b_V, C.ident_b], [psb[pb]])
            S.op("dve", lambda e, g6=g6, pb=pb: e.tensor_copy(out=Vtok[:, g6:g6 + 6, :], in_=C.psbf[pb][:, 0:768].rearrange("p (a b) -> p a b", a=6)), [psb[pb]], [b_Vtok])

        for d in range(2):
            with ExitStack() as sd:
                def sbd(name, shape, dt=F32):
                    return sd.enter_context(nc.sbuf_tensor(un(name), shape, dt))
                LC = sbd("LC", [128, T]); KD = sbd("KD", [128, T], BF16); AK = sbd("AK", [128, T], BF16)
                b_LC, b_LW, b_KD, b_AK = Buf(), Buf(), Buf(), Buf()
                lwg = nc.sbuf_tensor(un("LW"), [128, T], F32)
                LW = lwg.__enter__()
                hs = slice(d * 64, d * 64 + 64)
                for c, (t0, n, _) in enumerate(CHUNKS):
                    pw, pa = 2 + c % 2, 4 + c % 2
                    S.op("pe", lambda e, pw=pw, t0=t0, n=n: e.matmul(ps[pw][:, :n], L2[hs, 0, ocol], LH[hs, 0, t0:t0 + n], start=True, stop=True), [b_par, b_LH], [psb[pw]])
                    S.op("act", lambda e, pw=pw, t0=t0, n=n: e.activation(out=LW[:, t0:t0 + n], in_=ps[pw][:, :n], func=AF.Sigmoid, bias=VEC[:, 0 + d, o:o + 1]), [psb[pw], b_par], [b_LW])
                    tf, tb = next_tmpf(C)
                    S.op("pe", lambda e, pa=pa, t0=t0, n=n: e.matmul(ps[pa][:, :n], L2[hs, 1, ocol], LH[hs, 1, t0:t0 + n], start=True, stop=True), [b_par, b_LH], [psb[pa]])
                    S.op("act", lambda e, pa=pa, tf=tf, n=n: e.activation(out=tf[:, :n], in_=ps[pa][:, :n], func=AF.Sigmoid, bias=VEC[:, 2 + d, o:o + 1]), [psb[pa], b_par], [tb])
                    S.op("dve", lambda e, tf=tf, t0=t0, n=n: e.tensor_tensor(out=AK[:, t0:t0 + n], in0=tf[:, :n], in1=KK[:, t0:t0 + n], op=ALU.mult), [tb, b_KK], [b_AK])
                    S.op("dve", lambda e, tf=tf, n=n: e.tensor_scalar(out=tf[:, :n], in0=tf[:, :n], scalar1=-1.0, scalar2=VEC[:, 5, o:o + 1], op0=ALU.add, op1=ALU.mult), [tb, b_par], [tb])
                    S.op("dve", lambda e, tf=tf, t0=t0, n=n: e.scalar_tensor_tensor(out=KD[:, t0:t0 + n], in0=tf[:, :n], scalar=1.0, in1=Kk[:, t0:t0 + n], op0=ALU.add, op1=ALU.mult), [tb, b_K], [b_KD])
                    tg, tgb = next_tmpf(C)
                    S.op("dve", lambda e, tg=tg, t0=t0, n=n: e.scalar_tensor_tensor(out=tg[:, :n], in0=KD[:, t0:t0 + n], scalar=VEC[:, 6, o:o + 1], in1=Rr[:, t0:t0 + n], op0=ALU.mult, op1=ALU.mult), [b_KD, b_R, b_par], [tgb])
                    S.op("pe", lambda e, tg=tg, n=n: e.matmul(ps[6][:, :n], BD[:], tg[:, :n], start=True, stop=True), [tgb, b_par], [psb[6]])
                    S.op("dve", lambda e, tg=tg, t0=t0, n=n: e.tensor_tensor(out=tg[:, :n], in0=ps[6][:, :n], in1=Vv[:, t0:t0 + n], op=ALU.mult), [psb[6], b_V], [tgb])
                    S.op("pool", lambda e, tg=tg, t0=t0, n=n: e.tensor_tensor(out=BON[:, t0:t0 + n], in0=BON[:, t0:t0 + n], in1=tg[:, :n], op=ALU.add), [tgb, b_BON], [b_BON])
                S.op("dve", lambda e: e.tensor_scalar(out=LW[:], in0=LW[:], scalar1=-float(np.exp(-0.5)), scalar2=None, op0=ALU.mult), [b_LW], [b_LW])
                for n_ in range(T // 64):
                    cs = slice(n_ * 64, (n_ + 1) * 64)
                    if d == 0:
                        S.op("dve", lambda e, cs=cs: e.tensor_tensor_scan(out=LC[:, cs], data0=C.ones[:, 0:64], data1=LW[:, cs], initial=0.0, op0=ALU.mult, op1=ALU.add), [b_LW, C.ones_b], [b_LC])
                    else:
                        rs_ = slice(n_ * 64 + 63, (n_ * 64 - 1) if n_ > 0 else None, -1)
                        S.op("dve", lambda e, rs_=rs_: e.tensor_tensor_scan(out=LC[:, rs_], data0=C.ones[:, 0:64], data1=LW[:, rs_], initial=0.0, op0=ALU.mult, op1=ALU.add), [b_LW, C.ones_b], [b_LC])
                S.barrier()
                lwg.__exit__(None, None, None)
                LW = None
                if C.stop == 'RP':
                    continue
                rwkv_recurrence(C, o, d, Rr, KK, KD, AK, LC, LW, Vtok, OACC, b_OACC, MASK4,
                                [b_R, b_KK, b_KD, b_AK, b_LC, b_LW, b_Vtok, b_par], Gg, b_G, BON, b_BON, VEC)
                S.barrier()


def rwkv_recurrence(C, o, d, Rr, KK, KD, AK, LC, LW, Vtok, OACC, b_OACC, MASK4, inb, Gg, b_G, BON, b_BON, VEC):
    nc, S = C.nc, C.S
    from contextlib import ExitStack
    ps, psb = C.ps, C.ps_b
    b_R, b_KK, b_KD, b_AK, b_LC, b_LW, b_Vtok, b_par = inb
    with ExitStack() as st:
        def sbt(name, shape, dt=F32):
            return st.enter_context(nc.sbuf_tensor(un(name), shape, dt))
        Hf = sbt("rHf", [128, 64]); HbP = [sbt("rHb%d" % h, [128, 64], BF16) for h in range(2)]; b_H = [Buf(), Buf()]
        NR = 2

        def rot(name, shape, dt, nh=1, nr=NR):
            return [[sbt("%s%d_%d" % (name, h, i), shape, dt) for i in range(nr)] for h in range(nh)], [[Buf() for i in range(nr)] for h in range(nh)]
        EX, b_EX = rot("rEX", [128, 4, 128], F32)
        FM, b_FM = rot("rFM", [128, 6, 128], BF16)
        KAt, b_KAt = rot("rKAt", [128, 2, 128], BF16)
        DD, b_DD = rot("rDD", [128, 512], BF16, 2)
        UT, b_UT = rot("rUT", [128, 128], F32, 2, 1)
        PP, b_PP = rot("rPP", [128, 2, 256], F32, 2, 1)
        XX, b_XX = rot("rXX", [128, 2, 128], F32, 2, 1)
        U32, b_U32 = rot("rU32", [128, 128], F32, 2, 1)
        TTB, b_TTB = rot("rTTB", [128, 128], BF16, 2)
        X1, b_X1 = rot("rX1", [128, 64], BF16, 2)
        NE, b_NE = rot("rNE", [128, 64], BF16, 2)
        FIN, b_FIN = rot("rFIN", [128, 128], F32)
        FNb, b_FNb = rot("rFNb", [128, 128], BF16)
        YT, b_YT = rot("rYT", [128, 128], BF16)
        ST, b_ST = rot("rST", [128, 2, 8], F32)
        S.op("pool", lambda e: e.memset(Hf[:], 0.0), (), b_H)
        for h_ in range(2):
            S.op("pool", lambda e, h_=h_: e.memset(HbP[h_][:], 0.0), (), [b_H[h_]])
        order = list(range(18)) if d == 0 else [1, 0] + list(range(17, 1, -1))
        if C.stop and C.stop.startswith('T'):
            order = order[:int(C.stop[1:])]
        it = 0
        for tl in order:
            tt = slice(tl * 128, (tl + 1) * 128)
            ri = it % NR
            it += 1
            ex, bex = EX[0][ri], b_EX[0][ri]
            fm, bfm = FM[0][ri], b_FM[0][ri]
            LC3 = LC[:, tt].rearrange("p (a c) -> p a c", a=2)
            endc = 63 if d == 0 else 0
            ex03 = ex[:, 0, :].rearrange("p (a c) -> p a c", a=2)
            if d == 0:
                S.op("dve", lambda e, ex03=ex03, LC3=LC3: e.tensor_copy(out=ex03[:, :, 1:64], in_=LC3[:, :, 0:63]), [b_LC], [bex])
                S.op("pool", lambda e, ex03=ex03: e.memset(ex03[:, :, 0:1], 0.0), (), [bex])
            else:
                S.op("dve", lambda e, ex03=ex03, LC3=LC3: e.tensor_copy(out=ex03[:, :, 0:63], in_=LC3[:, :, 1:64]), [b_LC], [bex])
                S.op("pool", lambda e, ex03=ex03: e.memset(ex03[:, :, 63:64], 0.0), (), [bex])
            S.op("pool", lambda e, ex=ex, tt=tt: e.tensor_copy(out=ex[:, 1, :], in_=LC[:, tt]), [b_LC], [bex])
            S.op("pool", lambda e, ex=ex, tt=tt: e.tensor_scalar(out=ex[:, 2, :], in0=LC[:, tt], scalar1=-1.0, scalar2=None, op0=ALU.mult), [b_LC], [bex])
            S.op("dve", lambda e, ex=ex, LC3=LC3: e.tensor_tensor(out=ex[:, 3, :].rearrange("p (a c) -> p a c", a=2), in0=LC3[:, :, endc:endc + 1].to_broadcast([128, 2, 64]), in1=LC3, op=ALU.subtract), [b_LC], [bex])
            S.op("act", lambda e, ex=ex: e.activation(out=ex[:], in_=ex[:], func=AF.Exp), [bex], [bex])
            S.op("dve", lambda e, fm=fm, ex=ex, tt=tt: e.tensor_tensor(out=fm[:, 0, :], in0=KK[:, tt], in1=ex[:, 0, :], op=ALU.mult), [b_KK, bex], [bfm])
            S.op("pool", lambda e, fm=fm, ex=ex, tt=tt: e.tensor_tensor(out=fm[:, 1, :], in0=Rr[:, tt], in1=ex[:, 1, :], op=ALU.mult), [b_R, bex], [bfm])
            S.op("dve", lambda e, fm=fm, ex=ex, tt=tt: e.tensor_tensor(out=fm[:, 2, :], in0=KD[:, tt], in1=ex[:, 2, :], op=ALU.mult), [b_KD, bex], [bfm])
            S.op("pool", lambda e, fm=fm, ex=ex, tt=tt: e.tensor_tensor(out=fm[:, 3, :], in0=AK[:, tt], in1=ex[:, 2, :], op=ALU.mult), [b_AK, bex], [bfm])
            S.op("dve", lambda e, fm=fm, ex=ex, tt=tt: e.tensor_tensor(out=fm[:, 4, :], in0=KD[:, tt], in1=ex[:, 3, :], op=ALU.mult), [b_KD, bex], [bfm])
            S.op("pool", lambda e, fm=fm, ex=ex, tt=tt: e.tensor_tensor(out=fm[:, 5, :], in0=AK[:, tt], in1=ex[:, 3, :], op=ALU.mult), [b_AK, bex], [bfm])
            kat, bkat = KAt[0][ri], b_KAt[0][ri]

            def ftk(pe, fm=fm):
                pe.transpose(C.psbf[3][:, 0:128], fm[:, 4, :], C.identb[:])
                return pe.transpose(C.psbf[3][:, 128:256], fm[:, 5, :], C.identb[:])
            S.op("pe", ftk, [bfm, C.ident_b], [psb[3]])
            S.op("act", lambda e, kat=kat: e.activation(out=kat[:].rearrange("p a b -> p (a b)"), in_=C.psbf[3][:, 0:256], func=AF.Copy), [psb[3]], [bkat])
            for h in range(2):
                hp = slice(h * 64, h * 64 + 64)
                hc = hp
                dd, bdd = DD[h][ri], b_DD[h][ri]
                pG = h

                def fgm(pe, fm=fm, pG=pG, hp=hp):
                    pe.matmul(ps[pG][:, 0:128], fm[hp, 3, :], fm[hp, 0, :], start=True, stop=True)
                    pe.matmul(ps[pG][:, 128:256], fm[hp, 2, :], fm[hp, 0, :], start=True, stop=True)
                    pe.matmul(ps[pG][:, 256:384], fm[hp, 2, :], fm[hp, 1, :], start=True, stop=True)
                    return pe.matmul(ps[pG][:, 384:512], fm[hp, 3, :], fm[hp, 1, :], start=True, stop=True)
                S.op("pe", fgm, [bfm], [psb[pG]])
                S.op("dve", lambda e, dd=dd, pG=pG: e.tensor_tensor(out=dd[:], in0=ps[pG][:, 0:512], in1=MASK4[:, d, :], op=ALU.mult), [psb[pG], b_par], [bdd])
                u32, bu32 = U32[h][0], b_U32[h][0]
                S.op("dve", lambda e, u32=u32, pG=pG: e.tensor_tensor(out=u32[:], in0=ps[pG][:, 0:128], in1=MASK4[:, d, 0:128], op=ALU.mult), [psb[pG], b_par], [bu32])
                ut, but = UT[h][0], b_UT[h][0]
                pS = 4 + h
                xx, bxx = XX[h][0], b_XX[h][0]
                pp, bpp = PP[h][0], b_PP[h][0]
                ttb, bttb = TTB[h][ri], b_TTB[h][ri]
                solve_fp32(C, u32[:], bu32, pS, ut, but, pp, bpp, xx, bxx, ttb, bttb)
                bxx = bttb
                TT = ttb[:]
                x1, bx1 = X1[h][ri], b_X1[h][ri]
                ne, bne = NE[h][ri], b_NE[h][ri]
                pX, pO, pH = 6, 2, 7
                for cb in ((0, 1) if d == 0 else (1, 0)):
                    pp_ = slice(cb * 64, cb * 64 + 64)
                    cc = pp_
                    pccol = cb * 64 + (63 if d == 0 else 0)

                    def fx1(pe, pp_=pp_, cc=cc, fm=fm, dd=dd, hp=hp, hc=hc, tl=tl, h=h):
                        pe.matmul(ps[pX][pp_, 0:64], fm[:, 0, cc], HbP[h][:, :], start=True, stop=False)
                        return pe.matmul(ps[pX][pp_, 0:64], dd[pp_, 128 + cc.start:128 + cc.start + 64], Vtok[pp_, tl, hc], start=False, stop=True)
                    S.op("pe", fx1, [bfm, bdd, b_H[h], b_Vtok], [psb[pX]])
                    S.op("act", lambda e, pp_=pp_, x1=x1: e.activation(out=x1[pp_, :], in_=ps[pX][pp_, 0:64], func=AF.Copy), [psb[pX]], [bx1])
                    S.op("pe", lambda e, pp_=pp_, cc=cc, TT=TT, x1=x1: e.matmul(ps[pX][pp_, 64:128], TT[pp_, cc], x1[pp_, :], start=True, stop=True), [bxx, bx1], [psb[pX]])
                    S.op("dve", lambda e, pp_=pp_, ne=ne: e.tensor_scalar(out=ne[pp_, :], in0=ps[pX][pp_, 64:128], scalar1=-1.0, scalar2=None, op0=ALU.mult), [psb[pX]], [bne])

                    def fo(pe, pp_=pp_, cc=cc, fm=fm, dd=dd, ne=ne, hp=hp, hc=hc, tl=tl, h=h):
                        pe.matmul(ps[pO][pp_, hc], fm[:, 1, cc], HbP[h][:, :], start=True, stop=False)
                        pe.matmul(ps[pO][pp_, hc], dd[pp_, 256 + cc.start:256 + cc.start + 64], Vtok[pp_, tl, hc], start=False, stop=False)
                        return pe.matmul(ps[pO][pp_, hc], dd[pp_, 384 + cc.start:384 + cc.start + 64], ne[pp_, :], start=False, stop=True)
                    S.op("pe", fo, [bfm, bdd, bne, b_H[h], b_Vtok], [psb[pO]])

                    def fh(pe, pp_=pp_, kat=kat, ne=ne, hp=hp, hc=hc, tl=tl):
                        pe.matmul(ps[pH][hp, 0:64], kat[pp_, 0, hc], Vtok[pp_, tl, hc], start=True, stop=False)
                        return pe.matmul(ps[pH][hp, 0:64], kat[pp_, 1, hc], ne[pp_, :], start=False, stop=True)
                    S.op("pe", fh, [bkat, bne, b_Vtok], [psb[pH]])
                    S.op("dve", lambda e, hp=hp, ex=ex, pccol=pccol: e.scalar_tensor_tensor(
                        out=Hf[hp, :], in0=Hf[hp, :], scalar=ex[hp, 1, pccol:pccol + 1], in1=ps[pH][hp, 0:64], op0=ALU.mult, op1=ALU.add),
                        [psb[pH], bex, b_H[h]], [b_H[h]])
                    S.op("act", lambda e, hp=hp, h=h: e.activation(out=HbP[h][hp, :], in_=Hf[hp, :], func=AF.Copy), [b_H[h]], [b_H[h]])
            pO = 2
            if d == 0:
                S.op("act", lambda e, tl=tl: e.activation(out=OACC[:, tl, :], in_=ps[pO][:, 0:128], func=AF.Copy), [psb[pO]], [b_OACC[tl]])
            else:
                fin, bfin = FIN[0][ri], b_FIN[0][ri]
                fnb, bfnb = FNb[0][ri], b_FNb[0][ri]
                yt, byt = YT[0][ri], b_YT[0][ri]
                stt, bst = ST[0][ri], b_ST[0][ri]
                S.op("dve", lambda e, fin=fin, tl=tl: e.tensor_tensor(out=fin[:], in0=ps[pO][:, 0:128], in1=OACC[:, tl, :], op=ALU.add), [psb[pO], b_OACC[tl]], [bfin])
                for h in range(2):
                    hc = slice(h * 64, h * 64 + 64)
                    tf, tb = next_tmpf(C)
                    S.op("act", lambda e, tf=tf, fin=fin, hc=hc, stt=stt, h=h: e.activation(out=tf[:, 0:64], in_=fin[:, hc], func=AF.Identity, accum_out=stt[:, h, 0:1]), [bfin], [tb, bst])
                    S.op("act", lambda e, tf=tf, fin=fin, hc=hc, stt=stt, h=h: e.activation(out=tf[:, 64:128], in_=fin[:, hc], func=AF.Square, accum_out=stt[:, h, 1:2]), [bfin], [tb, bst])
                S.op("dve", lambda e, stt=stt: e.tensor_scalar(out=stt[:, :, 2:3], in0=stt[:, :, 0:1], scalar1=1.0 / 64, scalar2=None, op0=ALU.mult), [bst], [bst])
                S.op("dve", lambda e, stt=stt: e.tensor_tensor(out=stt[:, :, 3:4], in0=stt[:, :, 2:3], in1=stt[:, :, 2:3], op=ALU.mult), [bst], [bst])
                S.op("dve", lambda e, stt=stt: e.scalar_tensor_tensor(out=stt[:, :, 4:5], in0=stt[:, :, 1:2], scalar=1.0 / 64, in1=stt[:, :, 3:4], op0=ALU.mult, op1=ALU.subtract), [bst], [bst])
                S.op("act", lambda e, stt=stt: e.activation(out=stt[:, :, 5:6], in_=stt[:, :, 4:5], func=AF.Sqrt, bias=64e-5), [bst], [bst])
                S.op("dve", lambda e, stt=stt: e.reciprocal(out=stt[:, :, 6:7], in_=stt[:, :, 5:6]), [bst], [bst])
                for h in range(2):
                    hc = slice(h * 64, h * 64 + 64)
                    S.op("dve", lambda e, fnb=fnb, fin=fin, hc=hc, stt=stt, h=h: e.tensor_scalar(out=fnb[:, hc], in0=fin[:, hc], scalar1=stt[:, h, 2:3], scalar2=stt[:, h, 6:7], op0=ALU.subtract, op1=ALU.mult), [bfin, bst], [bfnb])
                S.op("pe", lambda e, fnb=fnb: e.transpose(C.psbf[3][:, 512:640], fnb[:], C.identb[:]), [bfnb, C.ident_b], [psb[3]])
                tf, tb = next_tmpf(C)
                S.op("dve", lambda e, tf=tf: e.tensor_scalar(out=tf[:, 0:128], in0=C.psbf[3][:, 512:640], scalar1=VEC[:, 7, o:o + 1], scalar2=VEC[:, 8, o:o + 1], op0=ALU.mult, op1=ALU.add), [psb[3], b_par], [tb])
                S.op("pool", lambda e, tf=tf, tt=tt: e.tensor_tensor(out=tf[:, 0:128], in0=tf[:, 0:128], in1=BON[:, tt], op=ALU.add), [tb, b_BON], [tb])
                S.op("dve", lambda e, tf=tf, yt=yt, tt=tt: e.tensor_tensor(out=yt[:], in0=tf[:, 0:128], in1=Gg[:, tt], op=ALU.mult), [tb, b_G], [byt])
                S.dma("sp", C.Yscr[o, :, tt], yt[:], [byt], [C.Yscr_b])


def rwkv_out_proj(C, l, s):
    nc, S = C.nc, C.S
    from contextlib import ExitStack
    ps, psb = C.ps, C.ps_b
    S.barrier()
    with ExitStack() as st:
        def sbt(name, shape, dt):
            return st.enter_context(nc.sbuf_tensor(un(name), shape, dt))
        WO = sbt("rWO", [128, KT, KT, 128], BF16)
        YB = [sbt("rYB%d" % i, [128, KT, 512], BF16) for i in range(2)]
        b_WO = Buf()
        b_YB = [Buf(), Buf()]
        for o in range(KT):
            S.dma("pool", WO[:, o], C.rw_wout[o], (), [b_WO])
        it = 0
        for c, (t0, n, is_ctx) in enumerate(CHUNKS):
            col = NSEQ if is_ctx else s
            yb, byb = YB[c % 2], b_YB[c % 2]
            S.dma("sp", yb[:, :, 0:n], C.Yscr[0:KT, :, t0:t0 + n].rearrange("f p t -> p f t"), [C.Yscr_b], [byb])
            for o in range(KT):
                po = it % 2
                it += 1
                mm(S, ps[po][:, :n], [(WO[:, o, f, :], yb[:, f, 0:n]) for f in range(KT)], [b_WO, byb], [psb[po]])
                S.op("dve", lambda e, po=po, n=n, o=o, t0=t0, col=col: e.scalar_tensor_tensor(
                    out=C.R[:, o, t0:t0 + n], in0=ps[po][:, :n], scalar=C.MODS[:, l, 2 * 8 + o, col:col + 1],
                    in1=C.R[:, o, t0:t0 + n], op0=ALU.mult, op1=ALU.add),
                    [psb[po], C.MODS_b, C.Rb[o][c]], [C.Rb[o][c]])
        S.barrier()


def tile_w(w, ncol_tiles=None):
    K, N = w.shape
    return np.ascontiguousarray(w.reshape(K // 128, 128, N // 128, 128).transpose(2, 1, 0, 3))


def vec_t(v):
    sh = v.shape
    n = sh[-1] // 128
    a = v.reshape(sh[:-1] + (n, 128))
    return np.ascontiguousarray(np.moveaxis(a, -1, 0))


def host_shared(inp):
    sh = {}
    sh["mod_w_t"] = np.stack([tile_w(inp["mod_w"][l]) for l in range(DEPTH)])
    sh["mod_b_t"] = vec_t(inp["mod_b"])
    sh["norm_mix_t"] = vec_t(inp["norm_mix"])
    sh["norm_ffn_t"] = vec_t(inp["norm_ffn"])
    sh["final_norm_t"] = vec_t(inp["final_norm"])
    sh["ffn_w_in_t"] = np.stack([tile_w(inp["ffn_w_in"][l]) for l in range(DEPTH)])
    sh["ffn_w_out_t"] = np.stack([tile_w(inp["ffn_w_out"][l]) for l in range(DEPTH)])
    sh.update(host_gdn(inp))
    sh.update(host_s5(inp))
    sh.update(host_rwkv(inp))
    return sh


def host_rwkv(inp):
    sh = {}
    sh["rw_mu"] = vec_t(inp["rwkv_mu"][0])
    sh["rw_wrkv_t"] = np.stack([tile_w(inp["rwkv_w_rkv"][0, i]) for i in range(3)])
    w1cat = np.concatenate([inp["rwkv_w1"][0, 0], inp["rwkv_w1"][0, 1]], axis=1)
    a1cat = np.concatenate([inp["rwkv_a1"][0, 0], inp["rwkv_a1"][0, 1]], axis=1)
    sh["rw_l1_t"] = np.stack([tile_w(w1cat)[0], tile_w(a1cat)[0], tile_w(inp["rwkv_g1"][0])[0]])
    w2cat = np.concatenate([inp["rwkv_w2"][0, 0], inp["rwkv_w2"][0, 1]], axis=0)
    a2cat = np.concatenate([inp["rwkv_a2"][0, 0], inp["rwkv_a2"][0, 1]], axis=0)
    sh["rw_l2"] = np.ascontiguousarray(np.stack([w2cat, a2cat, inp["rwkv_g2"][0]]))
    vecs = [inp["rwkv_w0"][0, 0], inp["rwkv_w0"][0, 1], inp["rwkv_a0"][0, 0], inp["rwkv_a0"][0, 1],
            inp["rwkv_k_k"][0], inp["rwkv_k_a"][0], inp["rwkv_r_k"][0].reshape(-1), inp["rwkv_ln_w"][0], inp["rwkv_ln_b"][0]]
    sh["rw_vecs"] = vec_t(np.stack(vecs))
    sh["rw_wout_t"] = tile_w(inp["rwkv_w_out"][0])
    jj = np.arange(128)[:, None]
    tt = np.arange(128)[None, :]
    same = (jj // 64) == (tt // 64)
    m = np.zeros((2, 128, 512), np.float32)
    for d, (st_, inc_) in enumerate((((jj < tt), (jj <= tt)), ((jj > tt), (jj >= tt)))):
        m[d, :, 0:128] = same & st_
        m[d, :, 128:256] = same & st_
        m[d, :, 256:384] = same & inc_
        m[d, :, 384:512] = same & inc_
    sh["c_mask4"] = m
    return sh


def host_s5(inp):
    sh = {}
    lr, li = inp["s5_lambda_re"][0], inp["s5_lambda_im"][0]
    lam = np.stack([lr, li], 0)
    sh["s5_lam"] = np.ascontiguousarray(lam.transpose(3, 0, 1, 2).reshape(64, 2, 128))
    sh["s5_dt"] = np.ascontiguousarray(np.broadcast_to(inp["s5_log_dt"][0].reshape(1, 128), (64, 128)))
    B = np.stack([inp["s5_b_re"][0], inp["s5_b_im"][0]], 0)
    sh["s5_B"] = np.ascontiguousarray(B.transpose(3, 0, 1, 2, 4).reshape(64, 2, 128, 16))
    Cm = np.stack([inp["s5_c_re"][0], inp["s5_c_im"][0]], 0)
    sh["s5_C"] = np.ascontiguousarray(Cm.transpose(4, 0, 1, 2, 3).reshape(64, 2, 128, 16))
    sh["s5_dskip"] = vec_t(inp["s5_d"][0])
    sh["s5_wglu_t"] = tile_w(inp["s5_w_glu"][0])
    sh["s5_bglu_t"] = vec_t(inp["s5_b_glu"][0])
    sel = np.zeros((128, 4, 8, 128), np.float32)
    for r in range(128):
        q, c = (r % 64) // 16, r % 16
        for j in range(8):
            sel[r, q, j, 16 * j + c] = 1.0
    sh["c_sel4"] = sel
    jj = np.arange(128)[:, None] // 16
    tt = np.arange(128)[None, :] // 16
    sh["c_maskz"] = np.stack([(jj <= tt), (jj >= tt)]).astype(np.float32)
    return sh


def host_gdn(inp):
    sh = {}
    NG = inp["gdn_w_qkvz"].shape[0]
    sh["g_wqkvz_t"] = np.stack([tile_w(inp["gdn_w_qkvz"][j]) for j in range(NG)])
    cv = inp["gdn_conv"].reshape(NG, 5, 32, 128)
    sh["g_conv_t"] = np.ascontiguousarray(cv.transpose(0, 3, 2, 1))
    wab = inp["gdn_w_ab"]
    cat = np.concatenate([wab[:, 0], wab[:, 1]], axis=-1)
    cat = np.concatenate([cat, cat], axis=-1)
    sh["g_wab_t"] = np.ascontiguousarray(cat.reshape(NG, KT, 128, 128).transpose(0, 2, 1, 3))
    par = np.zeros((NG, 128, 8), np.float32)
    for c in range(128):
        d = (c % 64) // 32
        kind = ((c % 64) % 32) // 16
        hv = c % 16
        par[:, c, 0] = 1.0 if kind == 0 else -1.0
        if kind == 0:
            par[:, c, 1] = inp["gdn_dt_bias"][:, d, hv]
            par[:, c, 2] = inp["gdn_a_log"][:, d, hv]
        par[:, c, 3] = 1.0 if d == 1 else 0.0
        par[:, c, 4] = -1.0 if c >= 64 else 0.0
    sh["g_par"] = par
    sh["g_norm_t"] = np.ascontiguousarray(inp["gdn_norm"][:, :, None])
    sh["g_wout_t"] = np.stack([tile_w(inp["gdn_w_out"][j]) for j in range(NG)])
    sh["c_ident"] = np.eye(128, dtype=np.float32)
    jj = np.arange(128)[:, None]
    tt = np.arange(128)[None, :]
    same = (jj // 64) == (tt // 64)
    m = np.zeros((2, 128, 256), np.float32)
    m[0, :, 0:128] = same & (jj < tt)
    m[0, :, 128:256] = same & (jj <= tt)
    m[1, :, 0:128] = same & (jj > tt)
    m[1, :, 128:256] = same & (jj >= tt)
    sh["c_mask"] = m
    return sh


def host_core(inp, core, nseq=NSEQ):
    b0 = core * NSEQ
    m = {}
    xs = []
    for s in range(nseq):
        full = np.concatenate([inp["ctx"][b0 + s], inp["x"][b0 + s]], axis=0)
        xs.append(full.T.reshape(KT, 128, T))
    m["xT"] = np.ascontiguousarray(np.stack(xs))
    cc = np.concatenate([inp["c"][b0:b0 + NSEQ], inp["c_ctx"][None, :]], axis=0)
    m["cT"] = np.ascontiguousarray(cc.T.reshape(KT, 128, NSEQ + 1).transpose(1, 0, 2))
    return m


_CACHE = {}


def kernel(**inp):
    inp = {k: np.asarray(v, dtype=np.float32) for k, v in inp.items()}
    if "nc" not in _CACHE:
        _CACHE["nc"] = build_program()
    nc = _CACHE["nc"]
    shared = host_shared(inp)
    in_maps = []
    for core in range(NCORES):
        m = dict(shared)
        m.update(host_core(inp, core))
        in_maps.append(m)
    res = run_bass_kernel_spmd(nc, in_maps, core_ids=list(range(NCORES)))
    out = np.empty((NCORES * NSEQ, T_X, D), np.float32)
    for core in range(NCORES):
        o = res.results[core]["outT"]
        for s in range(NSEQ):
            out[core * NSEQ + s] = o[s].reshape(D, T_X).T
    return out
```

```python
import numpy as np
import concourse.bass as bass
import concourse.mybir as mybir
from concourse.bass_utils import run_bass_kernel_spmd

F32 = mybir.dt.float32
BF16 = mybir.dt.bfloat16
AF = mybir.ActivationFunctionType
ALU = mybir.AluOpType

D = 1024
KT = 8
T_CTX = 256
T_X = 2048
T = T_CTX + T_X
DEPTH = 4
FFN_H = 2816
FT = FFN_H // 128
NCORES = 8
NSEQ = 4
EPS = 1e-6
CHUNKS = [(0, 256, True), (256, 512, False), (768, 512, False), (1280, 512, False), (1792, 512, False)]
HALVES = [[0, 1, 2], [3, 4]]


class Buf:
    __slots__ = ("name", "w", "r", "excl")

    def __init__(self, name="", excl=False):
        self.name = name
        self.w = None
        self.r = {}
        self.excl = excl


class Sched:
    ENG = ("pe", "act", "dve", "pool", "sp")

    def __init__(self, nc, ndma=10):
        self.nc = nc
        self.eng = dict(pe=nc.tensor, act=nc.scalar, dve=nc.vector, pool=nc.gpsimd, sp=nc.sync)
        self.sems = {}
        self.cnt = {}
        for e in self.ENG:
            self.sems[e] = nc.alloc_semaphore("s_" + e)
            self.cnt[e] = 0
        self.seen = {e: {} for e in self.ENG}
        self.dq = {}
        for q in ("sp", "pool", "act"):
            keys = []
            for i in range(ndma):
                k = "d_%s%d" % (q, i)
                self.sems[k] = nc.alloc_semaphore(k)
                self.cnt[k] = 0
                keys.append(k)
            self.dq[q] = [keys, 0]
        self.nops = 0

    def _wait(self, e, key, val):
        if val <= 0 or self.seen[e].get(key, 0) >= val:
            return
        self.eng[e].wait_ge(self.sems[key], val)
        self.seen[e][key] = val

    def _deps(self, e, reads, writes):
        need = {}
        for b in reads:
            if b.w is not None and b.w[1] > need.get(b.w[0], 0):
                need[b.w[0]] = b.w[1]
        for b in writes:
            if b.w is not None and b.w[1] > need.get(b.w[0], 0):
                need[b.w[0]] = b.w[1]
            for k, v in b.r.items():
                if v > need.get(k, 0):
                    need[k] = v
        for k, v in need.items():
            self._wait(e, k, v)

    def op(self, e, fn, reads=(), writes=()):
        xr = [b for b in reads if b.excl]
        if xr:
            writes = list(writes) + xr
            reads = [b for b in reads if not b.excl]
        self._deps(e, reads, writes)
        ins = fn(self.eng[e])
        self.cnt[e] += 1
        ins.then_inc(self.sems[e], 1)
        v = self.cnt[e]
        for b in reads:
            b.r[e] = v
        for b in writes:
            b.w = (e, v)
            b.r = {}
        self.nops += 1

    def dma(self, q, out, in_, reads=(), writes=()):
        keys, idx = self.dq[q]
        k = keys[idx % len(keys)]
        self.dq[q][1] += 1
        self._deps(q, reads, writes)
        self._wait(q, k, self.cnt[k])
        self.eng[q].dma_start(out=out, in_=in_).then_inc(self.sems[k], 16)
        self.cnt[k] += 16
        v = self.cnt[k]
        for b in reads:
            b.r[k] = v
        for b in writes:
            b.w = (k, v)
            b.r = {}
        self.nops += 1

    def barrier(self, engines=None):
        for e in (engines or self.ENG):
            for k, v in self.cnt.items():
                self._wait(e, k, v)

    def final_wait(self, e="sp"):
        for k, v in self.cnt.items():
            self._wait(e, k, v)


class Ctx:
    pass


_UID = [0]


def un(name):
    _UID[0] += 1
    return "%s_%d" % (name, _UID[0])


def mm(S, out_ap, pairs, reads, writes):
    n = len(pairs)

    def fn(pe):
        ins = None
        for i, (l, r) in enumerate(pairs):
            ins = pe.matmul(out_ap, l, r, start=(i == 0), stop=(i == n - 1))
        return ins
    S.op("pe", fn, reads, writes)


def build_program(nseq=NSEQ, depth=DEPTH, mixers=True, debug=None, stop=None):
    nc = bass.Bass("TRN2", target_bir_lowering=False)
    C = Ctx()
    C.stop = stop
    import os as _os
    C.stop2 = int(_os.environ.get('STOP2', '0'))
    C.x0eng = _os.environ.get('X0ENG', 'pool')
    C.nh = int(_os.environ.get('NH', '2'))
    C.nlev = int(_os.environ.get('NLEV', '6'))
    C.nc = nc
    C.nseq = nseq
    S = Sched(nc)
    C.S = S

    def din(name, shape, dt=F32):
        return nc.dram_tensor(name, list(shape), dt, kind="ExternalInput").ap()

    C.xT = din("xT", [nseq, KT, 128, T])
    C.cT = din("cT", [128, KT, NSEQ + 1])
    C.mod_w = din("mod_w_t", [DEPTH, 48, 128, KT, 128])
    C.mod_b = din("mod_b_t", [128, DEPTH, 48])
    C.norm_mix = din("norm_mix_t", [128, DEPTH, KT])
    C.norm_ffn = din("norm_ffn_t", [128, DEPTH, KT])
    C.final_norm = din("final_norm_t", [128, KT])
    C.ffn_w_in = din("ffn_w_in_t", [DEPTH, 2 * FT, 128, KT, 128])
    C.ffn_w_out = din("ffn_w_out_t", [DEPTH, KT, 128, FT, 128])
    C.outT = nc.dram_tensor("outT", [nseq, KT, 128, T_X], F32, kind="ExternalOutput").ap()
    if debug:
        C.dbg = nc.dram_tensor("dbg", [KT, 128, T], F32, kind="ExternalOutput").ap()
        C.dbgY = nc.dram_tensor("dbgY", [16, 128, T], BF16, kind="ExternalOutput").ap()

    sb = nc.alloc_sbuf_tensor
    C.H = sb("H", [128, KT, T], BF16)
    C.Hb = [[Buf("H%d_%d" % (k, c)) for c in range(5)] for k in range(KT)]
    C.ones = sb("ones", [128, 128], F32)
    C.ones_b = Buf("ones")
    C.MODS = sb("MODS", [128, DEPTH, 48, NSEQ + 1], F32)
    C.MODS_b = Buf("MODS")
    C.AM = sb("AM", [128, DEPTH, 2, KT, NSEQ + 1], F32)
    C.AM_b = Buf("AM")
    C.nrm = sb("nrm", [128, 2, DEPTH, KT], F32)
    C.fnrm = sb("fnrm", [128, KT], F32)
    C.nrm_b = Buf("nrm")
    C.modb = sb("modb", [128, DEPTH, 48], F32)
    C.scT = sb("scT", [128, KT, NSEQ + 1], F32)
    C.scT_b = Buf("scT")
    C.tmpf = [sb("tmpf%d" % i, [128, 512], F32) for i in range(3)]
    C.tmpf_b = [Buf("tmpf%d" % i) for i in range(3)]
    C.tmpf_i = 0
    C.rs = sb("rs", [128, 512], F32)
    C.rs_b = Buf("rs")
    C.ps = [nc.alloc_psum_tensor("ps%d" % i, [128, 512], F32) for i in range(8)]
    C.psbf = [p[:].bitcast(BF16) for p in C.ps]
    C.ps_b = [Buf("ps%d" % i, excl=True) for i in range(8)]
    gdn_declare(C)
    s5_declare(C)
    rwkv_declare(C)

    prologue(C)
    consts_load(C)
    if mixers and depth > 1:
        s5_setup(C)
    for s in range(nseq):
        src = C.xT[s]
        for l in range(depth):
            kind, j = l % 3, l // 3
            with r_scope(C, src):
                norm_modulate(C, l, 0, s)
            src = C.Rscr
            if mixers:
                if kind == 0:
                    gdn_mixer(C, l, j, s)
                    if debug and debug[1] == l and s == 0:
                        S.barrier()
                        for f in range(16):
                            S.dma("sp", C.dbgY[f], C.Yscr[f], (), ())
                        S.barrier()
                elif kind == 1:
                    s5_mixer(C, l, s)
                else:
                    rwkv_mixer(C, l, s)
            with r_scope(C, src):
                if mixers:
                    if kind == 0:
                        gdn_out_proj(C, l, j, s)
                    elif kind == 1:
                        apply_mixer_out(C, l, s)
                    else:
                        rwkv_out_proj(C, l, s)
                if debug == ("xmix", l) and s == 0:
                    dump_R(C)
                norm_modulate(C, l, 1, s)
                ffn(C, l, s)
                if debug == ("x", l) and s == 0:
                    dump_R(C)
                if l == depth - 1:
                    final_norm_store(C, s)
    S.final_wait("sp")
    return nc


class r_scope:
    def __init__(self, C, src):
        self.C = C
        self.src = src

    def __enter__(self):
        C = self.C
        C.S.barrier()
        self.g = C.nc.sbuf_tensor(un("R"), [128, KT, T], F32)
        C.R = self.g.__enter__()
        C.Rb = [[Buf("R%d_%d" % (k, c)) for c in range(5)] for k in range(KT)]
        for k in range(KT):
            C.S.dma("sp" if k % 2 == 0 else "act", C.R[:, k, :], self.src[k], (), C.Rb[k])
        return self

    def __exit__(self, *a):
        C = self.C
        for k in range(KT):
            C.S.dma("sp" if k % 2 == 0 else "act", C.Rscr[k], C.R[:, k, :], C.Rb[k], ())
        C.S.barrier()
        self.g.__exit__(*a)
        C.R = None
        return False


def next_tmpf(C):
    i = C.tmpf_i % len(C.tmpf)
    C.tmpf_i += 1
    return C.tmpf[i], C.tmpf_b[i]


def dump_R(C):
    S = C.S
    for k in range(KT):
        S.dma("sp", C.dbg[k], C.R[:, k, :], reads=C.Rb[k], writes=())


def prologue(C):
    nc, S = C.nc, C.S
    S.op("dve", lambda e: e.memset(C.ones[:], 1.0), (), [C.ones_b])
    S.dma("sp", C.scT[:], C.cT, (), [C.scT_b])
    S.dma("sp", C.modb[:], C.mod_b, (), [C.nrm_b])
    S.dma("sp", C.nrm[:, 0], C.norm_mix, (), [C.nrm_b])
    S.dma("sp", C.nrm[:, 1], C.norm_ffn, (), [C.nrm_b])
    S.dma("sp", C.fnrm[:], C.final_norm, (), [C.nrm_b])
    S.op("act", lambda e: e.activation(out=C.scT[:], in_=C.scT[:], func=AF.Silu), [C.scT_b], [C.scT_b])
    NW = 3
    with (nc.sbuf_tensor(un("wm0"), [128, KT, 128], F32) as wm0,
          nc.sbuf_tensor(un("wm1"), [128, KT, 128], F32) as wm1,
          nc.sbuf_tensor(un("wm2"), [128, KT, 128], F32) as wm2):
        wm = [wm0, wm1, wm2]
        wm_b = [Buf("wm%d" % i) for i in range(NW)]
        i = 0
        for l in range(DEPTH):
            for f in range(48):
                w, wb = wm[i % NW], wm_b[i % NW]
                S.dma("sp" if i % 2 == 0 else "act", w[:], C.mod_w[l, f], (), [wb])
                pb = i % 4
                mm(S, C.ps[pb][:, 0:NSEQ + 1],
                   [(w[:, k, :], C.scT[:, k, :]) for k in range(KT)],
                   [wb, C.scT_b], [C.ps_b[pb]])
                S.op("dve", lambda e, l=l, f=f, pb=pb: e.tensor_scalar(
                    out=C.MODS[:, l, f, :], in0=C.ps[pb][:, 0:NSEQ + 1], scalar1=C.modb[:, l, f:f + 1],
                    scalar2=None, op0=ALU.add), [C.ps_b[pb], C.nrm_b], [C.MODS_b])
                i += 1
        S.barrier()
    for l in range(DEPTH):
        for sub, j in ((0, 1), (1, 4)):
            S.op("dve", lambda e, l=l, sub=sub, j=j: e.scalar_tensor_tensor(
                out=C.AM[:, l, sub], in0=C.MODS[:, l, j * 8:(j + 1) * 8, :], scalar=1.0,
                in1=C.nrm[:, sub, l, :].unsqueeze(2).to_broadcast([128, KT, NSEQ + 1]),
                op0=ALU.add, op1=ALU.mult), [C.MODS_b, C.nrm_b], [C.AM_b])
    S.barrier()


def load_seq(C, s):
    S = C.S
    for k in range(KT):
        S.dma("sp" if k % 2 == 0 else "act", C.R[:, k, :], C.xT[s, k], (), C.Rb[k])


def rstd_chunk(C, c):
    S = C.S
    t0, n, _ = CHUNKS[c]
    pb = 7
    for k in range(KT):
        tf, tb = next_tmpf(C)
        S.op("act", lambda e, k=k, tf=tf: e.activation(out=tf[:, :n], in_=C.R[:, k, t0:t0 + n], func=AF.Square),
             [C.Rb[k][c]], [tb])
        S.op("pe", lambda e, k=k, tf=tf: e.matmul(C.ps[pb][:, :n], C.ones[:], tf[:, :n], start=(k == 0), stop=(k == KT - 1)),
             [tb, C.ones_b], [C.ps_b[pb]])
    S.op("act", lambda e: e.activation(out=C.rs[:, :n], in_=C.ps[pb][:, :n], func=AF.Sqrt, scale=1.0 / D, bias=EPS),
         [C.ps_b[pb]], [C.rs_b])
    S.op("dve", lambda e: e.reciprocal(out=C.rs[:, :n], in_=C.rs[:, :n]), [C.rs_b], [C.rs_b])


def norm_modulate(C, l, sub, s):
    S = C.S
    jshift = 0 if sub == 0 else 3
    for c, (t0, n, is_ctx) in enumerate(CHUNKS):
        col = NSEQ if is_ctx else s
        rstd_chunk(C, c)
        for k in range(KT):
            tf, tb = next_tmpf(C)
            S.op("dve", lambda e, k=k, tf=tf: e.tensor_tensor(out=tf[:, :n], in0=C.R[:, k, t0:t0 + n], in1=C.rs[:, :n], op=ALU.mult),
                 [C.Rb[k][c], C.rs_b], [tb])
            S.op("act", lambda e, k=k, tf=tf: e.activation(
                out=C.H[:, k, t0:t0 + n], in_=tf[:, :n], func=AF.Identity,
                scale=C.AM[:, l, sub, k, col:col + 1], bias=C.MODS[:, l, jshift * 8 + k, col:col + 1]),
                [tb, C.AM_b, C.MODS_b], [C.Hb[k][c]])


def ffn(C, l, s):
    nc, S = C.nc, C.S
    jg = 5
    S.barrier()
    with (
        nc.sbuf_tensor(un("ACTB"), [128, FT, 1280], BF16) as ACTB,
        nc.sbuf_tensor(un("wi0"), [128, 2, KT, 128], BF16) as wi0,
        nc.sbuf_tensor(un("wi1"), [128, 2, KT, 128], BF16) as wi1,
        nc.sbuf_tensor(un("wo0"), [128, FT, 128], BF16) as wo0,
        nc.sbuf_tensor(un("wo1"), [128, FT, 128], BF16) as wo1,
        nc.sbuf_tensor(un("sg0"), [128, 512], F32) as sg0,
        nc.sbuf_tensor(un("sg1"), [128, 512], F32) as sg1,
    ):
        wi, wi_b = [wi0, wi1], [Buf("wi0"), Buf("wi1")]
        wo, wo_b = [wo0, wo1], [Buf("wo0"), Buf("wo1")]
        sg, sg_b = [sg0, sg1], [Buf("sg0"), Buf("sg1")]
        it = 0
        for half in HALVES:
            hb = CHUNKS[half[0]][0]
            AB = [[Buf("AB") for _ in half] for _ in range(FT)]
            for f in range(FT):
                w, wb = wi[f % 2], wi_b[f % 2]
                S.dma("pool", w[:, 0], C.ffn_w_in[l, f], (), [wb])
                S.dma("pool", w[:, 1], C.ffn_w_in[l, FT + f], (), [wb])
                for ci, c in enumerate(half):
                    t0, n, is_ctx = CHUNKS[c]
                    pg, pu = (it % 3) * 2, (it % 3) * 2 + 1
                    hreads = [C.Hb[k][c] for k in range(KT)]
                    mm(S, C.ps[pg][:, :n], [(w[:, 0, k, :], C.H[:, k, t0:t0 + n]) for k in range(KT)],
                       [wb] + hreads, [C.ps_b[pg]])
                    mm(S, C.ps[pu][:, :n], [(w[:, 1, k, :], C.H[:, k, t0:t0 + n]) for k in range(KT)],
                       [wb] + hreads, [C.ps_b[pu]])
                    g_, gb = sg[it % 2], sg_b[it % 2]
                    S.op("act", lambda e, pg=pg, g_=g_, n=n: e.activation(out=g_[:, :n], in_=C.ps[pg][:, :n], func=AF.Silu),
                         [C.ps_b[pg]], [gb])
                    S.op("dve", lambda e, pu=pu, g_=g_, n=n, f=f, t0=t0: e.tensor_tensor(
                        out=ACTB[:, f, t0 - hb:t0 - hb + n], in0=g_[:, :n], in1=C.ps[pu][:, :n], op=ALU.mult),
                        [gb, C.ps_b[pu]], [AB[f][ci]])
                    it += 1
            for o in range(KT):
                w, wb = wo[o % 2], wo_b[o % 2]
                S.dma("pool", w[:], C.ffn_w_out[l, o], (), [wb])
                for ci, c in enumerate(half):
                    t0, n, is_ctx = CHUNKS[c]
                    col = NSEQ if is_ctx else s
                    po = 6 + (it % 2)
                    it += 1
                    mm(S, C.ps[po][:, :n], [(w[:, f, :], ACTB[:, f, t0 - hb:t0 - hb + n]) for f in range(FT)],
                       [wb] + [AB[f][ci] for f in range(FT)], [C.ps_b[po]])
                    S.op("dve", lambda e, po=po, n=n, o=o, t0=t0, col=col: e.scalar_tensor_tensor(
                        out=C.R[:, o, t0:t0 + n], in0=C.ps[po][:, :n], scalar=C.MODS[:, l, jg * 8 + o, col:col + 1],
                        in1=C.R[:, o, t0:t0 + n], op0=ALU.mult, op1=ALU.add),
                        [C.ps_b[po], C.MODS_b, C.Rb[o][c]], [C.Rb[o][c]])
        S.barrier()


def final_norm_store(C, s):
    S = C.S
    for c, (t0, n, is_ctx) in enumerate(CHUNKS):
        if is_ctx:
            continue
        rstd_chunk(C, c)
        for k in range(KT):
            tf, tb = next_tmpf(C)
            S.op("dve", lambda e, k=k, tf=tf: e.scalar_tensor_tensor(
                out=tf[:, :n], in0=C.R[:, k, t0:t0 + n], scalar=C.fnrm[:, k:k + 1], in1=C.rs[:, :n],
                op0=ALU.mult, op1=ALU.mult), [C.Rb[k][c], C.rs_b, C.nrm_b], [tb])
            S.dma("sp", C.outT[s, k, :, t0 - T_CTX:t0 - T_CTX + n], tf[:, :n], [tb], ())
    S.barrier()


def tile_tokens(tile):
    return tile * 128


def gdn_declare(C):
    nc = C.nc

    def din(name, shape, dt=F32):
        return nc.dram_tensor(name, list(shape), dt, kind="ExternalInput").ap()
    C.g_wqkvz = din("g_wqkvz_t", [2, 48, 128, KT, 128])
    C.g_conv = din("g_conv_t", [2, 128, 32, 5])
    C.g_wab = din("g_wab_t", [2, 128, KT, 128])
    C.g_par = din("g_par", [2, 128, 8])
    C.g_norm = din("g_norm_t", [2, 128, 1])
    C.g_wout = din("g_wout_t", [2, KT, 128, 16, 128])
    C.c_ident = din("c_ident", [128, 128])
    C.c_mask = din("c_mask", [2, 128, 256])
    C.Yscr = nc.dram_tensor("Yscr", [16, 128, T], BF16, kind="Internal").ap()
    sb = nc.alloc_sbuf_tensor
    C.ident = sb("ident", [128, 128], F32)
    C.identb = sb("identb", [128, 128], BF16)
    C.ident_b = Buf("ident")
    C.Yscr_b = Buf("Yscr")
    C.Rscr = nc.dram_tensor("Rscr", [KT, 128, T], F32, kind="Internal").ap()


def consts_load(C):
    S = C.S
    S.dma("sp", C.ident[:], C.c_ident, (), [C.ident_b])
    S.op("dve", lambda e: e.tensor_copy(out=C.identb[:], in_=C.ident[:]), [C.ident_b], [C.ident_b])


def gdn_mixer(C, l, j, s):
    nc, S = C.nc, C.S
    from contextlib import ExitStack
    S.barrier()
    ps = C.ps
    psb = C.ps_b
    with ExitStack() as st:
        def sbt(name, shape, dt):
            return st.enter_context(nc.sbuf_tensor(un(name), shape, dt))
        GCALL = sbt("GCALL", [128, T], F32)
        TOKP = sbt("TOKP", [128, 18, 6, 32], F32)
        GPAR = sbt("GPAR", [128, 8], F32)
        MASK = sbt("MASK", [128, 2, 256], F32)
        GNW = sbt("GNW", [128, 1], F32)
        CONVW = sbt("CONVW", [128, 32, 5], F32)
        b_GC, b_TOKP, b_par = Buf("GCALL"), Buf("TOKP"), Buf("gpar")
        S.dma("sp", GPAR[:], C.g_par[j], (), [b_par])
        S.dma("sp", MASK[:, 0, :], C.c_mask[0], (), [b_par])
        S.dma("sp", MASK[:, 1, :], C.c_mask[1], (), [b_par])
        S.dma("sp", GNW[:], C.g_norm[j], (), [b_par])
        S.dma("sp", CONVW[:], C.g_conv[j], (), [b_par])
        S.op("act", lambda e: e.activation(out=GPAR[:, 5:6], in_=GPAR[:, 2:3], func=AF.Exp), [b_par], [b_par])
        S.op("dve", lambda e: e.tensor_scalar(out=GPAR[:, 5:6], in0=GPAR[:, 5:6], scalar1=-1.0, scalar2=None, op0=ALU.mult), [b_par], [b_par])

        with ExitStack() as sa:
            def sba(name, shape, dt):
                return sa.enter_context(nc.sbuf_tensor(un(name), shape, dt))
            GL = sba("GL", [128, T], F32)
            PF = sba("PF", [128, T], F32)
            TMP = sba("TMPA", [128, T], F32)
            WAB = sba("WAB", [128, KT, 128], BF16)
            GLt = sba("GLt", [128, 128], F32)
            b_GL, b_PF, b_TMP, b_WAB, b_GLt = Buf(), Buf(), Buf(), Buf(), Buf()
            S.dma("pool", WAB[:], C.g_wab[j], (), [b_WAB])
            for c, (t0, n, _) in enumerate(CHUNKS):
                pb = c % 2
                mm(S, ps[pb][:, :n], [(WAB[:, k, :], C.H[:, k, t0:t0 + n]) for k in range(KT)],
                   [b_WAB] + [C.Hb[k][c] for k in range(KT)], [psb[pb]])
                tf, tb = next_tmpf(C)
                S.op("act", lambda e, pb=pb, tf=tf, n=n: e.activation(out=tf[:, :n], in_=ps[pb][:, :n], func=AF.Exp,
                                                                  scale=GPAR[:, 0:1], bias=GPAR[:, 1:2]), [psb[pb], b_par], [tb])
                S.op("act", lambda e, tf=tf, n=n: e.activation(out=tf[:, :n], in_=tf[:, :n], func=AF.Ln, bias=1.0), [tb], [tb])
                S.op("dve", lambda e, tf=tf, n=n, t0=t0: e.tensor_scalar(out=GL[:, t0:t0 + n], in0=tf[:, :n], scalar1=GPAR[:, 5:6],
                                                                       scalar2=None, op0=ALU.mult), [tb, b_par], [b_GL])
            for n_ in range(T // 64):
                S.op("dve", lambda e, n_=n_: e.tensor_tensor_scan(
                    out=PF[:, n_ * 64:(n_ + 1) * 64], data0=C.ones[:, 0:64], data1=GL[:, n_ * 64:(n_ + 1) * 64],
                    initial=0.0, op0=ALU.mult, op1=ALU.add), [b_GL, C.ones_b], [b_PF])
            PF3 = PF[:].rearrange("p (n c) -> p n c", c=64)
            TB = PF3[:, :, 63:64].to_broadcast([128, T // 64, 64])
            TMP3 = TMP[:].rearrange("p (n c) -> p n c", c=64)
            GL3 = GL[:].rearrange("p (n c) -> p n c", c=64)
            GC3 = GCALL[:].rearrange("p (n c) -> p n c", c=64)
            S.op("dve", lambda e: e.tensor_tensor(out=TMP3, in0=TB, in1=PF3, op=ALU.subtract), [b_PF], [b_TMP])
            S.op("dve", lambda e: e.tensor_tensor(out=TMP[:], in0=TMP[:], in1=PF[:], op=ALU.subtract), [b_TMP, b_PF], [b_TMP])
            S.op("dve", lambda e: e.tensor_tensor(out=TMP[:], in0=TMP[:], in1=GL[:], op=ALU.add), [b_TMP, b_GL], [b_TMP])
            S.op("dve", lambda e: e.scalar_tensor_tensor(out=TMP[:], in0=TMP[:], scalar=GPAR[:, 3:4], in1=PF[:],
                                                         op0=ALU.mult, op1=ALU.add), [b_TMP, b_PF, b_par], [b_TMP])
            S.op("dve", lambda e: e.scalar_tensor_tensor(out=GCALL[:], in0=GL[:], scalar=GPAR[:, 4:5], in1=TMP[:],
                                                         op0=ALU.mult, op1=ALU.add), [b_TMP, b_GL, b_par], [b_GC])
            S.op("dve", lambda e: e.tensor_tensor(out=TMP3, in0=TB, in1=GC3, op=ALU.subtract), [b_PF, b_GC, b_TMP], [b_TMP])
            for tl in range(18):
                tt = slice(tl * 128, (tl + 1) * 128)
                pb = 2 + (tl % 2)

                def ftr(pe, tt=tt, pb=pb):
                    pe.transpose(ps[pb][:, 0:128], GCALL[:, tt], C.ident[:])
                    pe.transpose(ps[pb][:, 128:256], TMP[:, tt], C.ident[:])
                    return pe.transpose(ps[pb][:, 256:384], GL[:, tt], C.ident[:])
                S.op("pe", ftr, [b_GC, b_TMP, b_GL, C.ident_b], [psb[pb]])
                S.op("act", lambda e, pb=pb: e.activation(out=GLt[:], in_=ps[pb][:, 256:384], func=AF.Copy), [psb[pb]], [b_GLt])
                def rows(ap_, base):
                    return ap_[:, base:base + 64].rearrange("p (d r) -> p d r", d=2)[:, :, 0:16]
                logb = rows(GLt[:], 16)
                gc_ = rows(ps[pb][:, 0:128], 0)
                gcx_ = rows(ps[pb][:, 0:128], 64)
                dk_ = rows(ps[pb][:, 128:256], 0)
                da_ = rows(ps[pb][:, 128:256], 64)

                def outv(w):
                    return TOKP[:, tl, w, :].rearrange("p (d r) -> p d r", d=2)
                for w, src, op_ in ((0, gcx_, ALU.subtract), (1, gc_, ALU.subtract), (2, gc_, ALU.subtract), (3, gcx_, ALU.subtract),
                                    (4, dk_, ALU.add), (5, da_, ALU.add)):
                    S.op("dve", lambda e, w=w, src=src, op_=op_, outv=outv, logb=logb: e.tensor_tensor(out=outv(w), in0=src, in1=logb, op=op_),
                         [psb[pb], b_GLt], [b_TOKP])
            S.op("act", lambda e: e.activation(out=TOKP[:, :, 4:6, :], in_=TOKP[:, :, 4:6, :], func=AF.Exp), [b_TOKP], [b_TOKP])
            S.barrier()

        for hg in range(8 if not C.stop else (0 if C.stop == 'A' else 1)):
            gdn_head_group(C, l, j, s, hg, GCALL, b_GC, TOKP, b_TOKP, MASK, GNW, CONVW, b_par)
        S.barrier()


def gdn_head_group(C, l, j, s, hg, GCALL, b_GC, TOKP, b_TOKP, MASK, GNW, CONVW, b_par):
    nc, S = C.nc, C.S
    from contextlib import ExitStack
    ps, psb = C.ps, C.ps_b
    with ExitStack() as st:
        def sbt(name, shape, dt):
            return st.enter_context(nc.sbuf_tensor(un(name), shape, dt))
        QT = sbt("QT", [128, T], BF16)
        KTt = sbt("KTt", [128, T], BF16)
        Vtok = sbt("Vtok", [128, 18, 2, 128], BF16)
        Ktok = sbt("Ktok", [128, 18, 128], BF16)
        SZ = sbt("SZ", [128, 2, T], BF16)
        b_QT, b_KT, b_Vtok, b_Ktok, b_SZ = Buf(), Buf(), Buf(), Buf(), Buf()
        with ExitStack() as sp_:
            def sbp(name, shape, dt):
                return sp_.enter_context(nc.sbuf_tensor(un(name), shape, dt))
            PRE = sbp("PRE", [128, T + 8], F32)
            CONVO = sbp("CONVO", [128, T + 4], F32)
            VT = sbp("VT", [128, T], BF16)
            W0 = sbp("W0", [128, KT, 128], BF16)
            W1 = sbp("W1", [128, KT, 128], BF16)
            Wb = [W0, W1]
            b_W = [Buf(), Buf()]
            b_PRE, b_CONVO, b_VT = Buf(), Buf(), Buf()
            S.op("pool", lambda e: e.memset(PRE[:], 0.0), (), [b_PRE])
            feats = [("q", hg), ("k", 8 + hg), ("v0", 16 + 2 * hg), ("v1", 16 + 2 * hg + 1),
                     ("z0", 32 + 2 * hg), ("z1", 32 + 2 * hg + 1)]
            for fi, (kind, f) in enumerate(feats):
                W, bW = Wb[fi % 2], b_W[fi % 2]
                S.dma("pool", W[:], C.g_wqkvz[j, f], (), [bW])
                for c, (t0, n, is_ctx) in enumerate(CHUNKS):
                    pb = c % 2
                    mm(S, ps[pb][:, :n], [(W[:, k, :], C.H[:, k, t0:t0 + n]) for k in range(KT)],
                       [bW] + [C.Hb[k][c] for k in range(KT)], [psb[pb]])
                    if kind[0] == "z":
                        S.op("act", lambda e, pb=pb, n=n, t0=t0, kind=kind: e.activation(
                            out=SZ[:, int(kind[1]), t0:t0 + n], in_=ps[pb][:, :n], func=AF.Silu), [psb[pb]], [b_SZ])
                    else:
                        off = t0 + 2 if is_ctx else t0 + 6
                        S.op("act", lambda e, pb=pb, n=n, off=off: e.activation(
                            out=PRE[:, off:off + n], in_=ps[pb][:, :n], func=AF.Copy), [psb[pb]], [b_PRE])
                if kind[0] == "z":
                    continue
                NW_ = T + 4
                S.op("dve", lambda e, f=f: e.tensor_scalar(out=CONVO[:, 0:NW_], in0=PRE[:, 0:NW_], scalar1=CONVW[:, f, 0:1],
                                                          scalar2=None, op0=ALU.mult), [b_PRE, b_par], [b_CONVO])
                for tap in range(1, 5):
                    S.op("dve", lambda e, f=f, tap=tap: e.scalar_tensor_tensor(
                        out=CONVO[:, 0:NW_], in0=PRE[:, tap:tap + NW_], scalar=CONVW[:, f, tap:tap + 1], in1=CONVO[:, 0:NW_],
                        op0=ALU.mult, op1=ALU.add), [b_PRE, b_CONVO, b_par], [b_CONVO])
                if kind[0] == "v":
                    hvl = int(kind[1])
                    S.op("act", lambda e: e.activation(out=VT[:, 0:T_CTX], in_=CONVO[:, 0:T_CTX], func=AF.Silu), [b_CONVO], [b_VT])
                    S.op("act", lambda e: e.activation(out=VT[:, T_CTX:T], in_=CONVO[:, T_CTX + 4:T + 4], func=AF.Silu), [b_CONVO], [b_VT])
                    for g4 in range(0, 18, 6):
                        pb = 2 + ((g4 // 6) % 2)

                        def ftr(pe, g4=g4, pb=pb):
                            ins = None
                            for i in range(6):
                                ins = pe.transpose(C.psbf[pb][:, i * 128:(i + 1) * 128], VT[:, (g4 + i) * 128:(g4 + i + 1) * 128], C.identb[:])
                            return ins
                        S.op("pe", ftr, [b_VT, C.ident_b], [psb[pb]])
                        S.op("dve", lambda e, g4=g4, pb=pb, hvl=hvl: e.tensor_copy(
                            out=Vtok[:, g4:g4 + 6, hvl, :], in_=C.psbf[pb][:, 0:768].rearrange("p (a b) -> p a b", a=6)), [psb[pb]], [b_Vtok])
                else:
                    S.op("act", lambda e: e.activation(out=CONVO[:, 0:T_CTX], in_=CONVO[:, 0:T_CTX], func=AF.Silu), [b_CONVO], [b_CONVO])
                    S.op("act", lambda e: e.activation(out=CONVO[:, T_CTX + 4:T + 4], in_=CONVO[:, T_CTX + 4:T + 4], func=AF.Silu), [b_CONVO], [b_CONVO])
                    dst, bdst = (QT, b_QT) if kind == "q" else (KTt, b_KT)
                    qscale = 128.0 ** -0.5 if kind == "q" else 1.0
                    for c, (t0, n, is_ctx) in enumerate(CHUNKS):
                        off = t0 if is_ctx else t0 + 4
                        tf, tb = next_tmpf(C)
                        S.op("act", lambda e, tf=tf, n=n, off=off: e.activation(out=tf[:, :n], in_=CONVO[:, off:off + n], func=AF.Square), [b_CONVO], [tb])
                        S.op("pe", lambda e, tf=tf, n=n: e.matmul(ps[7][:, :n], C.ones[:], tf[:, :n], start=True, stop=True), [tb, C.ones_b], [psb[7]])
                        S.op("act", lambda e, n=n: e.activation(out=C.rs[:, :n], in_=ps[7][:, :n], func=AF.Sqrt, bias=1e-6), [psb[7]], [C.rs_b])
                        S.op("dve", lambda e, n=n: e.reciprocal(out=C.rs[:, :n], in_=C.rs[:, :n]), [C.rs_b], [C.rs_b])
                        S.op("dve", lambda e, n=n, off=off, t0=t0, dst=dst, qscale=qscale: e.scalar_tensor_tensor(
                            out=dst[:, t0:t0 + n], in0=CONVO[:, off:off + n], scalar=qscale, in1=C.rs[:, :n], op0=ALU.mult, op1=ALU.mult),
                            [b_CONVO, C.rs_b], [bdst])
                    if kind == "k":
                        for g4 in range(0, 18, 6):
                            pb = 2 + ((g4 // 6) % 2)

                            def ftr(pe, g4=g4, pb=pb):
                                ins = None
                                for i in range(6):
                                    ins = pe.transpose(C.psbf[pb][:, i * 128:(i + 1) * 128], KTt[:, (g4 + i) * 128:(g4 + i + 1) * 128], C.identb[:])
                                return ins
                            S.op("pe", ftr, [b_KT, C.ident_b], [psb[pb]])
                            S.op("dve", lambda e, g4=g4, pb=pb: e.tensor_copy(
                                out=Ktok[:, g4:g4 + 6, :], in_=C.psbf[pb][:, 0:768].rearrange("p (a b) -> p a b", a=6)), [psb[pb]], [b_Ktok])
            S.barrier()
        if C.stop == 'P':
            return
        gdn_recurrence(C, l, j, s, hg, GCALL, b_GC, TOKP, b_TOKP, MASK, GNW, b_par, QT, KTt, Vtok, Ktok, SZ,
                       [b_QT, b_KT, b_Vtok, b_Ktok, b_SZ])
        S.barrier()


def run_lockstep(gens):
    live = list(gens)
    while live:
        nxt = []
        for g in live:
            try:
                next(g)
                nxt.append(g)
            except StopIteration:
                pass
        live = nxt


def gdn_recurrence(C, l, j, s, hg, GCALL, b_GC, TOKP, b_TOKP, MASK, GNW, b_par, QT, KTt, Vtok, Ktok, SZ, inb):
    nc, S = C.nc, C.S
    from contextlib import ExitStack
    ps, psb = C.ps, C.ps_b
    b_QT, b_KT, b_Vtok, b_Ktok, b_SZ = inb
    order1 = [1, 0] + list(range(17, 1, -1))
    pos1 = {tl: i for i, tl in enumerate(order1)}
    with ExitStack() as st:
        def sbt(name, shape, dt):
            return st.enter_context(nc.sbuf_tensor(un(name), shape, dt))
        OACC = sbt("OACC", [128, 18, 2, 128], F32)
        b_OACC = [[Buf() for _ in range(2)] for _ in range(18)]
        fin_it = [0]

        def chain(h, d):
            c = 2 * d + h
            hv = 2 * hg + h
            idx = d * 16 + hv
            rc = d * 32 + hv
            rx = 64 + rc
            pR, pGm, pX, pO, pS = c // 2, 2, 2, 3, 4 + c
            rb0 = (c % 2) * 256
            RB = ps[pR][:, rb0:rb0 + 256]
            ocol = slice(c * 128, (c + 1) * 128)

            def tb(name, shape, dt):
                return sbt("%s_c%d" % (name, c), shape, dt), Buf()

            def tb2(name, shape, dt):
                return [tb(name + "a", shape, dt), tb(name + "b", shape, dt)]
            Hf, b_H = tb("Hf", [128, 128], F32)
            Hb, _ = tb("Hb", [128, 128], BF16)
            GM, b_GM = tb("GM", [128, 256], F32)
            E2 = tb2("ERB", [128, 256], F32)
            BR2 = tb2("BR", [128, 2, 128], BF16)
            KA2 = tb2("KA", [128, 2, 128], BF16)
            ar, bar = tb("ARG", [128, 512], F32)
            DD2 = tb2("DD", [128, 512], BF16)
            ut, but = tb("UT", [128, 128], F32)
            pp, bpp = tb("PP", [128, 2, 256], F32)
            xx, bxx = tb("XX", [128, 2, 128], F32)
            u32, bu32 = tb("U32", [128, 128], F32)
            TT2 = tb2("TTB", [128, 128], BF16)
            x1, bx1 = tb("X1", [128, 128], BF16)
            ne, bne = tb("NE", [128, 128], BF16)
            fos, bfos = tb("FOS", [128, 128], F32)
            fon, bfon = tb("FON", [128, 128], BF16)
            fyt, bfyt = tb("FYT", [128, 128], BF16)
            fsq, bfsq = tb("FSQ", [128, 4], F32)
            order = list(range(18)) if d == 0 else order1
            if C.stop and C.stop.startswith('T'):
                order = order[:int(C.stop[1:])]
            prog = {"prep": 0, "state": 0}

            def prep():
                for i, tl in enumerate(order):
                    while i - prog["state"] >= 2:
                        yield
                    k = i % 2
                    (e_, be_), (br, bbr), (ka, bka), (dd, bdd), (ttb, bttb) = E2[k], BR2[k], KA2[k], DD2[k], TT2[k]
                    tt = slice(tl * 128, (tl + 1) * 128)

                    def fg(pe):
                        pe.matmul(ps[pGm][:, 256:384], KTt[:, tt], KTt[:, tt], start=True, stop=True)
                        return pe.matmul(ps[pGm][:, 384:512], KTt[:, tt], QT[:, tt], start=True, stop=True)
                    S.op("pe", fg, [b_KT, b_QT], [psb[pGm]])
                    S.op("dve", lambda e: e.tensor_tensor(out=GM[:], in0=ps[pGm][:, 256:512], in1=MASK[:, d, :], op=ALU.mult), [psb[pGm], b_par], [b_GM])

                    def frb(pe):
                        pe.matmul(RB[:, 0:128], C.ident[:, rx:rx + 1].to_broadcast([128, 128]), GCALL[:, tt], start=True, stop=True)
                        return pe.matmul(RB[:, 128:256], C.ident[:, rc:rc + 1].to_broadcast([128, 128]), GCALL[:, tt], start=True, stop=True)
                    S.op("pe", frb, [b_GC, C.ident_b], [psb[pR]])
                    yield
                    S.op("act", lambda e: e.activation(out=e_[:], in_=RB, func=AF.Exp), [psb[pR]], [be_])
                    in0 = RB.rearrange("p (a t) -> p a t", a=2).unsqueeze(2).to_broadcast([128, 2, 2, 128])
                    in1 = TOKP[:, tl, 0:4, idx:idx + 1].rearrange("p (a b) o -> p a b o", a=2).to_broadcast([128, 2, 2, 128])
                    ar4 = ar[:].rearrange("p (a b t) -> p a b t", a=2, b=2)
                    S.op("dve", lambda e: e.tensor_tensor(out=ar4, in0=in0, in1=in1, op=ALU.subtract), [psb[pR], b_TOKP], [bar])
                    yield
                    S.op("pool", lambda e: e.tensor_scalar(out=ar[:], in0=ar[:], scalar1=0.0, scalar2=None, op0=ALU.min), [bar], [bar])
                    S.op("pool", lambda e: e.tensor_tensor(out=br[:, 0, :], in0=KTt[:, tt], in1=e_[:, 0:128], op=ALU.mult), [b_KT, be_], [bbr])
                    S.op("pool", lambda e: e.tensor_tensor(out=br[:, 1, :], in0=QT[:, tt], in1=e_[:, 128:256], op=ALU.mult), [b_QT, be_], [bbr])
                    yield
                    S.op("act", lambda e: e.activation(out=ar[:], in_=ar[:], func=AF.Exp), [bar], [bar])
                    S.op("pool", lambda e: e.tensor_scalar(out=ka[:, 0, :], in0=Ktok[:, tl, :], scalar1=TOKP[:, tl, 4, idx:idx + 1], scalar2=None, op0=ALU.mult), [b_Ktok, b_TOKP], [bka])
                    S.op("pool", lambda e: e.tensor_scalar(out=ka[:, 1, :], in0=Ktok[:, tl, :], scalar1=TOKP[:, tl, 5, idx:idx + 1], scalar2=None, op0=ALU.mult), [b_Ktok, b_TOKP], [bka])
                    yield
                    gm4 = GM[:].rearrange("p (a t) -> p a t", a=2).unsqueeze(2).to_broadcast([128, 2, 2, 128])
                    dd4 = dd[:].rearrange("p (a b t) -> p a b t", a=2, b=2)
                    S.op("dve", lambda e: e.tensor_tensor(out=u32[:], in0=ar[:, 0:128], in1=GM[:, 0:128], op=ALU.mult), [bar, b_GM], [bu32])
                    S.op("dve", lambda e: e.tensor_tensor(out=dd4, in0=ar4, in1=gm4, op=ALU.mult), [bar, b_GM], [bdd])
                    yield
                    yield from solve_fp32(C, u32[:], bu32, pS, ut, but, pp, bpp, xx, bxx, ttb, bttb)
                    prog["prep"] = i + 1

            def state():
                S.op("pool", lambda e: e.memset(Hf[:], 0.0), (), [b_H])
                S.op("pool", lambda e: e.memset(Hb[:], 0.0), (), [b_H])
                yield
                for i, tl in enumerate(order):
                    while prog["prep"] <= i:
                        yield
                    k = i % 2
                    (e_, be_), (br, bbr), (ka, bka), (dd, bdd), (ttb, bttb) = E2[k], BR2[k], KA2[k], DD2[k], TT2[k]
                    tt = slice(tl * 128, (tl + 1) * 128)
                    TT = ttb[:]
                    for cb in ((0, 1) if d == 0 else (1, 0)):
                        pp_ = slice(cb * 64, cb * 64 + 64)
                        cc = pp_
                        pccol = 128 + (cb * 64 + 63 if d == 0 else cb * 64)

                        def fx1(pe):
                            pe.matmul(ps[pX][pp_, 0:128], br[:, 0, cc], Hb[:], start=True, stop=False)
                            return pe.matmul(ps[pX][pp_, 0:128], dd[pp_, 128 + cc.start:128 + cc.start + 64], Vtok[pp_, tl, h, :], start=False, stop=True)
                        S.op("pe", fx1, [bbr, bdd, b_H, b_Vtok], [psb[pX]])
                        S.op("act", lambda e: e.activation(out=x1[pp_, :], in_=ps[pX][pp_, 0:128], func=AF.Copy), [psb[pX]], [bx1])
                        yield
                        S.op("pe", lambda e: e.matmul(ps[pX][pp_, 0:128], TT[pp_, cc], x1[pp_, :], start=True, stop=True), [bttb, bx1], [psb[pX]])
                        S.op("dve", lambda e: e.tensor_scalar(out=ne[pp_, :], in0=ps[pX][pp_, 0:128], scalar1=-1.0, scalar2=None, op0=ALU.mult), [psb[pX]], [bne])
                        yield

                        def fo(pe):
                            pe.matmul(ps[pO][pp_, ocol], br[:, 1, cc], Hb[:], start=True, stop=False)
                            pe.matmul(ps[pO][pp_, ocol], dd[pp_, 256 + cc.start:256 + cc.start + 64], Vtok[pp_, tl, h, :], start=False, stop=False)
                            return pe.matmul(ps[pO][pp_, ocol], dd[pp_, 384 + cc.start:384 + cc.start + 64], ne[pp_, :], start=False, stop=True)
                        S.op("pe", fo, [bbr, bdd, bne, b_H, b_Vtok], [psb[pO]])

                        def fh(pe):
                            pe.matmul(ps[pX][:, 128:256], ka[pp_, 0, :], Vtok[pp_, tl, h, :], start=True, stop=False)
                            return pe.matmul(ps[pX][:, 128:256], ka[pp_, 1, :], ne[pp_, :], start=False, stop=True)
                        S.op("pe", fh, [bka, bne, b_Vtok], [psb[pX]])
                        S.op("dve", lambda e: e.scalar_tensor_tensor(
                            out=Hf[:], in0=Hf[:], scalar=e_[:, pccol:pccol + 1], in1=ps[pX][:, 128:256], op0=ALU.mult, op1=ALU.add),
                            [psb[pX], be_, b_H], [b_H])
                        yield
                        S.op("act", lambda e: e.activation(out=Hb[:], in_=Hf[:], func=AF.Copy), [b_H], [b_H])
                        yield
                    first = (tl < pos1[tl]) if d == 0 else (pos1[tl] < tl)
                    if C.stop and C.stop.startswith('T'):
                        first = (d == 0)
                    if first:
                        S.op("act", lambda e: e.activation(out=OACC[:, tl, h, :], in_=ps[pO][:, ocol], func=AF.Copy), [psb[pO]], [b_OACC[tl][h]])
                        yield
                    else:
                        os_, bos, on, bon, yt, byt, sq, bsq = fos, bfos, fon, bfon, fyt, bfyt, fsq, bfsq
                        S.op("dve", lambda e: e.tensor_tensor(out=os_[:], in0=ps[pO][:, ocol], in1=OACC[:, tl, h, :], op=ALU.add), [psb[pO], b_OACC[tl][h]], [bos])
                        yield
                        tf, tbf = next_tmpf(C)
                        S.op("act", lambda e: e.activation(out=tf[:, 0:128], in_=os_[:], func=AF.Square, accum_out=sq[:, 0:1]), [bos], [tbf, bsq])
                        S.op("act", lambda e: e.activation(out=sq[:, 1:2], in_=sq[:, 0:1], func=AF.Sqrt, scale=1.0 / 128, bias=EPS), [bsq], [bsq])
                        yield
                        S.op("dve", lambda e: e.reciprocal(out=sq[:, 2:3], in_=sq[:, 1:2]), [bsq], [bsq])
                        S.op("dve", lambda e: e.tensor_scalar(out=on[:], in0=os_[:], scalar1=sq[:, 2:3], scalar2=None, op0=ALU.mult), [bos, bsq], [bon])
                        yield
                        S.op("pe", lambda e: e.transpose(C.psbf[pS][:, 896:1024], on[:], C.identb[:]), [bon, C.ident_b], [psb[pS]])
                        S.op("dve", lambda e: e.scalar_tensor_tensor(
                            out=yt[:], in0=C.psbf[pS][:, 896:1024], scalar=GNW[:, 0:1], in1=SZ[:, h, tt], op0=ALU.mult, op1=ALU.mult),
                            [psb[pS], b_par, b_SZ], [byt])
                        yield
                        S.dma("sp", C.Yscr[hv, :, tt], yt[:], [byt], [C.Yscr_b])
                        yield
                    prog["state"] = i + 1
            return [prep(), state()]
        gens = []
        for d in range(2):
            for h in range(C.nh):
                gens += chain(h, d)
        run_lockstep(gens)


def solve_fp32(C, U32, bU, pS, UT, bUT, PP, bPP, XX, bXX, TTb, bTT):
    S = C.S
    ps, psb = C.ps, C.ps_b
    S.op("pe", lambda e: e.transpose(ps[pS][:, 384:512], U32, C.ident[:]), [bU, C.ident_b], [psb[pS]])
    S.op("pool", lambda e: e.tensor_tensor(out=XX[:, 0, :], in0=C.ident[:], in1=U32, op=ALU.subtract), [bU, C.ident_b], [bXX])
    yield
    S.op("act", lambda e: e.activation(out=UT[:], in_=ps[pS][:, 384:512], func=AF.Copy), [psb[pS]], [bUT])
    yield
    Pk, PTk = U32, UT[:]
    pdeps = [bU, bUT]
    for k in range(1, 6):
        cur = k % 2
        last = (k == 5)

        def fp(pe, Pk=Pk, PTk=PTk, last=last):
            if not last:
                pe.matmul(ps[pS][:, 0:128], PTk, Pk, start=True, stop=True)
            return pe.matmul(ps[pS][:, 128:256], Pk, PTk, start=True, stop=True)
        S.op("pe", fp, pdeps, [psb[pS]])
        yield
        S.op("act", lambda e, cur=cur: e.activation(out=PP[:, cur, :], in_=ps[pS][:, 0:256], func=AF.Copy), [psb[pS]], [bPP])
        yield
        Pk, PTk = PP[:, cur, 0:128], PP[:, cur, 128:256]
        pdeps = [bPP]
        xprev = XX[:, (k - 1) % 2, :]
        S.op("pe", lambda e, PTk=PTk, xprev=xprev: e.matmul(ps[pS][:, 256:384], PTk, xprev, start=True, stop=True), [bPP, bXX], [psb[pS]])
        yield
        if last:
            S.op("dve", lambda e, xprev=xprev: e.tensor_tensor(out=TTb[:], in0=ps[pS][:, 256:384], in1=xprev, op=ALU.add), [psb[pS], bXX], [bTT])
        else:
            xcur = XX[:, k % 2, :]
            S.op("dve", lambda e, xcur=xcur, xprev=xprev: e.tensor_tensor(out=xcur, in0=ps[pS][:, 256:384], in1=xprev, op=ALU.add), [psb[pS], bXX], [bXX])
        yield

def gdn_out_proj(C, l, j, s):
    nc, S = C.nc, C.S
    from contextlib import ExitStack
    ps, psb = C.ps, C.ps_b
    S.barrier()
    with ExitStack() as st:
        def sbt(name, shape, dt):
            return st.enter_context(nc.sbuf_tensor(un(name), shape, dt))
        WO = sbt("WO", [128, KT, 16, 128], BF16)
        YB = [sbt("YB%d" % i, [128, 16, 512], BF16) for i in range(2)]
        b_WO = Buf()
        b_YB = [Buf(), Buf()]
        for o in range(KT):
            S.dma("pool", WO[:, o], C.g_wout[j, o], (), [b_WO])
        it = 0
        for c, (t0, n, is_ctx) in enumerate(CHUNKS):
            col = NSEQ if is_ctx else s
            yb, byb = YB[c % 2], b_YB[c % 2]
            S.dma("sp", yb[:, :, 0:n], C.Yscr[:, :, t0:t0 + n].rearrange("f p t -> p f t"), [C.Yscr_b], [byb])
            for o in range(KT):
                po = it % 2
                it += 1
                mm(S, ps[po][:, :n], [(WO[:, o, f, :], yb[:, f, 0:n]) for f in range(16)], [b_WO, byb], [psb[po]])
                S.op("dve", lambda e, po=po, n=n, o=o, t0=t0, col=col: e.scalar_tensor_tensor(
                    out=C.R[:, o, t0:t0 + n], in0=ps[po][:, :n], scalar=C.MODS[:, l, 2 * 8 + o, col:col + 1],
                    in1=C.R[:, o, t0:t0 + n], op0=ALU.mult, op1=ALU.add),
                    [psb[po], C.MODS_b, C.Rb[o][c]], [C.Rb[o][c]])
        S.barrier()


NCH = T // 8
NCC = T_CTX // 8
I32 = mybir.dt.int32


def s5_declare(C):
    nc = C.nc

    def din(name, shape, dt=F32):
        return nc.dram_tensor(name, list(shape), dt, kind="ExternalInput").ap()
    C.s5_lam = din("s5_lam", [64, 2, 128])
    C.s5_dt = din("s5_dt", [64, 128])
    C.s5_B = din("s5_B", [64, 2, 128, 16])
    C.s5_C = din("s5_C", [64, 2, 128, 16])
    C.s5_dskip = din("s5_dskip", [128, KT])
    C.s5_wglu = din("s5_wglu_t", [16, 128, KT, 128])
    C.s5_bglu = din("s5_bglu_t", [128, 16])
    C.c_sel4 = din("c_sel4", [128, 4, 8, 128])
    C.c_maskz = din("c_maskz", [2, 128, 128])
    C.s5_Tz = nc.dram_tensor("s5_Tz", [128, 128, 128], BF16, kind="Internal").ap()
    C.s5_W = nc.dram_tensor("s5_W", [128, 128, 2, 64], BF16, kind="Internal").ap()
    C.s5_Vi = nc.dram_tensor("s5_Vi", [128, 64, 2, 128], BF16, kind="Internal").ap()
    C.A8 = nc.alloc_sbuf_tensor("A8", [64, 2, 128], F32)
    C.A8_b = Buf("A8")
    C.MOscr = nc.dram_tensor("MOscr", [KT, 128, T], F32, kind="Internal").ap()


def s5_setup(C):
    nc, S = C.nc, C.S
    from contextlib import ExitStack
    ps, psb = C.ps, C.ps_b
    S.barrier()
    with ExitStack() as st:
        def sbt(name, shape, dt=F32):
            return st.enter_context(nc.sbuf_tensor(un(name), shape, dt))
        LAM = sbt("LAM", [64, 2, 128]); DT = sbt("DT", [64, 128])
        Bt = sbt("Bt", [64, 2, 128, 16]); Ct = sbt("Ct", [64, 2, 128, 16])
        MZ = sbt("MZ", [128, 2, 128])
        AA = sbt("AA", [64, 2, 128]); AI = sbt("AI", [64, 2, 128]); FF = sbt("FF", [64, 2, 128])
        ZT = sbt("ZT", [64, 9, 2, 128]); QT_ = sbt("QT", [64, 15, 2, 128])
        W1 = [sbt("s5w%d" % i, [64, 128]) for i in range(6)]
        KI = sbt("KI", [64, 128], I32)
        b = Buf("s5setup")
        S.dma("sp", LAM[:], C.s5_lam, (), [b]); S.dma("sp", DT[:], C.s5_dt, (), [b])
        S.dma("sp", Bt[:], C.s5_B, (), [b]); S.dma("sp", Ct[:], C.s5_C, (), [b])
        S.dma("sp", MZ[:, 0, :], C.c_maskz[0], (), [b]); S.dma("sp", MZ[:, 1, :], C.c_maskz[1], (), [b])

        def V(fn):
            S.op("dve", fn, [b], [b])

        def A(fn):
            S.op("act", fn, [b], [b])
        lr, li = LAM[:, 0, :], LAM[:, 1, :]
        t0_, t1_, t2_, t3_, t4_, t5_ = [w[:] for w in W1]
        A(lambda e: e.activation(out=DT[:], in_=DT[:], func=AF.Exp))
        V(lambda e: e.tensor_tensor(out=t0_, in0=DT[:], in1=lr, op=ALU.mult))
        A(lambda e: e.activation(out=t0_, in_=t0_, func=AF.Exp))
        V(lambda e: e.tensor_tensor(out=t1_, in0=DT[:], in1=li, op=ALU.mult))

        def sincos(dst, shift):
            V(lambda e: e.tensor_scalar(out=t2_, in0=t1_, scalar1=1.0 / (2 * np.pi), scalar2=64.0 + shift, op0=ALU.mult, op1=ALU.add))
            V(lambda e: e.tensor_copy(out=KI[:], in_=t2_))
            V(lambda e: e.tensor_copy(out=t3_, in_=KI[:]))
            V(lambda e: e.tensor_tensor(out=t2_, in0=t2_, in1=t3_, op=ALU.subtract))
            A(lambda e: e.activation(out=dst, in_=t2_, func=AF.Sin, scale=2 * np.pi))
        sincos(t4_, 0.0)
        sincos(t5_, 0.25)
        V(lambda e: e.tensor_tensor(out=AA[:, 0, :], in0=t0_, in1=t5_, op=ALU.mult))
        V(lambda e: e.tensor_tensor(out=AA[:, 1, :], in0=t0_, in1=t4_, op=ALU.mult))
        V(lambda e: e.tensor_tensor(out=t1_, in0=t0_, in1=t0_, op=ALU.mult))
        V(lambda e: e.reciprocal(out=t1_, in_=t1_))
        V(lambda e: e.tensor_tensor(out=AI[:, 0, :], in0=AA[:, 0, :], in1=t1_, op=ALU.mult))
        V(lambda e: e.scalar_tensor_tensor(out=AI[:, 1, :], in0=AA[:, 1, :], scalar=-1.0, in1=t1_, op0=ALU.mult, op1=ALU.mult))
        V(lambda e: e.tensor_tensor(out=t1_, in0=lr, in1=lr, op=ALU.mult))
        V(lambda e: e.tensor_tensor(out=t2_, in0=li, in1=li, op=ALU.mult))
        V(lambda e: e.tensor_tensor(out=t1_, in0=t1_, in1=t2_, op=ALU.add))
        V(lambda e: e.reciprocal(out=t1_, in_=t1_))
        V(lambda e: e.tensor_scalar(out=t2_, in0=AA[:, 0, :], scalar1=-1.0, scalar2=None, op0=ALU.add))
        V(lambda e: e.tensor_tensor(out=t3_, in0=t2_, in1=lr, op=ALU.mult))
        V(lambda e: e.tensor_tensor(out=t4_, in0=AA[:, 1, :], in1=li, op=ALU.mult))
        V(lambda e: e.tensor_tensor(out=t3_, in0=t3_, in1=t4_, op=ALU.add))
        V(lambda e: e.tensor_tensor(out=FF[:, 0, :], in0=t3_, in1=t1_, op=ALU.mult))
        V(lambda e: e.tensor_tensor(out=t3_, in0=AA[:, 1, :], in1=lr, op=ALU.mult))
        V(lambda e: e.tensor_tensor(out=t4_, in0=t2_, in1=li, op=ALU.mult))
        V(lambda e: e.tensor_tensor(out=t3_, in0=t3_, in1=t4_, op=ALU.subtract))
        V(lambda e: e.tensor_tensor(out=FF[:, 1, :], in0=t3_, in1=t1_, op=ALU.mult))

        def cmul(dst, x, y):
            V(lambda e: e.tensor_tensor(out=t0_, in0=x[:, 0, :], in1=y[:, 0, :], op=ALU.mult))
            V(lambda e: e.tensor_tensor(out=t1_, in0=x[:, 1, :], in1=y[:, 1, :], op=ALU.mult))
            V(lambda e: e.tensor_tensor(out=dst[:, 0, :], in0=t0_, in1=t1_, op=ALU.subtract))
            V(lambda e: e.tensor_tensor(out=t0_, in0=x[:, 0, :], in1=y[:, 1, :], op=ALU.mult))
            V(lambda e: e.tensor_tensor(out=t1_, in0=x[:, 1, :], in1=y[:, 0, :], op=ALU.mult))
            V(lambda e: e.tensor_tensor(out=dst[:, 1, :], in0=t0_, in1=t1_, op=ALU.add))
        V(lambda e: e.memset(ZT[:, 0, 0, :], 1.0))
        V(lambda e: e.memset(ZT[:, 0, 1, :], 0.0))
        for e_ in range(1, 9):
            cmul(ZT[:, e_], ZT[:, e_ - 1], AA[:])
        V(lambda e: e.tensor_copy(out=QT_[:, 7], in_=FF[:]))
        for e_ in range(1, 8):
            cmul(QT_[:, 7 + e_], QT_[:, 7 + e_ - 1], AA[:])
            cmul(QT_[:, 7 - e_], QT_[:, 7 - e_ + 1], AI[:])
        S.op("dve", lambda e: e.tensor_copy(out=C.A8[:], in_=ZT[:, 8]), [b], [C.A8_b])

        VP = sbt("VP", [64, 16, 2, 8, 16]); VI = sbt("VI", [64, 16, 2, 8, 16])
        WA = sbt("WA", [64, 16, 2, 8, 16]); WB = sbt("WB", [64, 16, 2, 8, 16])
        TM = sbt("TMs5", [64, 16, 16])
        TZs = sbt("TZs", [128, 16, 128], BF16); WSs = sbt("WSs", [128, 16, 2, 64], BF16); VIs = sbt("VIs", [64, 16, 2, 128], BF16)
        b_st = Buf("s5stage")
        for d in range(2):
            ea = (lambda j: 7 - j) if d == 0 else (lambda j: j)
            eb = (lambda j: -j) if d == 0 else (lambda j: j - 7)
            ev = (lambda t: t) if d == 0 else (lambda t: 7 - t)
            ei = (lambda t: t + 1) if d == 0 else (lambda t: 8 - t)
            for gb in range(4):
                dg = slice(d * 64 + gb * 16, d * 64 + gb * 16 + 16)
                Cr, Ci = Ct[:, 0, dg, :], Ct[:, 1, dg, :]
                Br, Bi = Bt[:, 0, dg, :], Bt[:, 1, dg, :]

                def bc(ap_):
                    return ap_.unsqueeze(2).to_broadcast([64, 16, 16])
                for t in range(8):
                    for (tab, ee) in ((VP, ev(t)), (VI, ei(t))):
                        zr, zi = bc(ZT[:, ee, 0, dg]), bc(ZT[:, ee, 1, dg])
                        V(lambda e, tab=tab, t=t, zr=zr: e.tensor_tensor(out=tab[:, :, 0, t, :], in0=Cr, in1=zr, op=ALU.mult))
                        V(lambda e, zi=zi: e.tensor_tensor(out=TM[:], in0=Ci, in1=zi, op=ALU.mult))
                        V(lambda e, tab=tab, t=t: e.tensor_tensor(out=tab[:, :, 0, t, :], in0=tab[:, :, 0, t, :], in1=TM[:], op=ALU.subtract))
                        V(lambda e, tab=tab, t=t, zi=zi: e.tensor_tensor(out=tab[:, :, 1, t, :], in0=Cr, in1=zi, op=ALU.mult))
                        V(lambda e, zr=zr: e.tensor_tensor(out=TM[:], in0=Ci, in1=zr, op=ALU.mult))
                        V(lambda e, tab=tab, t=t: e.scalar_tensor_tensor(out=tab[:, :, 1, t, :], in0=tab[:, :, 1, t, :], scalar=-1.0, in1=TM[:],
                                                                       op0=ALU.mult, op1=ALU.subtract))
                    for (tab, ee) in ((WA, ea(t)), (WB, eb(t))):
                        qr, qi = bc(QT_[:, 7 + ee, 0, dg]), bc(QT_[:, 7 + ee, 1, dg])
                        V(lambda e, tab=tab, t=t, qr=qr: e.tensor_tensor(out=tab[:, :, 0, t, :], in0=Br, in1=qr, op=ALU.mult))
                        V(lambda e, qi=qi: e.tensor_tensor(out=TM[:], in0=Bi, in1=qi, op=ALU.mult))
                        V(lambda e, tab=tab, t=t: e.tensor_tensor(out=tab[:, :, 0, t, :], in0=tab[:, :, 0, t, :], in1=TM[:], op=ALU.subtract))
                        V(lambda e, tab=tab, t=t, qr=qr: e.tensor_tensor(out=tab[:, :, 1, t, :], in0=Bi, in1=qr, op=ALU.mult))
                        V(lambda e, qi=qi: e.tensor_tensor(out=TM[:], in0=Br, in1=qi, op=ALU.mult))
                        V(lambda e, tab=tab, t=t: e.tensor_tensor(out=tab[:, :, 1, t, :], in0=tab[:, :, 1, t, :], in1=TM[:], op=ALU.add))
                for gl in range(16):
                    pb = gl % 2

                    def flat(tab, ri, gl=gl):
                        return tab[:, gl, ri].rearrange("p t c -> p (t c)")
                    mm(S, ps[pb][:, 0:128], [(flat(WB, 0), flat(VP, 0)), (flat(WB, 1), flat(VP, 1))], [b], [psb[pb]])
                    S.op("dve", lambda e, pb=pb, gl=gl: e.tensor_tensor(out=TZs[:, gl, :], in0=ps[pb][:, 0:128], in1=MZ[:, d, :], op=ALU.mult),
                         [psb[pb], b], [b_st])

                    def ftr(pe, pb=pb, flat=flat):
                        pe.transpose(ps[2 + pb][:, 0:64], flat(WA, 0), C.ident[0:64, 0:64])
                        return pe.transpose(ps[2 + pb][:, 64:128], flat(WA, 1), C.ident[0:64, 0:64])
                    S.op("pe", ftr, [b, C.ident_b], [psb[2 + pb]])
                    S.op("act", lambda e, pb=pb, gl=gl: e.activation(out=WSs[:, gl].rearrange("p r q -> p (r q)"), in_=ps[2 + pb][:, 0:128], func=AF.Copy),
                         [psb[2 + pb]], [b_st])
                S.op("act", lambda e: e.activation(out=VIs[:].rearrange("p g r q -> p (g r q)"), in_=VI[:].rearrange("p g r t c -> p (g r t c)"), func=AF.Copy),
                     [b], [b_st])
                S.dma("sp", C.s5_Tz[dg].rearrange("g p q -> p g q"), TZs[:], [b_st], ())
                S.dma("sp", C.s5_W[dg].rearrange("g p r q -> p g r q"), WSs[:], [b_st], ())
                S.dma("sp", C.s5_Vi[dg].rearrange("g p r q -> p g r q"), VIs[:], [b_st], ())
        S.barrier()


def s5_mixer(C, l, s):
    nc, S = C.nc, C.S
    from contextlib import ExitStack
    ps, psb = C.ps, C.ps_b
    GB = 8
    S.barrier()
    with ExitStack() as st:
        def sbt(name, shape, dt=F32):
            return st.enter_context(nc.sbuf_tensor(un(name), shape, dt))
        ZG = sbt("ZG", [128, KT, T], BF16)
        b_ZG = Buf("ZG")
        DSK = sbt("DSK", [128, KT]); BGL = sbt("BGL", [128, 16])
        b_c = Buf("s5c")
        S.dma("sp", DSK[:], C.s5_dskip, (), [b_c])
        S.dma("sp", BGL[:], C.s5_bglu, (), [b_c])
        with ExitStack() as st2:
            def sb2(name, shape, dt=F32):
                return st2.enter_context(nc.sbuf_tensor(un(name), shape, dt))
            SEL4 = sb2("SEL4", [128, 4, 8, 128], BF16)
            S.dma("pool", SEL4[:], C.c_sel4, (), [b_c])
            Ublk = sb2("Ublk", [128, GB, NCH], BF16); b_U = [Buf() for _ in range(GB)]
            Yblk = sb2("Yblk", [128, GB, NCH], BF16); b_Y = [Buf() for _ in range(GB)]
            SS = sb2("SS", [64, 2 * GB, 2, NCH + 1]); b_SS = Buf("SS")
            TZ = sb2("TZ", [128, 2 * GB, 128], BF16); WW = sb2("WW", [128, 2 * GB, 2, 64], BF16); VV = sb2("VV", [64, 2 * GB, 2, 128], BF16)
            b_P = Buf("s5par")
            T1 = sb2("T1", [64, 2 * GB, 2]); T2 = sb2("T2", [64, 2 * GB, 2]); b_T1 = Buf("s5T1"); b_T2 = Buf("s5T2")
            AB_ = sb2("ABlk", [64, 2, 2 * GB]); b_AB = Buf()
            SPB = sb2("SPB", [64, 2, 2, 2, NCH], BF16); b_SPB = [Buf(), Buf()]
            ZT_ = [sb2("zt%d" % i, [128, NCH]) for i in range(2)]; b_zt = [Buf(), Buf()]
            for blk in range(64 // GB):
                tile = blk
                for d in range(2):
                    dg = slice(d * 64 + blk * GB, d * 64 + blk * GB + GB)
                    S.dma("sp", TZ[:, d * GB:(d + 1) * GB, :], C.s5_Tz[dg].rearrange("g p q -> p g q"), (), [b_P])
                    S.dma("sp", WW[:, d * GB:(d + 1) * GB], C.s5_W[dg].rearrange("g p r q -> p g r q"), (), [b_P])
                    S.dma("sp", VV[:, d * GB:(d + 1) * GB], C.s5_Vi[dg].rearrange("g p r q -> p g r q"), (), [b_P])
                    S.op("dve", lambda e, d=d, dg=dg: e.tensor_copy(out=AB_[:, :, d * GB:(d + 1) * GB], in_=C.A8[:, :, dg]), [C.A8_b], [b_AB])
                hreads = [C.Hb[tile][c] for c in range(5)]
                for gl in range(GB):
                    b2, q = gl // 4, gl % 4
                    pb = gl % 2
                    rr = slice(64 * b2, 64 * b2 + 64)
                    mm(S, ps[pb][:, 0:NCH], [(SEL4[rr, q, j, :], C.H[rr, tile, j::8]) for j in range(8)], [b_c] + hreads, [psb[pb]])
                    S.op("act", lambda e, pb=pb, gl=gl: e.activation(out=Ublk[:, gl, :], in_=ps[pb][:, 0:NCH], func=AF.Copy), [psb[pb]], [b_U[gl]])
                S.op("pool", lambda e: e.memset(SS[:, :, :, 0:1], 0.0), (), [b_SS])
                for gl in range(GB):
                    for d in range(2):
                        ix = d * GB + gl
                        for ri in range(2):
                            pb = 2 + ((gl * 4 + d * 2 + ri) % 4)
                            S.op("pe", lambda e, pb=pb, ix=ix, ri=ri, gl=gl: e.matmul(ps[pb][0:64, 0:NCH], WW[:, ix, ri, :], Ublk[:, gl, :], start=True, stop=True),
                                 [b_P, b_U[gl]], [psb[pb]])
                            eng = "act" if ri == 0 else "dve"

                            def fcp(e, pb=pb, ix=ix, ri=ri, eng=eng, d=d):
                                if d == 0:
                                    segs = [(SS[:, ix, ri, 1:NCH + 1], ps[pb][0:64, 0:NCH])]
                                else:
                                    segs = [(SS[:, ix, ri, 1:NCC + 1], ps[pb][0:64, NCC - 1::-1]),
                                            (SS[:, ix, ri, NCC + 1:NCH + 1], ps[pb][0:64, NCH - 1:NCC - 1:-1])]
                                ins = None
                                for (o_, i_) in segs:
                                    ins = e.activation(out=o_, in_=i_, func=AF.Copy) if eng == "act" else e.tensor_copy(out=o_, in_=i_)
                                return ins
                            S.op(eng, fcp, [psb[pb]], [b_SS])
                Ar = AB_[:, 0, :].unsqueeze(2).to_broadcast([64, 2 * GB, 2])
                Ai = AB_[:, 1, :].unsqueeze(2).to_broadcast([64, 2 * GB, 2])
                for k in range(1, NCH):
                    S.op("dve", lambda e, k=k: e.tensor_tensor(out=T1[:], in0=SS[:, :, :, k], in1=Ar, op=ALU.mult), [b_SS, b_AB], [b_T1])
                    S.op("pool", lambda e, k=k: e.tensor_tensor(out=T2[:], in0=SS[:, :, :, k], in1=Ai, op=ALU.mult), [b_SS, b_AB], [b_T2])
                    S.op("dve", lambda e, k=k: e.tensor_tensor(out=SS[:, :, :, k + 1], in0=SS[:, :, :, k + 1], in1=T1[:], op=ALU.add), [b_T1, b_SS], [b_SS])
                    S.op("dve", lambda e, k=k: e.tensor_tensor(out=SS[:, :, 0, k + 1], in0=SS[:, :, 0, k + 1], in1=T2[:, :, 1], op=ALU.subtract), [b_T2, b_SS], [b_SS])
                    S.op("dve", lambda e, k=k: e.tensor_tensor(out=SS[:, :, 1, k + 1], in0=SS[:, :, 1, k + 1], in1=T2[:, :, 0], op=ALU.add), [b_T2, b_SS], [b_SS])
                for gl in range(GB):
                    pb = gl % 2
                    rot = gl % 2
                    S.op("act", lambda e, gl=gl, rot=rot: e.activation(out=SPB[:, rot, 0], in_=SS[:, gl, :, 0:NCH], func=AF.Copy), [b_SS], [b_SPB[rot]])

                    def fsp(e, gl=gl, rot=rot):
                        e.tensor_copy(out=SPB[:, rot, 1, :, 0:NCC], in_=SS[:, GB + gl, :, NCC - 1::-1])
                        return e.tensor_copy(out=SPB[:, rot, 1, :, NCC:NCH], in_=SS[:, GB + gl, :, NCH - 1:NCC - 1:-1])
                    S.op("dve", fsp, [b_SS], [b_SPB[rot]])

                    def fy(pe, pb=pb, gl=gl, rot=rot):
                        ins = None
                        for (c0, c1) in ((0, NCC), (NCC, NCH)):
                            terms = [(TZ[:, gl, :], Ublk[:, gl, c0:c1]), (TZ[:, GB + gl, :], Ublk[:, gl, c0:c1])]
                            for ri in range(2):
                                terms.append((VV[:, gl, ri, :], SPB[:, rot, 0, ri, c0:c1]))
                                terms.append((VV[:, GB + gl, ri, :], SPB[:, rot, 1, ri, c0:c1]))
                            for i, (l_, r_) in enumerate(terms):
                                ins = pe.matmul(ps[pb][:, c0:c1], l_, r_, start=(i == 0), stop=(i == len(terms) - 1))
                        return ins
                    S.op("pe", fy, [b_P, b_U[gl], b_SPB[rot]], [psb[pb]])
                    S.op("act", lambda e, pb=pb, gl=gl: e.activation(out=Yblk[:, gl, :], in_=ps[pb][:, 0:NCH], func=AF.Copy), [psb[pb]], [b_Y[gl]])
                for t in range(8):
                    b2, q = t // 4, t % 4
                    rr = slice(64 * b2, 64 * b2 + 64)
                    pb = 4 + (t % 2)
                    mm(S, ps[pb][:, 0:NCH], [(SEL4[rr, q, g8, :], Yblk[rr, g8, :]) for g8 in range(8)], [b_c] + b_Y, [psb[pb]])
                    z, bz = ZT_[t % 2], b_zt[t % 2]
                    S.op("dve", lambda e, z=z, tile=tile, t=t, pb=pb: e.scalar_tensor_tensor(
                        out=z[:], in0=C.H[:, tile, t::8], scalar=DSK[:, tile:tile + 1], in1=ps[pb][:, 0:NCH], op0=ALU.mult, op1=ALU.add),
                        [psb[pb], b_c] + hreads, [bz])
                    gelu_tanh(C, ZG[:, tile, t::8], z, bz, [b_ZG])
            S.barrier()
        WG = sbt("WG", [128, 16, KT, 128], BF16); b_WG = Buf()
        SGT = [sbt("sgt%d" % i, [128, 512]) for i in range(3)]; b_sg = [Buf(), Buf(), Buf()]
        for f in range(16):
            S.dma("pool", WG[:, f], C.s5_wglu[f], (), [b_WG])
        it = 0
        for c, (t0, n, is_ctx) in enumerate(CHUNKS):
            for o in range(KT):
                pv, pg = (it % 3) * 2, (it % 3) * 2 + 1
                sg, bsg = SGT[it % 3], b_sg[it % 3]
                it += 1
                mm(S, ps[pv][:, :n], [(WG[:, o, k, :], ZG[:, k, t0:t0 + n]) for k in range(KT)], [b_WG, b_ZG], [psb[pv]])
                mm(S, ps[pg][:, :n], [(WG[:, 8 + o, k, :], ZG[:, k, t0:t0 + n]) for k in range(KT)], [b_WG, b_ZG], [psb[pg]])
                S.op("act", lambda e, pg=pg, sg=sg, n=n, o=o: e.activation(out=sg[:, :n], in_=ps[pg][:, :n], func=AF.Sigmoid, bias=BGL[:, 8 + o:9 + o]), [psb[pg], b_c], [bsg])
                S.op("dve", lambda e, pv=pv, sg=sg, n=n, o=o: e.scalar_tensor_tensor(
                    out=sg[:, :n], in0=ps[pv][:, :n], scalar=BGL[:, o:o + 1], in1=sg[:, :n], op0=ALU.add, op1=ALU.mult), [psb[pv], bsg, b_c], [bsg])
                S.dma("sp", C.MOscr[o, :, t0:t0 + n], sg[:, :n], [bsg], ())
        S.barrier()


def apply_mixer_out(C, l, s):
    nc, S = C.nc, C.S
    S.barrier()
    with (nc.sbuf_tensor(un("mo0"), [128, 512], F32) as mo0, nc.sbuf_tensor(un("mo1"), [128, 512], F32) as mo1,
          nc.sbuf_tensor(un("mo2"), [128, 512], F32) as mo2):
        mo, bmo = [mo0, mo1, mo2], [Buf(), Buf(), Buf()]
        it = 0
        for c, (t0, n, is_ctx) in enumerate(CHUNKS):
            col = NSEQ if is_ctx else s
            for o in range(KT):
                m_, bm = mo[it % 3], bmo[it % 3]
                it += 1
                S.dma("sp" if it % 2 == 0 else "act", m_[:, :n], C.MOscr[o, :, t0:t0 + n], (), [bm])
                S.op("dve", lambda e, m_=m_, n=n, o=o, t0=t0, col=col: e.scalar_tensor_tensor(
                    out=C.R[:, o, t0:t0 + n], in0=m_[:, :n], scalar=C.MODS[:, l, 2 * 8 + o, col:col + 1], in1=C.R[:, o, t0:t0 + n],
                    op0=ALU.mult, op1=ALU.add), [bm, C.MODS_b, C.Rb[o][c]], [C.Rb[o][c]])
        S.barrier()


def gelu_tanh(C, out_ap, z, bz, wbufs):
    S = C.S
    tf, tb = next_tmpf(C)
    n = z.shape[1]
    S.op("dve", lambda e: e.tensor_tensor(out=tf[:, :n], in0=z[:], in1=z[:], op=ALU.mult), [bz], [tb])
    S.op("dve", lambda e: e.tensor_scalar(out=tf[:, :n], in0=tf[:, :n], scalar1=0.044715, scalar2=1.0, op0=ALU.mult, op1=ALU.add), [tb], [tb])
    S.op("dve", lambda e: e.tensor_tensor(out=tf[:, :n], in0=tf[:, :n], in1=z[:], op=ALU.mult), [tb, bz], [tb])
    S.op("act", lambda e: e.activation(out=tf[:, :n], in_=tf[:, :n], func=AF.Tanh, scale=0.7978845608028654), [tb], [tb])
    S.op("dve", lambda e: e.tensor_scalar(out=tf[:, :n], in0=tf[:, :n], scalar1=1.0, scalar2=0.5, op0=ALU.add, op1=ALU.mult), [tb], [tb])
    S.op("dve", lambda e: e.tensor_tensor(out=out_ap, in0=tf[:, :n], in1=z[:], op=ALU.mult), [tb, bz], wbufs)


RW_NV = 9


def rwkv_declare(C):
    nc = C.nc

    def din(name, shape, dt=F32):
        return nc.dram_tensor(name, list(shape), dt, kind="ExternalInput").ap()
    C.rw_mu = din("rw_mu", [128, 6, KT])
    C.rw_wrkv = din("rw_wrkv_t", [3, KT, 128, KT, 128])
    C.rw_l1 = din("rw_l1_t", [3, 128, KT, 128])
    C.rw_l2 = din("rw_l2", [3, 128, D])
    C.rw_vecs = din("rw_vecs", [128, RW_NV, KT])
    C.rw_wout = din("rw_wout_t", [KT, 128, KT, 128])
    C.c_mask4 = din("c_mask4", [2, 128, 512])


def rwkv_mixer(C, l, s):
    nc, S = C.nc, C.S
    from contextlib import ExitStack
    ps, psb = C.ps, C.ps_b
    S.barrier()
    with ExitStack() as st:
        def sbt(name, shape, dt=F32):
            return st.enter_context(nc.sbuf_tensor(un(name), shape, dt))
        SH = sbt("SH", [128, KT, T], BF16); b_SH = Buf("SH")
        MU = sbt("MU", [128, 2, 6, KT]); VEC = sbt("VEC", [128, RW_NV, KT]); b_par = Buf("rwpar")
        L2 = sbt("L2", [128, 3, D], BF16)
        LH = sbt("LH", [128, 3, T], BF16); b_LH = Buf("LH")
        BD = sbt("BD", [128, 128]); MASK4 = sbt("MASK4", [128, 2, 512], BF16)
        S.dma("sp", MU[:, 0], C.rw_mu, (), [b_par])
        S.dma("sp", VEC[:], C.rw_vecs, (), [b_par])
        S.dma("pool", MASK4[:, 0, :], C.c_mask4[0], (), [b_par])
        S.dma("pool", MASK4[:, 1, :], C.c_mask4[1], (), [b_par])
        S.dma("pool", L2[:], C.rw_l2.rearrange("a p n -> p a n"), (), [b_par])
        S.op("dve", lambda e: e.tensor_scalar(out=MU[:, 1], in0=MU[:, 0], scalar1=-1.0, scalar2=1.0, op0=ALU.mult, op1=ALU.add), [b_par], [b_par])
        S.op("pool", lambda e: e.memset(BD[:], 0.0), (), [b_par])
        S.op("pool", lambda e: e.memset(BD[0:64, 0:64], 1.0), [b_par], [b_par])
        S.op("pool", lambda e: e.memset(BD[64:128, 64:128], 1.0), [b_par], [b_par])
        S.op("pool", lambda e: e.memset(SH[:], 0.0), (), [b_SH])
        allH = [C.Hb[k][c] for k in range(KT) for c in range(5)]
        for k in range(KT):
            eng = ("dve", "pool", "act")[k % 3]

            def cp(e, o_, i_, eng=eng):
                return e.activation(out=o_, in_=i_, func=AF.Copy) if eng == "act" else e.tensor_copy(out=o_, in_=i_)
            xs = slice(T_CTX, T)
            if k < 4:
                S.op(eng, lambda e, k=k, cp=cp: cp(e, SH[:, k, 1:T_CTX], C.H[:, k, 0:T_CTX - 1]), allH, [b_SH])
            else:
                S.op(eng, lambda e, k=k, cp=cp: cp(e, SH[:, k, 0:T_CTX - 1], C.H[:, k, 1:T_CTX]), allH, [b_SH])
            hx = C.H[:, k, xs].rearrange("p (r c) -> p r c", c=64)
            sx = SH[:, k, xs].rearrange("p (r c) -> p r c", c=64)
            if k < 2:
                S.op(eng, lambda e, cp=cp, sx=sx, hx=hx: cp(e, sx[:, :, 1:64], hx[:, :, 0:63]), allH, [b_SH])
            elif k < 4:
                S.op(eng, lambda e, cp=cp, sx=sx, hx=hx: cp(e, sx[:, :, 0:63], hx[:, :, 1:64]), allH, [b_SH])
            elif k < 6:
                S.op(eng, lambda e, k=k, cp=cp: cp(e, SH[:, k, T_CTX + 64:T], C.H[:, k, T_CTX:T - 64]), allH, [b_SH])
            else:
                S.op(eng, lambda e, k=k, cp=cp: cp(e, SH[:, k, T_CTX:T - 64], C.H[:, k, T_CTX + 64:T]), allH, [b_SH])

        cnt = [0]
        wb = {}

        class wscope:
            def __enter__(self_):
                self_.st = ExitStack()
                wb["WF"] = [self_.st.enter_context(nc.sbuf_tensor(un("rwWF"), [128, KT, 128], F32)) for i in range(2)]
                wb["WP"] = [self_.st.enter_context(nc.sbuf_tensor(un("rwWP"), [128, 2, KT, 128], BF16)) for i in range(2)]
                wb["bWF"] = [Buf(), Buf()]
                wb["bWP"] = [Buf(), Buf()]
                return self_

            def __exit__(self_, *a):
                S.barrier()
                self_.st.close()
                return False

        def proj2(w_ap, mu_idx, out_fn):
            i = cnt[0] % 2
            cnt[0] += 1
            wf, bwf, wp, bwp = wb["WF"][i], wb["bWF"][i], wb["WP"][i], wb["bWP"][i]
            S.dma("sp" if i == 0 else "act", wf[:], w_ap, (), [bwf])
            S.op("dve", lambda e: e.tensor_tensor(out=wp[:, 0], in0=wf[:], in1=MU[:, 1, mu_idx, :].unsqueeze(2).to_broadcast([128, KT, 128]), op=ALU.mult), [bwf, b_par], [bwp])
            S.op("pool", lambda e: e.tensor_tensor(out=wp[:, 1], in0=wf[:], in1=MU[:, 0, mu_idx, :].unsqueeze(2).to_broadcast([128, KT, 128]), op=ALU.mult), [bwf, b_par], [bwp])
            for c, (t0, n, _) in enumerate(CHUNKS):
                pb = c % 2
                pairs = [(wp[:, 0, k, :], C.H[:, k, t0:t0 + n]) for k in range(KT)] + [(wp[:, 1, k, :], SH[:, k, t0:t0 + n]) for k in range(KT)]
                mm(S, ps[pb][:, :n], pairs, [bwp, b_SH] + [C.Hb[k][c] for k in range(KT)], [psb[pb]])
                out_fn(c, t0, n, ps[pb][:, :n], psb[pb])
        ws_ = wscope()
        ws_.__enter__()
        proj2(C.rw_l1[0], 1, lambda c, t0, n, p_, pb_: S.op("act", lambda e: e.activation(out=LH[:, 0, t0:t0 + n], in_=p_, func=AF.Tanh), [pb_], [b_LH]))
        proj2(C.rw_l1[1], 4, lambda c, t0, n, p_, pb_: S.op("act", lambda e: e.activation(out=LH[:, 1, t0:t0 + n], in_=p_, func=AF.Copy), [pb_], [b_LH]))
        proj2(C.rw_l1[2], 5, lambda c, t0, n, p_, pb_: S.op("act", lambda e: e.activation(out=LH[:, 2, t0:t0 + n], in_=p_, func=AF.Sigmoid), [pb_], [b_LH]))

        ws_.__exit__(None, None, None)
        for o in range(KT if not C.stop else 1):
            rwkv_tile(C, l, s, o, SH, b_SH, MU, VEC, b_par, L2, LH, b_LH, BD, MASK4, proj2, wscope)
        S.barrier()


def rwkv_tile(C, l, s, o, SH, b_SH, MU, VEC, b_par, L2, LH, b_LH, BD, MASK4, proj2, wscope):
    nc, S = C.nc, C.S
    from contextlib import ExitStack
    ps, psb = C.ps, C.ps_b
    ocol = slice(o * 128, (o + 1) * 128)
    with ExitStack() as st:
        def sbt(name, shape, dt=F32):
            return st.enter_context(nc.sbuf_tensor(un(name), shape, dt))
        Rr = sbt("Rr", [128, T], BF16); Kk = sbt("Kk", [128, T], BF16); Vv = sbt("Vv", [128, T], BF16)
        KK = sbt("KK", [128, T], BF16); Gg = sbt("Gg", [128, T], BF16); BON = sbt("BON", [128, T], BF16)
        Vtok = sbt("rVtok", [128, 18, 128], BF16)
        OACC = sbt("rOACC", [128, 18, 128]); b_OACC = [Buf() for _ in range(18)]
        b_R, b_K, b_V, b_KK, b_G, b_BON, b_Vtok = [Buf() for _ in range(7)]
        ws_ = wscope()
        ws_.__enter__()
        proj2(C.rw_wrkv[0, o], 0, lambda c, t0, n, p_, pb_: S.op("act", lambda e: e.activation(out=Rr[:, t0:t0 + n], in_=p_, func=AF.Copy), [pb_], [b_R]))
        proj2(C.rw_wrkv[1, o], 2, lambda c, t0, n, p_, pb_: S.op("act", lambda e: e.activation(out=Kk[:, t0:t0 + n], in_=p_, func=AF.Copy), [pb_], [b_K]))
        proj2(C.rw_wrkv[2, o], 3, lambda c, t0, n, p_, pb_: S.op("act", lambda e: e.activation(out=Vv[:, t0:t0 + n], in_=p_, func=AF.Copy), [pb_], [b_V]))
        ws_.__exit__(None, None, None)
        S.op("pool", lambda e: e.memset(BON[:], 0.0), (), [b_BON])
        for c, (t0, n, _) in enumerate(CHUNKS):
            pb = 2 + c % 2
            S.op("pe", lambda e, pb=pb, t0=t0, n=n: e.matmul(ps[pb][:, :n], L2[:, 2, ocol], LH[:, 2, t0:t0 + n], start=True, stop=True), [b_par, b_LH], [psb[pb]])
            S.op("act", lambda e, pb=pb, t0=t0, n=n: e.activation(out=Gg[:, t0:t0 + n], in_=ps[pb][:, :n], func=AF.Copy), [psb[pb]], [b_G])
            tf, tb = next_tmpf(C)
            tg, tgb = next_tmpf(C)
            S.op("dve", lambda e, tf=tf, t0=t0, n=n: e.tensor_scalar(out=tf[:, :n], in0=Kk[:, t0:t0 + n], scalar1=VEC[:, 4, o:o + 1], scalar2=None, op0=ALU.mult), [b_K, b_par], [tb])
            S.op("act", lambda e, tf=tf, tg=tg, n=n: e.activation(out=tg[:, :n], in_=tf[:, :n], func=AF.Square), [tb], [tgb])
            S.op("pe", lambda e, tg=tg, n=n: e.matmul(ps[7][:, :n], BD[:], tg[:, :n], start=True, stop=True), [tgb, b_par], [psb[7]])
            S.op("act", lambda e, n=n: e.activation(out=C.rs[:, :n], in_=ps[7][:, :n], func=AF.Sqrt, bias=1e-6), [psb[7]], [C.rs_b])
            S.op("dve", lambda e, n=n: e.reciprocal(out=C.rs[:, :n], in_=C.rs[:, :n]), [C.rs_b], [C.rs_b])
            S.op("dve", lambda e, tf=tf, t0=t0, n=n: e.tensor_tensor(out=KK[:, t0:t0 + n], in0=tf[:, :n], in1=C.rs[:, :n], op=ALU.mult), [tb, C.rs_b], [b_KK])
        for g6 in range(0, 18, 6):
            pb = 2 + ((g6 // 6) % 2)

            def ftr(pe, g6=g6, pb=pb):
                ins = None
                for i in range(6):
                    ins = pe.transpose(C.psbf[pb][:, i * 128:(i + 1) * 128], Vv[:, (g6 + i) * 128:(g6 + i + 1) * 128], C.identb[:])
                return ins
            S.op("pe", ftr, [b_V, C.ident_b], [psb[pb]])
            S.op("dve", lambda e, g6=g6, pb=pb: e.tensor_copy(out=Vtok[:, g6:g6 + 6, :], in_=C.psbf[pb][:, 0:768].rearrange("p (a b) -> p a b", a=6)), [psb[pb]], [b_Vtok])

        for d in range(2):
            with ExitStack() as sd:
                def sbd(name, shape, dt=F32):
                    return sd.enter_context(nc.sbuf_tensor(un(name), shape, dt))
                LC = sbd("LC", [128, T]); KD = sbd("KD", [128, T], BF16); AK = sbd("AK", [128, T], BF16)
                b_LC, b_LW, b_KD, b_AK = Buf(), Buf(), Buf(), Buf()
                lwg = nc.sbuf_tensor(un("LW"), [128, T], F32)
                LW = lwg.__enter__()
                hs = slice(d * 64, d * 64 + 64)
                for c, (t0, n, _) in enumerate(CHUNKS):
                    pw, pa = 2 + c % 2, 4 + c % 2
                    S.op("pe", lambda e, pw=pw, t0=t0, n=n: e.matmul(ps[pw][:, :n], L2[hs, 0, ocol], LH[hs, 0, t0:t0 + n], start=True, stop=True), [b_par, b_LH], [psb[pw]])
                    S.op("act", lambda e, pw=pw, t0=t0, n=n: e.activation(out=LW[:, t0:t0 + n], in_=ps[pw][:, :n], func=AF.Sigmoid, bias=VEC[:, 0 + d, o:o + 1]), [psb[pw], b_par], [b_LW])
                    tf, tb = next_tmpf(C)
                    S.op("pe", lambda e, pa=pa, t0=t0, n=n: e.matmul(ps[pa][:, :n], L2[hs, 1, ocol], LH[hs, 1, t0:t0 + n], start=True, stop=True), [b_par, b_LH], [psb[pa]])
                    S.op("act", lambda e, pa=pa, tf=tf, n=n: e.activation(out=tf[:, :n], in_=ps[pa][:, :n], func=AF.Sigmoid, bias=VEC[:, 2 + d, o:o + 1]), [psb[pa], b_par], [tb])
                    S.op("dve", lambda e, tf=tf, t0=t0, n=n: e.tensor_tensor(out=AK[:, t0:t0 + n], in0=tf[:, :n], in1=KK[:, t0:t0 + n], op=ALU.mult), [tb, b_KK], [b_AK])
                    S.op("dve", lambda e, tf=tf, n=n: e.tensor_scalar(out=tf[:, :n], in0=tf[:, :n], scalar1=-1.0, scalar2=VEC[:, 5, o:o + 1], op0=ALU.add, op1=ALU.mult), [tb, b_par], [tb])
                    S.op("dve", lambda e, tf=tf, t0=t0, n=n: e.scalar_tensor_tensor(out=KD[:, t0:t0 + n], in0=tf[:, :n], scalar=1.0, in1=Kk[:, t0:t0 + n], op0=ALU.add, op1=ALU.mult), [tb, b_K], [b_KD])
                    tg, tgb = next_tmpf(C)
                    S.op("dve", lambda e, tg=tg, t0=t0, n=n: e.scalar_tensor_tensor(out=tg[:, :n], in0=KD[:, t0:t0 + n], scalar=VEC[:, 6, o:o + 1], in1=Rr[:, t0:t0 + n], op0=ALU.mult, op1=ALU.mult), [b_KD, b_R, b_par], [tgb])
                    S.op("pe", lambda e, tg=tg, n=n: e.matmul(ps[6][:, :n], BD[:], tg[:, :n], start=True, stop=True), [tgb, b_par], [psb[6]])
                    S.op("dve", lambda e, tg=tg, t0=t0, n=n: e.tensor_tensor(out=tg[:, :n], in0=ps[6][:, :n], in1=Vv[:, t0:t0 + n], op=ALU.mult), [psb[6], b_V], [tgb])
                    S.op("pool", lambda e, tg=tg, t0=t0, n=n: e.tensor_tensor(out=BON[:, t0:t0 + n], in0=BON[:, t0:t0 + n], in1=tg[:, :n], op=ALU.add), [tgb, b_BON], [b_BON])
                S.op("dve", lambda e: e.tensor_scalar(out=LW[:], in0=LW[:], scalar1=-float(np.exp(-0.5)), scalar2=None, op0=ALU.mult), [b_LW], [b_LW])
                for n_ in range(T // 64):
                    cs = slice(n_ * 64, (n_ + 1) * 64)
                    if d == 0:
                        S.op("dve", lambda e, cs=cs: e.tensor_tensor_scan(out=LC[:, cs], data0=C.ones[:, 0:64], data1=LW[:, cs], initial=0.0, op0=ALU.mult, op1=ALU.add), [b_LW, C.ones_b], [b_LC])
                    else:
                        rs_ = slice(n_ * 64 + 63, (n_ * 64 - 1) if n_ > 0 else None, -1)
                        S.op("dve", lambda e, rs_=rs_: e.tensor_tensor_scan(out=LC[:, rs_], data0=C.ones[:, 0:64], data1=LW[:, rs_], initial=0.0, op0=ALU.mult, op1=ALU.add), [b_LW, C.ones_b], [b_LC])
                S.barrier()
                lwg.__exit__(None, None, None)
                LW = None
                if C.stop == 'RP':
                    continue
                rwkv_recurrence(C, o, d, Rr, KK, KD, AK, LC, LW, Vtok, OACC, b_OACC, MASK4,
                                [b_R, b_KK, b_KD, b_AK, b_LC, b_LW, b_Vtok, b_par], Gg, b_G, BON, b_BON, VEC)
                S.barrier()


def rwkv_recurrence(C, o, d, Rr, KK, KD, AK, LC, LW, Vtok, OACC, b_OACC, MASK4, inb, Gg, b_G, BON, b_BON, VEC):
    nc, S = C.nc, C.S
    from contextlib import ExitStack
    ps, psb = C.ps, C.ps_b
    b_R, b_KK, b_KD, b_AK, b_LC, b_LW, b_Vtok, b_par = inb
    with ExitStack() as st:
        def sbt(name, shape, dt=F32):
            return st.enter_context(nc.sbuf_tensor(un(name), shape, dt))
        Hf = sbt("rHf", [128, 64]); HbP = [sbt("rHb%d" % h, [128, 64], BF16) for h in range(2)]; b_H = [Buf(), Buf()]
        NR = 2

        def rot(name, shape, dt, nh=1, nr=NR):
            return [[sbt("%s%d_%d" % (name, h, i), shape, dt) for i in range(nr)] for h in range(nh)], [[Buf() for i in range(nr)] for h in range(nh)]
        EX, b_EX = rot("rEX", [128, 4, 128], F32)
        FM, b_FM = rot("rFM", [128, 6, 128], BF16)
        KAt, b_KAt = rot("rKAt", [128, 2, 128], BF16)
        DD, b_DD = rot("rDD", [128, 512], BF16, 2)
        UT, b_UT = rot("rUT", [128, 128], F32, 2, 1)
        PP, b_PP = rot("rPP", [128, 2, 256], F32, 2, 1)
        XX, b_XX = rot("rXX", [128, 2, 128], F32, 2, 1)
        U32, b_U32 = rot("rU32", [128, 128], F32, 2, 1)
        TTB, b_TTB = rot("rTTB", [128, 128], BF16, 2)
        X1, b_X1 = rot("rX1", [128, 64], BF16, 2)
        NE, b_NE = rot("rNE", [128, 64], BF16, 2)
        FIN, b_FIN = rot("rFIN", [128, 128], F32)
        FNb, b_FNb = rot("rFNb", [128, 128], BF16)
        YT, b_YT = rot("rYT", [128, 128], BF16)
        ST, b_ST = rot("rST", [128, 2, 8], F32)
        S.op("pool", lambda e: e.memset(Hf[:], 0.0), (), b_H)
        for h_ in range(2):
            S.op("pool", lambda e, h_=h_: e.memset(HbP[h_][:], 0.0), (), [b_H[h_]])
        order = list(range(18)) if d == 0 else [1, 0] + list(range(17, 1, -1))
        if C.stop and C.stop.startswith('T'):
            order = order[:int(C.stop[1:])]
        it = 0
        for tl in order:
            tt = slice(tl * 128, (tl + 1) * 128)
            ri = it % NR
            it += 1
            ex, bex = EX[0][ri], b_EX[0][ri]
            fm, bfm = FM[0][ri], b_FM[0][ri]
            LC3 = LC[:, tt].rearrange("p (a c) -> p a c", a=2)
            endc = 63 if d == 0 else 0
            ex03 = ex[:, 0, :].rearrange("p (a c) -> p a c", a=2)
            if d == 0:
                S.op("dve", lambda e, ex03=ex03, LC3=LC3: e.tensor_copy(out=ex03[:, :, 1:64], in_=LC3[:, :, 0:63]), [b_LC], [bex])
                S.op("pool", lambda e, ex03=ex03: e.memset(ex03[:, :, 0:1], 0.0), (), [bex])
            else:
                S.op("dve", lambda e, ex03=ex03, LC3=LC3: e.tensor_copy(out=ex03[:, :, 0:63], in_=LC3[:, :, 1:64]), [b_LC], [bex])
                S.op("pool", lambda e, ex03=ex03: e.memset(ex03[:, :, 63:64], 0.0), (), [bex])
            S.op("pool", lambda e, ex=ex, tt=tt: e.tensor_copy(out=ex[:, 1, :], in_=LC[:, tt]), [b_LC], [bex])
            S.op("pool", lambda e, ex=ex, tt=tt: e.tensor_scalar(out=ex[:, 2, :], in0=LC[:, tt], scalar1=-1.0, scalar2=None, op0=ALU.mult), [b_LC], [bex])
            S.op("dve", lambda e, ex=ex, LC3=LC3: e.tensor_tensor(out=ex[:, 3, :].rearrange("p (a c) -> p a c", a=2), in0=LC3[:, :, endc:endc + 1].to_broadcast([128, 2, 64]), in1=LC3, op=ALU.subtract), [b_LC], [bex])
            S.op("act", lambda e, ex=ex: e.activation(out=ex[:], in_=ex[:], func=AF.Exp), [bex], [bex])
            S.op("dve", lambda e, fm=fm, ex=ex, tt=tt: e.tensor_tensor(out=fm[:, 0, :], in0=KK[:, tt], in1=ex[:, 0, :], op=ALU.mult), [b_KK, bex], [bfm])
            S.op("pool", lambda e, fm=fm, ex=ex, tt=tt: e.tensor_tensor(out=fm[:, 1, :], in0=Rr[:, tt], in1=ex[:, 1, :], op=ALU.mult), [b_R, bex], [bfm])
            S.op("dve", lambda e, fm=fm, ex=ex, tt=tt: e.tensor_tensor(out=fm[:, 2, :], in0=KD[:, tt], in1=ex[:, 2, :], op=ALU.mult), [b_KD, bex], [bfm])
            S.op("pool", lambda e, fm=fm, ex=ex, tt=tt: e.tensor_tensor(out=fm[:, 3, :], in0=AK[:, tt], in1=ex[:, 2, :], op=ALU.mult), [b_AK, bex], [bfm])
            S.op("dve", lambda e, fm=fm, ex=ex, tt=tt: e.tensor_tensor(out=fm[:, 4, :], in0=KD[:, tt], in1=ex[:, 3, :], op=ALU.mult), [b_KD, bex], [bfm])
            S.op("pool", lambda e, fm=fm, ex=ex, tt=tt: e.tensor_tensor(out=fm[:, 5, :], in0=AK[:, tt], in1=ex[:, 3, :], op=ALU.mult), [b_AK, bex], [bfm])
            kat, bkat = KAt[0][ri], b_KAt[0][ri]

            def ftk(pe, fm=fm):
                pe.transpose(C.psbf[3][:, 0:128], fm[:, 4, :], C.identb[:])
                return pe.transpose(C.psbf[3][:, 128:256], fm[:, 5, :], C.identb[:])
            S.op("pe", ftk, [bfm, C.ident_b], [psb[3]])
            S.op("act", lambda e, kat=kat: e.activation(out=kat[:].rearrange("p a b -> p (a b)"), in_=C.psbf[3][:, 0:256], func=AF.Copy), [psb[3]], [bkat])
            def head_gen(h, tl=tl, tt=tt, ri=ri, ex=ex, bex=bex, fm=fm, bfm=bfm, kat=kat, bkat=bkat):
                hp = slice(h * 64, h * 64 + 64)
                hc = hp
                dd, bdd = DD[h][ri], b_DD[h][ri]
                pG = h

                def fgm(pe, fm=fm, pG=pG, hp=hp):
                    pe.matmul(ps[pG][:, 0:128], fm[hp, 3, :], fm[hp, 0, :], start=True, stop=True)
                    pe.matmul(ps[pG][:, 128:256], fm[hp, 2, :], fm[hp, 0, :], start=True, stop=True)
                    pe.matmul(ps[pG][:, 256:384], fm[hp, 2, :], fm[hp, 1, :], start=True, stop=True)
                    return pe.matmul(ps[pG][:, 384:512], fm[hp, 3, :], fm[hp, 1, :], start=True, stop=True)
                S.op("pe", fgm, [bfm], [psb[pG]])
                yield
                S.op("dve", lambda e, dd=dd, pG=pG: e.tensor_tensor(out=dd[:], in0=ps[pG][:, 0:512], in1=MASK4[:, d, :], op=ALU.mult), [psb[pG], b_par], [bdd])
                u32, bu32 = U32[h][0], b_U32[h][0]
                S.op("dve", lambda e, u32=u32, pG=pG: e.tensor_tensor(out=u32[:], in0=ps[pG][:, 0:128], in1=MASK4[:, d, 0:128], op=ALU.mult), [psb[pG], b_par], [bu32])
                ut, but = UT[h][0], b_UT[h][0]
                pS = 4 + h
                xx, bxx = XX[h][0], b_XX[h][0]
                pp, bpp = PP[h][0], b_PP[h][0]
                ttb, bttb = TTB[h][ri], b_TTB[h][ri]
                yield
                yield from solve_fp32(C, u32[:], bu32, pS, ut, but, pp, bpp, xx, bxx, ttb, bttb)
                bxx = bttb
                TT = ttb[:]
                x1, bx1 = X1[h][ri], b_X1[h][ri]
                ne, bne = NE[h][ri], b_NE[h][ri]
                pX, pO, pH = 6, 2, 7
                for cb in ((0, 1) if d == 0 else (1, 0)):
                    pp_ = slice(cb * 64, cb * 64 + 64)
                    cc = pp_
                    pccol = cb * 64 + (63 if d == 0 else 0)

                    def fx1(pe, pp_=pp_, cc=cc, fm=fm, dd=dd, hp=hp, hc=hc, tl=tl, h=h):
                        pe.matmul(ps[pX][pp_, h * 128:h * 128 + 64], fm[:, 0, cc], HbP[h][:, :], start=True, stop=False)
                        return pe.matmul(ps[pX][pp_, h * 128:h * 128 + 64], dd[pp_, 128 + cc.start:128 + cc.start + 64], Vtok[pp_, tl, hc], start=False, stop=True)
                    S.op("pe", fx1, [bfm, bdd, b_H[h], b_Vtok], [psb[pX]])
                    yield
                    S.op("act", lambda e, pp_=pp_, x1=x1: e.activation(out=x1[pp_, :], in_=ps[pX][pp_, h * 128:h * 128 + 64], func=AF.Copy), [psb[pX]], [bx1])
                    yield
                    S.op("pe", lambda e, pp_=pp_, cc=cc, TT=TT, x1=x1: e.matmul(ps[pX][pp_, h * 128 + 64:h * 128 + 128], TT[pp_, cc], x1[pp_, :], start=True, stop=True), [bxx, bx1], [psb[pX]])
                    yield
                    S.op("dve", lambda e, pp_=pp_, ne=ne: e.tensor_scalar(out=ne[pp_, :], in0=ps[pX][pp_, h * 128 + 64:h * 128 + 128], scalar1=-1.0, scalar2=None, op0=ALU.mult), [psb[pX]], [bne])
                    yield

                    def fo(pe, pp_=pp_, cc=cc, fm=fm, dd=dd, ne=ne, hp=hp, hc=hc, tl=tl, h=h):
                        pe.matmul(ps[pO][pp_, hc], fm[:, 1, cc], HbP[h][:, :], start=True, stop=False)
                        pe.matmul(ps[pO][pp_, hc], dd[pp_, 256 + cc.start:256 + cc.start + 64], Vtok[pp_, tl, hc], start=False, stop=False)
                        return pe.matmul(ps[pO][pp_, hc], dd[pp_, 384 + cc.start:384 + cc.start + 64], ne[pp_, :], start=False, stop=True)
                    S.op("pe", fo, [bfm, bdd, bne, b_H[h], b_Vtok], [psb[pO]])

                    def fh(pe, pp_=pp_, kat=kat, ne=ne, hp=hp, hc=hc, tl=tl):
                        pe.matmul(ps[pH][hp, 0:64], kat[pp_, 0, hc], Vtok[pp_, tl, hc], start=True, stop=False)
                        return pe.matmul(ps[pH][hp, 0:64], kat[pp_, 1, hc], ne[pp_, :], start=False, stop=True)
                    S.op("pe", fh, [bkat, bne, b_Vtok], [psb[pH]])
                    yield
                    S.op("dve", lambda e, hp=hp, ex=ex, pccol=pccol: e.scalar_tensor_tensor(
                        out=Hf[hp, :], in0=Hf[hp, :], scalar=ex[hp, 1, pccol:pccol + 1], in1=ps[pH][hp, 0:64], op0=ALU.mult, op1=ALU.add),
                        [psb[pH], bex, b_H[h]], [b_H[h]])
                    yield
                    S.op("act", lambda e, hp=hp, h=h: e.activation(out=HbP[h][hp, :], in_=Hf[hp, :], func=AF.Copy), [b_H[h]], [b_H[h]])
            run_lockstep([head_gen(0), head_gen(1)])
            pO = 2
            if d == 0:
                S.op("act", lambda e, tl=tl: e.activation(out=OACC[:, tl, :], in_=ps[pO][:, 0:128], func=AF.Copy), [psb[pO]], [b_OACC[tl]])
            else:
                fin, bfin = FIN[0][ri], b_FIN[0][ri]
                fnb, bfnb = FNb[0][ri], b_FNb[0][ri]
                yt, byt = YT[0][ri], b_YT[0][ri]
                stt, bst = ST[0][ri], b_ST[0][ri]
                S.op("dve", lambda e, fin=fin, tl=tl: e.tensor_tensor(out=fin[:], in0=ps[pO][:, 0:128], in1=OACC[:, tl, :], op=ALU.add), [psb[pO], b_OACC[tl]], [bfin])
                for h in range(2):
                    hc = slice(h * 64, h * 64 + 64)
                    tf, tb = next_tmpf(C)
                    S.op("act", lambda e, tf=tf, fin=fin, hc=hc, stt=stt, h=h: e.activation(out=tf[:, 0:64], in_=fin[:, hc], func=AF.Identity, accum_out=stt[:, h, 0:1]), [bfin], [tb, bst])
                    S.op("act", lambda e, tf=tf, fin=fin, hc=hc, stt=stt, h=h: e.activation(out=tf[:, 64:128], in_=fin[:, hc], func=AF.Square, accum_out=stt[:, h, 1:2]), [bfin], [tb, bst])
                S.op("dve", lambda e, stt=stt: e.tensor_scalar(out=stt[:, :, 2:3], in0=stt[:, :, 0:1], scalar1=1.0 / 64, scalar2=None, op0=ALU.mult), [bst], [bst])
                S.op("dve", lambda e, stt=stt: e.tensor_tensor(out=stt[:, :, 3:4], in0=stt[:, :, 2:3], in1=stt[:, :, 2:3], op=ALU.mult), [bst], [bst])
                S.op("dve", lambda e, stt=stt: e.scalar_tensor_tensor(out=stt[:, :, 4:5], in0=stt[:, :, 1:2], scalar=1.0 / 64, in1=stt[:, :, 3:4], op0=ALU.mult, op1=ALU.subtract), [bst], [bst])
                S.op("act", lambda e, stt=stt: e.activation(out=stt[:, :, 5:6], in_=stt[:, :, 4:5], func=AF.Sqrt, bias=64e-5), [bst], [bst])
                S.op("dve", lambda e, stt=stt: e.reciprocal(out=stt[:, :, 6:7], in_=stt[:, :, 5:6]), [bst], [bst])
                for h in range(2):
                    hc = slice(h * 64, h * 64 + 64)
                    S.op("dve", lambda e, fnb=fnb, fin=fin, hc=hc, stt=stt, h=h: e.tensor_scalar(out=fnb[:, hc], in0=fin[:, hc], scalar1=stt[:, h, 2:3], scalar2=stt[:, h, 6:7], op0=ALU.subtract, op1=ALU.mult), [bfin, bst], [bfnb])
                S.op("pe", lambda e, fnb=fnb: e.transpose(C.psbf[3][:, 512:640], fnb[:], C.identb[:]), [bfnb, C.ident_b], [psb[3]])
                tf, tb = next_tmpf(C)
                S.op("dve", lambda e, tf=tf: e.tensor_scalar(out=tf[:, 0:128], in0=C.psbf[3][:, 512:640], scalar1=VEC[:, 7, o:o + 1], scalar2=VEC[:, 8, o:o + 1], op0=ALU.mult, op1=ALU.add), [psb[3], b_par], [tb])
                S.op("pool", lambda e, tf=tf, tt=tt: e.tensor_tensor(out=tf[:, 0:128], in0=tf[:, 0:128], in1=BON[:, tt], op=ALU.add), [tb, b_BON], [tb])
                S.op("dve", lambda e, tf=tf, yt=yt, tt=tt: e.tensor_tensor(out=yt[:], in0=tf[:, 0:128], in1=Gg[:, tt], op=ALU.mult), [tb, b_G], [byt])
                S.dma("sp", C.Yscr[o, :, tt], yt[:], [byt], [C.Yscr_b])


def rwkv_out_proj(C, l, s):
    nc, S = C.nc, C.S
    from contextlib import ExitStack
    ps, psb = C.ps, C.ps_b
    S.barrier()
    with ExitStack() as st:
        def sbt(name, shape, dt):
            return st.enter_context(nc.sbuf_tensor(un(name), shape, dt))
        WO = sbt("rWO", [128, KT, KT, 128], BF16)
        YB = [sbt("rYB%d" % i, [128, KT, 512], BF16) for i in range(2)]
        b_WO = Buf()
        b_YB = [Buf(), Buf()]
        for o in range(KT):
            S.dma("pool", WO[:, o], C.rw_wout[o], (), [b_WO])
        it = 0
        for c, (t0, n, is_ctx) in enumerate(CHUNKS):
            col = NSEQ if is_ctx else s
            yb, byb = YB[c % 2], b_YB[c % 2]
            S.dma("sp", yb[:, :, 0:n], C.Yscr[0:KT, :, t0:t0 + n].rearrange("f p t -> p f t"), [C.Yscr_b], [byb])
            for o in range(KT):
                po = it % 2
                it += 1
                mm(S, ps[po][:, :n], [(WO[:, o, f, :], yb[:, f, 0:n]) for f in range(KT)], [b_WO, byb], [psb[po]])
                S.op("dve", lambda e, po=po, n=n, o=o, t0=t0, col=col: e.scalar_tensor_tensor(
                    out=C.R[:, o, t0:t0 + n], in0=ps[po][:, :n], scalar=C.MODS[:, l, 2 * 8 + o, col:col + 1],
                    in1=C.R[:, o, t0:t0 + n], op0=ALU.mult, op1=ALU.add),
                    [psb[po], C.MODS_b, C.Rb[o][c]], [C.Rb[o][c]])
        S.barrier()


def tile_w(w, ncol_tiles=None):
    K, N = w.shape
    return np.ascontiguousarray(w.reshape(K // 128, 128, N // 128, 128).transpose(2, 1, 0, 3))


def vec_t(v):
    sh = v.shape
    n = sh[-1] // 128
    a = v.reshape(sh[:-1] + (n, 128))
    return np.ascontiguousarray(np.moveaxis(a, -1, 0))


def host_shared(inp):
    sh = {}
    sh["mod_w_t"] = np.stack([tile_w(inp["mod_w"][l]) for l in range(DEPTH)])
    sh["mod_b_t"] = vec_t(inp["mod_b"])
    sh["norm_mix_t"] = vec_t(inp["norm_mix"])
    sh["norm_ffn_t"] = vec_t(inp["norm_ffn"])
    sh["final_norm_t"] = vec_t(inp["final_norm"])
    sh["ffn_w_in_t"] = np.stack([tile_w(inp["ffn_w_in"][l]) for l in range(DEPTH)])
    sh["ffn_w_out_t"] = np.stack([tile_w(inp["ffn_w_out"][l]) for l in range(DEPTH)])
    sh.update(host_gdn(inp))
    sh.update(host_s5(inp))
    sh.update(host_rwkv(inp))
    return sh


def host_rwkv(inp):
    sh = {}
    sh["rw_mu"] = vec_t(inp["rwkv_mu"][0])
    sh["rw_wrkv_t"] = np.stack([tile_w(inp["rwkv_w_rkv"][0, i]) for i in range(3)])
    w1cat = np.concatenate([inp["rwkv_w1"][0, 0], inp["rwkv_w1"][0, 1]], axis=1)
    a1cat = np.concatenate([inp["rwkv_a1"][0, 0], inp["rwkv_a1"][0, 1]], axis=1)
    sh["rw_l1_t"] = np.stack([tile_w(w1cat)[0], tile_w(a1cat)[0], tile_w(inp["rwkv_g1"][0])[0]])
    w2cat = np.concatenate([inp["rwkv_w2"][0, 0], inp["rwkv_w2"][0, 1]], axis=0)
    a2cat = np.concatenate([inp["rwkv_a2"][0, 0], inp["rwkv_a2"][0, 1]], axis=0)
    sh["rw_l2"] = np.ascontiguousarray(np.stack([w2cat, a2cat, inp["rwkv_g2"][0]]))
    vecs = [inp["rwkv_w0"][0, 0], inp["rwkv_w0"][0, 1], inp["rwkv_a0"][0, 0], inp["rwkv_a0"][0, 1],
            inp["rwkv_k_k"][0], inp["rwkv_k_a"][0], inp["rwkv_r_k"][0].reshape(-1), inp["rwkv_ln_w"][0], inp["rwkv_ln_b"][0]]
    sh["rw_vecs"] = vec_t(np.stack(vecs))
    sh["rw_wout_t"] = tile_w(inp["rwkv_w_out"][0])
    jj = np.arange(128)[:, None]
    tt = np.arange(128)[None, :]
    same = (jj // 64) == (tt // 64)
    m = np.zeros((2, 128, 512), np.float32)
    for d, (st_, inc_) in enumerate((((jj < tt), (jj <= tt)), ((jj > tt), (jj >= tt)))):
        m[d, :, 0:128] = same & st_
        m[d, :, 128:256] = same & st_
        m[d, :, 256:384] = same & inc_
        m[d, :, 384:512] = same & inc_
    sh["c_mask4"] = m
    return sh


def host_s5(inp):
    sh = {}
    lr, li = inp["s5_lambda_re"][0], inp["s5_lambda_im"][0]
    lam = np.stack([lr, li], 0)
    sh["s5_lam"] = np.ascontiguousarray(lam.transpose(3, 0, 1, 2).reshape(64, 2, 128))
    sh["s5_dt"] = np.ascontiguousarray(np.broadcast_to(inp["s5_log_dt"][0].reshape(1, 128), (64, 128)))
    B = np.stack([inp["s5_b_re"][0], inp["s5_b_im"][0]], 0)
    sh["s5_B"] = np.ascontiguousarray(B.transpose(3, 0, 1, 2, 4).reshape(64, 2, 128, 16))
    Cm = np.stack([inp["s5_c_re"][0], inp["s5_c_im"][0]], 0)
    sh["s5_C"] = np.ascontiguousarray(Cm.transpose(4, 0, 1, 2, 3).reshape(64, 2, 128, 16))
    sh["s5_dskip"] = vec_t(inp["s5_d"][0])
    sh["s5_wglu_t"] = tile_w(inp["s5_w_glu"][0])
    sh["s5_bglu_t"] = vec_t(inp["s5_b_glu"][0])
    sel = np.zeros((128, 4, 8, 128), np.float32)
    for r in range(128):
        q, c = (r % 64) // 16, r % 16
        for j in range(8):
            sel[r, q, j, 16 * j + c] = 1.0
    sh["c_sel4"] = sel
    jj = np.arange(128)[:, None] // 16
    tt = np.arange(128)[None, :] // 16
    sh["c_maskz"] = np.stack([(jj <= tt), (jj >= tt)]).astype(np.float32)
    return sh


def host_gdn(inp):
    sh = {}
    NG = inp["gdn_w_qkvz"].shape[0]
    sh["g_wqkvz_t"] = np.stack([tile_w(inp["gdn_w_qkvz"][j]) for j in range(NG)])
    cv = inp["gdn_conv"].reshape(NG, 5, 32, 128)
    sh["g_conv_t"] = np.ascontiguousarray(cv.transpose(0, 3, 2, 1))
    wab = inp["gdn_w_ab"]
    cat = np.concatenate([wab[:, 0], wab[:, 1]], axis=-1)
    cat = np.concatenate([cat, cat], axis=-1)
    sh["g_wab_t"] = np.ascontiguousarray(cat.reshape(NG, KT, 128, 128).transpose(0, 2, 1, 3))
    par = np.zeros((NG, 128, 8), np.float32)
    for c in range(128):
        d = (c % 64) // 32
        kind = ((c % 64) % 32) // 16
        hv = c % 16
        par[:, c, 0] = 1.0 if kind == 0 else -1.0
        if kind == 0:
            par[:, c, 1] = inp["gdn_dt_bias"][:, d, hv]
            par[:, c, 2] = inp["gdn_a_log"][:, d, hv]
        par[:, c, 3] = 1.0 if d == 1 else 0.0
        par[:, c, 4] = -1.0 if c >= 64 else 0.0
    sh["g_par"] = par
    sh["g_norm_t"] = np.ascontiguousarray(inp["gdn_norm"][:, :, None])
    sh["g_wout_t"] = np.stack([tile_w(inp["gdn_w_out"][j]) for j in range(NG)])
    sh["c_ident"] = np.eye(128, dtype=np.float32)
    jj = np.arange(128)[:, None]
    tt = np.arange(128)[None, :]
    same = (jj // 64) == (tt // 64)
    m = np.zeros((2, 128, 256), np.float32)
    m[0, :, 0:128] = same & (jj < tt)
    m[0, :, 128:256] = same & (jj <= tt)
    m[1, :, 0:128] = same & (jj > tt)
    m[1, :, 128:256] = same & (jj >= tt)
    sh["c_mask"] = m
    return sh


def host_core(inp, core, nseq=NSEQ):
    b0 = core * NSEQ
    m = {}
    xs = []
    for s in range(nseq):
        full = np.concatenate([inp["ctx"][b0 + s], inp["x"][b0 + s]], axis=0)
        xs.append(full.T.reshape(KT, 128, T))
    m["xT"] = np.ascontiguousarray(np.stack(xs))
    cc = np.concatenate([inp["c"][b0:b0 + NSEQ], inp["c_ctx"][None, :]], axis=0)
    m["cT"] = np.ascontiguousarray(cc.T.reshape(KT, 128, NSEQ + 1).transpose(1, 0, 2))
    return m


_CACHE = {}


def kernel(**inp):
    inp = {k: np.asarray(v, dtype=np.float32) for k, v in inp.items()}
    if "nc" not in _CACHE:
        _CACHE["nc"] = build_program()
    nc = _CACHE["nc"]
    shared = host_shared(inp)
    in_maps = []
    for core in range(NCORES):
        m = dict(shared)
        m.update(host_core(inp, core))
        in_maps.append(m)
    res = run_bass_kernel_spmd(nc, in_maps, core_ids=list(range(NCORES)))
    out = np.empty((NCORES * NSEQ, T_X, D), np.float32)
    for core in range(NCORES):
        o = res.results[core]["outT"]
        for s in range(NSEQ):
            out[core * NSEQ + s] = o[s].reshape(D, T_X).T
    return out
```

```python
import numpy as np
import concourse.bass as bass
import concourse.mybir as mybir
from concourse.bass_utils import run_bass_kernel_spmd

F32 = mybir.dt.float32
BF16 = mybir.dt.bfloat16
AF = mybir.ActivationFunctionType
ALU = mybir.AluOpType

D = 1024
KT = 8
T_CTX = 256
T_X = 2048
T = T_CTX + T_X
DEPTH = 4
FFN_H = 2816
FT = FFN_H // 128
NCORES = 8
NSEQ = 4
EPS = 1e-6
CHUNKS = [(0, 256, True), (256, 512, False), (768, 512, False), (1280, 512, False), (1792, 512, False)]
HALVES = [[0, 1, 2], [3, 4]]


class Buf:
    __slots__ = ("name", "w", "r", "excl")

    def __init__(self, name="", excl=False):
        self.name = name
        self.w = None
        self.r = {}
        self.excl = excl


class Sched:
    ENG = ("pe", "act", "dve", "pool", "sp")

    def __init__(self, nc, ndma=10):
        self.nc = nc
        self.eng = dict(pe=nc.tensor, act=nc.scalar, dve=nc.vector, pool=nc.gpsimd, sp=nc.sync)
        self.sems = {}
        self.cnt = {}
        for e in self.ENG:
            self.sems[e] = nc.alloc_semaphore("s_" + e)
            self.cnt[e] = 0
        self.seen = {e: {} for e in self.ENG}
        self.dq = {}
        for q in ("sp", "pool", "act"):
            keys = []
            for i in range(ndma):
                k = "d_%s%d" % (q, i)
                self.sems[k] = nc.alloc_semaphore(k)
                self.cnt[k] = 0
                keys.append(k)
            self.dq[q] = [keys, 0]
        self.nops = 0
        self.single = None
        self.record = []
        self.opidx = 0

    def _wait(self, e, key, val):
        if val <= 0 or self.seen[e].get(key, 0) >= val:
            return
        self.eng[e].wait_ge(self.sems[key], val)
        self.seen[e][key] = val

    def _deps(self, e, reads, writes, keep_one=False):
        need = {}
        for b in reads:
            if b.w is not None and b.w[1] > need.get(b.w[0], 0):
                need[b.w[0]] = b.w[1]
        for b in writes:
            if b.w is not None and b.w[1] > need.get(b.w[0], 0):
                need[b.w[0]] = b.w[1]
            for k, v in b.r.items():
                if v > need.get(k, 0):
                    need[k] = v
        todo = [(k, v) for k, v in need.items() if v > 0 and self.seen[e].get(k, 0) < v]
        kept = None
        if keep_one and todo:
            kept = todo.pop()
        for k, v in todo:
            self._wait(e, k, v)
        return kept

    def op(self, e, fn, reads=(), writes=()):
        xr = [b for b in reads if b.excl]
        if xr:
            writes = list(writes) + xr
            reads = [b for b in reads if not b.excl]
        i = self.opidx
        self.opidx += 1
        attach = None
        if self.single is not None and self.single[i]:
            attach = self._deps(e, reads, writes, keep_one=True)
        else:
            self._deps(e, reads, writes)
        n0 = self.nc.n_instructions() if self.single is None else 0
        ins = fn(self.eng[e])
        first = ins
        if isinstance(ins, tuple):
            first, ins = ins
            if self.single is None:
                self.record.append(True)
        elif self.single is None:
            self.record.append(self.nc.n_instructions() - n0 == 1)
        if attach is not None:
            key, val = attach
            first._wait_ge(self.sems[key], self.eng[e].lower_val(val))
            self.seen[e][key] = val
        self.cnt[e] += 1
        ins.then_inc(self.sems[e], 1)
        v = self.cnt[e]
        for b in reads:
            b.r[e] = v
        for b in writes:
            b.w = (e, v)
            b.r = {}
        self.nops += 1

    def dma(self, q, out, in_, reads=(), writes=()):
        keys, idx = self.dq[q]
        k = keys[idx % len(keys)]
        self.dq[q][1] += 1
        self._deps(q, reads, writes)
        self._wait(q, k, self.cnt[k])
        self.eng[q].dma_start(out=out, in_=in_).then_inc(self.sems[k], 16)
        self.cnt[k] += 16
        v = self.cnt[k]
        for b in reads:
            b.r[k] = v
        for b in writes:
            b.w = (k, v)
            b.r = {}
        self.nops += 1

    def barrier(self, engines=None):
        for e in (engines or self.ENG):
            for k, v in self.cnt.items():
                self._wait(e, k, v)

    def final_wait(self, e="sp"):
        for k, v in self.cnt.items():
            self._wait(e, k, v)


class Ctx:
    pass


_UID = [0]


def un(name):
    _UID[0] += 1
    return "%s_%d" % (name, _UID[0])


def mm(S, out_ap, pairs, reads, writes):
    n = len(pairs)

    def fn(pe):
        ins = first = None
        for i, (l, r) in enumerate(pairs):
            ins = pe.matmul(out_ap, l, r, start=(i == 0), stop=(i == n - 1))
            if first is None:
                first = ins
        return (first, ins)
    S.op("pe", fn, reads, writes)


def build_program(nseq=NSEQ, depth=DEPTH, mixers=True, debug=None, stop=None, single=None):
    nc = bass.Bass("TRN2", target_bir_lowering=False)
    C = Ctx()
    C.stop = stop
    import os as _os
    C.stop2 = int(_os.environ.get('STOP2', '0'))
    C.x0eng = _os.environ.get('X0ENG', 'pool')
    C.nh = int(_os.environ.get('NH', '2'))
    C.nlev = int(_os.environ.get('NLEV', '6'))
    C.nc = nc
    C.nseq = nseq
    S = Sched(nc)
    S.single = single
    C.S = S

    def din(name, shape, dt=F32):
        return nc.dram_tensor(name, list(shape), dt, kind="ExternalInput").ap()

    C.xT = din("xT", [nseq, KT, 128, T])
    C.cT = din("cT", [128, KT, NSEQ + 1])
    C.mod_w = din("mod_w_t", [DEPTH, 48, 128, KT, 128])
    C.mod_b = din("mod_b_t", [128, DEPTH, 48])
    C.norm_mix = din("norm_mix_t", [128, DEPTH, KT])
    C.norm_ffn = din("norm_ffn_t", [128, DEPTH, KT])
    C.final_norm = din("final_norm_t", [128, KT])
    C.ffn_w_in = din("ffn_w_in_t", [DEPTH, 2 * FT, 128, KT, 128])
    C.ffn_w_out = din("ffn_w_out_t", [DEPTH, KT, 128, FT, 128])
    C.outT = nc.dram_tensor("outT", [nseq, KT, 128, T_X], F32, kind="ExternalOutput").ap()
    if debug:
        C.dbg = nc.dram_tensor("dbg", [KT, 128, T], F32, kind="ExternalOutput").ap()
        C.dbgY = nc.dram_tensor("dbgY", [16, 128, T], BF16, kind="ExternalOutput").ap()

    sb = nc.alloc_sbuf_tensor
    C.H = sb("H", [128, KT, T], BF16)
    C.Hb = [[Buf("H%d_%d" % (k, c)) for c in range(5)] for k in range(KT)]
    C.ones = sb("ones", [128, 128], F32)
    C.ones_b = Buf("ones")
    C.MODS = sb("MODS", [128, DEPTH, 48, NSEQ + 1], F32)
    C.MODS_b = Buf("MODS")
    C.AM = sb("AM", [128, DEPTH, 2, KT, NSEQ + 1], F32)
    C.AM_b = Buf("AM")
    C.nrm = sb("nrm", [128, 2, DEPTH, KT], F32)
    C.fnrm = sb("fnrm", [128, KT], F32)
    C.nrm_b = Buf("nrm")
    C.modb = sb("modb", [128, DEPTH, 48], F32)
    C.scT = sb("scT", [128, KT, NSEQ + 1], F32)
    C.scT_b = Buf("scT")
    C.tmpf = [sb("tmpf%d" % i, [128, 512], F32) for i in range(3)]
    C.tmpf_b = [Buf("tmpf%d" % i) for i in range(3)]
    C.tmpf_i = 0
    C.rs = sb("rs", [128, 512], F32)
    C.rs_b = Buf("rs")
    C.ps = [nc.alloc_psum_tensor("ps%d" % i, [128, 512], F32) for i in range(8)]
    C.psbf = [p[:].bitcast(BF16) for p in C.ps]
    C.ps_b = [Buf("ps%d" % i, excl=True) for i in range(8)]
    gdn_declare(C)
    s5_declare(C)
    rwkv_declare(C)

    prologue(C)
    consts_load(C)
    if mixers and depth > 1:
        s5_setup(C)
    for s in range(nseq):
        with r_scope(C, C.xT[s]):
            norm_modulate(C, 0, 0, s)
        for l in range(depth):
            kind, j = l % 3, l // 3
            if mixers:
                if kind == 0:
                    gdn_mixer(C, l, j, s)
                    if debug and debug[1] == l and s == 0:
                        S.barrier()
                        for f in range(16):
                            S.dma("sp", C.dbgY[f], C.Yscr[f], (), ())
                        S.barrier()
                elif kind == 1:
                    s5_mixer(C, l, s)
                else:
                    rwkv_mixer(C, l, s)
            last = (l == depth - 1)
            with r_scope(C, C.Rscr, store=not last):
                if mixers:
                    if kind == 0:
                        gdn_out_proj(C, l, j, s)
                    elif kind == 1:
                        apply_mixer_out(C, l, s)
                    else:
                        rwkv_out_proj(C, l, s)
                if debug == ("xmix", l) and s == 0:
                    dump_R(C)
                norm_modulate(C, l, 1, s)
                ffn(C, l, s)
                if debug == ("x", l) and s == 0:
                    dump_R(C)
                if last:
                    final_norm_store(C, s)
                else:
                    norm_modulate(C, l + 1, 0, s)
    S.final_wait("sp")
    nc._sched_record = S.record
    return nc


class r_scope:
    def __init__(self, C, src, store=True):
        self.C = C
        self.src = src
        self.store = store

    def __enter__(self):
        C = self.C
        C.S.barrier()
        self.g = C.nc.sbuf_tensor(un("R"), [128, KT, T], F32)
        C.R = self.g.__enter__()
        C.Rb = [[Buf("R%d_%d" % (k, c)) for c in range(5)] for k in range(KT)]
        for k in range(KT):
            C.S.dma("sp" if k % 2 == 0 else "act", C.R[:, k, :], self.src[k], (), C.Rb[k])
        return self

    def __exit__(self, *a):
        C = self.C
        if self.store:
            for k in range(KT):
                C.S.dma("sp" if k % 2 == 0 else "act", C.Rscr[k], C.R[:, k, :], C.Rb[k], ())
        C.S.barrier()
        self.g.__exit__(*a)
        C.R = None
        return False


def next_tmpf(C):
    i = C.tmpf_i % len(C.tmpf)
    C.tmpf_i += 1
    return C.tmpf[i], C.tmpf_b[i]


def dump_R(C):
    S = C.S
    for k in range(KT):
        S.dma("sp", C.dbg[k], C.R[:, k, :], reads=C.Rb[k], writes=())


def prologue(C):
    nc, S = C.nc, C.S
    S.op("dve", lambda e: e.memset(C.ones[:], 1.0), (), [C.ones_b])
    S.dma("sp", C.scT[:], C.cT, (), [C.scT_b])
    S.dma("sp", C.modb[:], C.mod_b, (), [C.nrm_b])
    S.dma("sp", C.nrm[:, 0], C.norm_mix, (), [C.nrm_b])
    S.dma("sp", C.nrm[:, 1], C.norm_ffn, (), [C.nrm_b])
    S.dma("sp", C.fnrm[:], C.final_norm, (), [C.nrm_b])
    S.op("act", lambda e: e.activation(out=C.scT[:], in_=C.scT[:], func=AF.Silu), [C.scT_b], [C.scT_b])
    NW = 3
    with (nc.sbuf_tensor(un("wm0"), [128, KT, 128], F32) as wm0,
          nc.sbuf_tensor(un("wm1"), [128, KT, 128], F32) as wm1,
          nc.sbuf_tensor(un("wm2"), [128, KT, 128], F32) as wm2):
        wm = [wm0, wm1, wm2]
        wm_b = [Buf("wm%d" % i) for i in range(NW)]
        i = 0
        for l in range(DEPTH):
            for f in range(48):
                w, wb = wm[i % NW], wm_b[i % NW]
                S.dma("sp" if i % 2 == 0 else "act", w[:], C.mod_w[l, f], (), [wb])
                pb = i % 4
                mm(S, C.ps[pb][:, 0:NSEQ + 1],
                   [(w[:, k, :], C.scT[:, k, :]) for k in range(KT)],
                   [wb, C.scT_b], [C.ps_b[pb]])
                S.op("dve", lambda e, l=l, f=f, pb=pb: e.tensor_scalar(
                    out=C.MODS[:, l, f, :], in0=C.ps[pb][:, 0:NSEQ + 1], scalar1=C.modb[:, l, f:f + 1],
                    scalar2=None, op0=ALU.add), [C.ps_b[pb], C.nrm_b], [C.MODS_b])
                i += 1
        S.barrier()
    for l in range(DEPTH):
        for sub, j in ((0, 1), (1, 4)):
            S.op("dve", lambda e, l=l, sub=sub, j=j: e.scalar_tensor_tensor(
                out=C.AM[:, l, sub], in0=C.MODS[:, l, j * 8:(j + 1) * 8, :], scalar=1.0,
                in1=C.nrm[:, sub, l, :].unsqueeze(2).to_broadcast([128, KT, NSEQ + 1]),
                op0=ALU.add, op1=ALU.mult), [C.MODS_b, C.nrm_b], [C.AM_b])
    S.barrier()


def load_seq(C, s):
    S = C.S
    for k in range(KT):
        S.dma("sp" if k % 2 == 0 else "act", C.R[:, k, :], C.xT[s, k], (), C.Rb[k])


def rstd_chunk(C, c):
    S = C.S
    t0, n, _ = CHUNKS[c]
    pb = 7
    for k in range(KT):
        tf, tb = next_tmpf(C)
        S.op("act", lambda e, k=k, tf=tf: e.activation(out=tf[:, :n], in_=C.R[:, k, t0:t0 + n], func=AF.Square),
             [C.Rb[k][c]], [tb])
        S.op("pe", lambda e, k=k, tf=tf: e.matmul(C.ps[pb][:, :n], C.ones[:], tf[:, :n], start=(k == 0), stop=(k == KT - 1)),
             [tb, C.ones_b], [C.ps_b[pb]])
    S.op("act", lambda e: e.activation(out=C.rs[:, :n], in_=C.ps[pb][:, :n], func=AF.Sqrt, scale=1.0 / D, bias=EPS),
         [C.ps_b[pb]], [C.rs_b])
    S.op("dve", lambda e: e.reciprocal(out=C.rs[:, :n], in_=C.rs[:, :n]), [C.rs_b], [C.rs_b])


def norm_modulate(C, l, sub, s):
    S = C.S
    jshift = 0 if sub == 0 else 3
    for c, (t0, n, is_ctx) in enumerate(CHUNKS):
        col = NSEQ if is_ctx else s
        rstd_chunk(C, c)
        for k in range(KT):
            tf, tb = next_tmpf(C)
            S.op("dve", lambda e, k=k, tf=tf: e.tensor_tensor(out=tf[:, :n], in0=C.R[:, k, t0:t0 + n], in1=C.rs[:, :n], op=ALU.mult),
                 [C.Rb[k][c], C.rs_b], [tb])
            S.op("act", lambda e, k=k, tf=tf: e.activation(
                out=C.H[:, k, t0:t0 + n], in_=tf[:, :n], func=AF.Identity,
                scale=C.AM[:, l, sub, k, col:col + 1], bias=C.MODS[:, l, jshift * 8 + k, col:col + 1]),
                [tb, C.AM_b, C.MODS_b], [C.Hb[k][c]])


def ffn(C, l, s):
    nc, S = C.nc, C.S
    jg = 5
    S.barrier()
    with (
        nc.sbuf_tensor(un("ACTB"), [128, FT, 1280], BF16) as ACTB,
        nc.sbuf_tensor(un("wi0"), [128, 2, KT, 128], BF16) as wi0,
        nc.sbuf_tensor(un("wi1"), [128, 2, KT, 128], BF16) as wi1,
        nc.sbuf_tensor(un("wo0"), [128, FT, 128], BF16) as wo0,
        nc.sbuf_tensor(un("wo1"), [128, FT, 128], BF16) as wo1,
        nc.sbuf_tensor(un("sg0"), [128, 512], F32) as sg0,
        nc.sbuf_tensor(un("sg1"), [128, 512], F32) as sg1,
    ):
        wi, wi_b = [wi0, wi1], [Buf("wi0"), Buf("wi1")]
        wo, wo_b = [wo0, wo1], [Buf("wo0"), Buf("wo1")]
        sg, sg_b = [sg0, sg1], [Buf("sg0"), Buf("sg1")]
        it = 0
        for half in HALVES:
            hb = CHUNKS[half[0]][0]
            AB = [[Buf("AB") for _ in half] for _ in range(FT)]
            for f in range(FT):
                w, wb = wi[f % 2], wi_b[f % 2]
                S.dma("pool", w[:, 0], C.ffn_w_in[l, f], (), [wb])
                S.dma("pool", w[:, 1], C.ffn_w_in[l, FT + f], (), [wb])
                for ci, c in enumerate(half):
                    t0, n, is_ctx = CHUNKS[c]
                    pg, pu = (it % 3) * 2, (it % 3) * 2 + 1
                    hreads = [C.Hb[k][c] for k in range(KT)]
                    mm(S, C.ps[pg][:, :n], [(w[:, 0, k, :], C.H[:, k, t0:t0 + n]) for k in range(KT)],
                       [wb] + hreads, [C.ps_b[pg]])
                    mm(S, C.ps[pu][:, :n], [(w[:, 1, k, :], C.H[:, k, t0:t0 + n]) for k in range(KT)],
                       [wb] + hreads, [C.ps_b[pu]])
                    g_, gb = sg[it % 2], sg_b[it % 2]
                    S.op("act", lambda e, pg=pg, g_=g_, n=n: e.activation(out=g_[:, :n], in_=C.ps[pg][:, :n], func=AF.Silu),
                         [C.ps_b[pg]], [gb])
                    S.op("dve", lambda e, pu=pu, g_=g_, n=n, f=f, t0=t0: e.tensor_tensor(
                        out=ACTB[:, f, t0 - hb:t0 - hb + n], in0=g_[:, :n], in1=C.ps[pu][:, :n], op=ALU.mult),
                        [gb, C.ps_b[pu]], [AB[f][ci]])
                    it += 1
            for o in range(KT):
                w, wb = wo[o % 2], wo_b[o % 2]
                S.dma("pool", w[:], C.ffn_w_out[l, o], (), [wb])
                for ci, c in enumerate(half):
                    t0, n, is_ctx = CHUNKS[c]
                    col = NSEQ if is_ctx else s
                    po = 6 + (it % 2)
                    it += 1
                    mm(S, C.ps[po][:, :n], [(w[:, f, :], ACTB[:, f, t0 - hb:t0 - hb + n]) for f in range(FT)],
                       [wb] + [AB[f][ci] for f in range(FT)], [C.ps_b[po]])
                    S.op("dve", lambda e, po=po, n=n, o=o, t0=t0, col=col: e.scalar_tensor_tensor(
                        out=C.R[:, o, t0:t0 + n], in0=C.ps[po][:, :n], scalar=C.MODS[:, l, jg * 8 + o, col:col + 1],
                        in1=C.R[:, o, t0:t0 + n], op0=ALU.mult, op1=ALU.add),
                        [C.ps_b[po], C.MODS_b, C.Rb[o][c]], [C.Rb[o][c]])
        S.barrier()


def final_norm_store(C, s):
    S = C.S
    for c, (t0, n, is_ctx) in enumerate(CHUNKS):
        if is_ctx:
            continue
        rstd_chunk(C, c)
        for k in range(KT):
            tf, tb = next_tmpf(C)
            S.op("dve", lambda e, k=k, tf=tf: e.scalar_tensor_tensor(
                out=tf[:, :n], in0=C.R[:, k, t0:t0 + n], scalar=C.fnrm[:, k:k + 1], in1=C.rs[:, :n],
                op0=ALU.mult, op1=ALU.mult), [C.Rb[k][c], C.rs_b, C.nrm_b], [tb])
            S.dma("sp", C.outT[s, k, :, t0 - T_CTX:t0 - T_CTX + n], tf[:, :n], [tb], ())
    S.barrier()


def tile_tokens(tile):
    return tile * 128


def gdn_declare(C):
    nc = C.nc

    def din(name, shape, dt=F32):
        return nc.dram_tensor(name, list(shape), dt, kind="ExternalInput").ap()
    C.g_wqkvz = din("g_wqkvz_t", [2, 48, 128, KT, 128])
    C.g_conv = din("g_conv_t", [2, 128, 32, 5])
    C.g_wab = din("g_wab_t", [2, 128, KT, 128])
    C.g_par = din("g_par", [2, 128, 8])
    C.g_norm = din("g_norm_t", [2, 128, 1])
    C.g_wout = din("g_wout_t", [2, KT, 128, 16, 128])
    C.c_ident = din("c_ident", [128, 128])
    C.c_mask = din("c_mask", [2, 128, 256])
    C.Yscr = nc.dram_tensor("Yscr", [16, 128, T], BF16, kind="Internal").ap()
    sb = nc.alloc_sbuf_tensor
    C.ident = sb("ident", [128, 128], F32)
    C.identb = sb("identb", [128, 128], BF16)
    C.ident_b = Buf("ident")
    C.Yscr_b = Buf("Yscr")
    C.Rscr = nc.dram_tensor("Rscr", [KT, 128, T], F32, kind="Internal").ap()


def consts_load(C):
    S = C.S
    S.dma("sp", C.ident[:], C.c_ident, (), [C.ident_b])
    S.op("dve", lambda e: e.tensor_copy(out=C.identb[:], in_=C.ident[:]), [C.ident_b], [C.ident_b])


def gdn_mixer(C, l, j, s):
    nc, S = C.nc, C.S
    from contextlib import ExitStack
    S.barrier()
    ps = C.ps
    psb = C.ps_b
    with ExitStack() as st:
        def sbt(name, shape, dt):
            return st.enter_context(nc.sbuf_tensor(un(name), shape, dt))
        GCALL = sbt("GCALL", [128, T], F32)
        TOKP = sbt("TOKP", [128, 18, 6, 32], F32)
        GPAR = sbt("GPAR", [128, 8], F32)
        MASK = sbt("MASK", [128, 2, 256], F32)
        GNW = sbt("GNW", [128, 1], F32)
        CONVW = sbt("CONVW", [128, 32, 5], F32)
        b_GC, b_TOKP, b_par = Buf("GCALL"), Buf("TOKP"), Buf("gpar")
        S.dma("sp", GPAR[:], C.g_par[j], (), [b_par])
        S.dma("sp", MASK[:, 0, :], C.c_mask[0], (), [b_par])
        S.dma("sp", MASK[:, 1, :], C.c_mask[1], (), [b_par])
        S.dma("sp", GNW[:], C.g_norm[j], (), [b_par])
        S.dma("sp", CONVW[:], C.g_conv[j], (), [b_par])
        S.op("act", lambda e: e.activation(out=GPAR[:, 5:6], in_=GPAR[:, 2:3], func=AF.Exp), [b_par], [b_par])
        S.op("dve", lambda e: e.tensor_scalar(out=GPAR[:, 5:6], in0=GPAR[:, 5:6], scalar1=-1.0, scalar2=None, op0=ALU.mult), [b_par], [b_par])

        with ExitStack() as sa:
            def sba(name, shape, dt):
                return sa.enter_context(nc.sbuf_tensor(un(name), shape, dt))
            GL = sba("GL", [128, T], F32)
            PF = sba("PF", [128, T], F32)
            TMP = sba("TMPA", [128, T], F32)
            WAB = sba("WAB", [128, KT, 128], BF16)
            GLt = sba("GLt", [128, 128], F32)
            b_GL, b_PF, b_TMP, b_WAB, b_GLt = Buf(), Buf(), Buf(), Buf(), Buf()
            S.dma("pool", WAB[:], C.g_wab[j], (), [b_WAB])
            for c, (t0, n, _) in enumerate(CHUNKS):
                pb = c % 2
                mm(S, ps[pb][:, :n], [(WAB[:, k, :], C.H[:, k, t0:t0 + n]) for k in range(KT)],
                   [b_WAB] + [C.Hb[k][c] for k in range(KT)], [psb[pb]])
                tf, tb = next_tmpf(C)
                S.op("act", lambda e, pb=pb, tf=tf, n=n: e.activation(out=tf[:, :n], in_=ps[pb][:, :n], func=AF.Exp,
                                                                  scale=GPAR[:, 0:1], bias=GPAR[:, 1:2]), [psb[pb], b_par], [tb])
                S.op("act", lambda e, tf=tf, n=n: e.activation(out=tf[:, :n], in_=tf[:, :n], func=AF.Ln, bias=1.0), [tb], [tb])
                S.op("dve", lambda e, tf=tf, n=n, t0=t0: e.tensor_scalar(out=GL[:, t0:t0 + n], in0=tf[:, :n], scalar1=GPAR[:, 5:6],
                                                                       scalar2=None, op0=ALU.mult), [tb, b_par], [b_GL])
            for n_ in range(T // 64):
                S.op("dve", lambda e, n_=n_: e.tensor_tensor_scan(
                    out=PF[:, n_ * 64:(n_ + 1) * 64], data0=C.ones[:, 0:64], data1=GL[:, n_ * 64:(n_ + 1) * 64],
                    initial=0.0, op0=ALU.mult, op1=ALU.add), [b_GL, C.ones_b], [b_PF])
            PF3 = PF[:].rearrange("p (n c) -> p n c", c=64)
            TB = PF3[:, :, 63:64].to_broadcast([128, T // 64, 64])
            TMP3 = TMP[:].rearrange("p (n c) -> p n c", c=64)
            GL3 = GL[:].rearrange("p (n c) -> p n c", c=64)
            GC3 = GCALL[:].rearrange("p (n c) -> p n c", c=64)
            S.op("dve", lambda e: e.tensor_tensor(out=TMP3, in0=TB, in1=PF3, op=ALU.subtract), [b_PF], [b_TMP])
            S.op("dve", lambda e: e.tensor_tensor(out=TMP[:], in0=TMP[:], in1=PF[:], op=ALU.subtract), [b_TMP, b_PF], [b_TMP])
            S.op("dve", lambda e: e.tensor_tensor(out=TMP[:], in0=TMP[:], in1=GL[:], op=ALU.add), [b_TMP, b_GL], [b_TMP])
            S.op("dve", lambda e: e.scalar_tensor_tensor(out=TMP[:], in0=TMP[:], scalar=GPAR[:, 3:4], in1=PF[:],
                                                         op0=ALU.mult, op1=ALU.add), [b_TMP, b_PF, b_par], [b_TMP])
            S.op("dve", lambda e: e.scalar_tensor_tensor(out=GCALL[:], in0=GL[:], scalar=GPAR[:, 4:5], in1=TMP[:],
                                                         op0=ALU.mult, op1=ALU.add), [b_TMP, b_GL, b_par], [b_GC])
            S.op("dve", lambda e: e.tensor_tensor(out=TMP3, in0=TB, in1=GC3, op=ALU.subtract), [b_PF, b_GC, b_TMP], [b_TMP])
            for tl in range(18):
                tt = slice(tl * 128, (tl + 1) * 128)
                pb = 2 + (tl % 2)

                def ftr(pe, tt=tt, pb=pb):
                    pe.transpose(ps[pb][:, 0:128], GCALL[:, tt], C.ident[:])
                    pe.transpose(ps[pb][:, 128:256], TMP[:, tt], C.ident[:])
                    return pe.transpose(ps[pb][:, 256:384], GL[:, tt], C.ident[:])
                S.op("pe", ftr, [b_GC, b_TMP, b_GL, C.ident_b], [psb[pb]])
                S.op("act", lambda e, pb=pb: e.activation(out=GLt[:], in_=ps[pb][:, 256:384], func=AF.Copy), [psb[pb]], [b_GLt])
                def rows(ap_, base):
                    return ap_[:, base:base + 64].rearrange("p (d r) -> p d r", d=2)[:, :, 0:16]
                logb = rows(GLt[:], 16)
                gc_ = rows(ps[pb][:, 0:128], 0)
                gcx_ = rows(ps[pb][:, 0:128], 64)
                dk_ = rows(ps[pb][:, 128:256], 0)
                da_ = rows(ps[pb][:, 128:256], 64)

                def outv(w):
                    return TOKP[:, tl, w, :].rearrange("p (d r) -> p d r", d=2)
                for w, src, op_ in ((0, gcx_, ALU.subtract), (1, gc_, ALU.subtract), (2, gc_, ALU.subtract), (3, gcx_, ALU.subtract),
                                    (4, dk_, ALU.add), (5, da_, ALU.add)):
                    S.op("dve", lambda e, w=w, src=src, op_=op_, outv=outv, logb=logb: e.tensor_tensor(out=outv(w), in0=src, in1=logb, op=op_),
                         [psb[pb], b_GLt], [b_TOKP])
            S.op("act", lambda e: e.activation(out=TOKP[:, :, 4:6, :], in_=TOKP[:, :, 4:6, :], func=AF.Exp), [b_TOKP], [b_TOKP])
            S.barrier()

        for hg in range(8 if not C.stop else (0 if C.stop == 'A' else 1)):
            gdn_head_group(C, l, j, s, hg, GCALL, b_GC, TOKP, b_TOKP, MASK, GNW, CONVW, b_par)
        S.barrier()


def gdn_head_group(C, l, j, s, hg, GCALL, b_GC, TOKP, b_TOKP, MASK, GNW, CONVW, b_par):
    nc, S = C.nc, C.S
    from contextlib import ExitStack
    ps, psb = C.ps, C.ps_b
    with ExitStack() as st:
        def sbt(name, shape, dt):
            return st.enter_context(nc.sbuf_tensor(un(name), shape, dt))
        QT = sbt("QT", [128, T], BF16)
        KTt = sbt("KTt", [128, T], BF16)
        Vtok = sbt("Vtok", [128, 18, 2, 128], BF16)
        Ktok = sbt("Ktok", [128, 18, 128], BF16)
        SZ = sbt("SZ", [128, 2, T], BF16)
        b_QT, b_KT, b_Vtok, b_Ktok, b_SZ = Buf(), Buf(), Buf(), Buf(), Buf()
        with ExitStack() as sp_:
            def sbp(name, shape, dt):
                return sp_.enter_context(nc.sbuf_tensor(un(name), shape, dt))
            PRE = sbp("PRE", [128, T + 8], F32)
            CONVO = sbp("CONVO", [128, T + 4], F32)
            VT = sbp("VT", [128, T], BF16)
            W0 = sbp("W0", [128, KT, 128], BF16)
            W1 = sbp("W1", [128, KT, 128], BF16)
            Wb = [W0, W1]
            b_W = [Buf(), Buf()]
            b_PRE, b_CONVO, b_VT = Buf(), Buf(), Buf()
            S.op("pool", lambda e: e.memset(PRE[:], 0.0), (), [b_PRE])
            feats = [("q", hg), ("k", 8 + hg), ("v0", 16 + 2 * hg), ("v1", 16 + 2 * hg + 1),
                     ("z0", 32 + 2 * hg), ("z1", 32 + 2 * hg + 1)]
            for fi, (kind, f) in enumerate(feats):
                W, bW = Wb[fi % 2], b_W[fi % 2]
                S.dma("pool", W[:], C.g_wqkvz[j, f], (), [bW])
                for c, (t0, n, is_ctx) in enumerate(CHUNKS):
                    pb = c % 2
                    mm(S, ps[pb][:, :n], [(W[:, k, :], C.H[:, k, t0:t0 + n]) for k in range(KT)],
                       [bW] + [C.Hb[k][c] for k in range(KT)], [psb[pb]])
                    if kind[0] == "z":
                        S.op("act", lambda e, pb=pb, n=n, t0=t0, kind=kind: e.activation(
                            out=SZ[:, int(kind[1]), t0:t0 + n], in_=ps[pb][:, :n], func=AF.Silu), [psb[pb]], [b_SZ])
                    else:
                        off = t0 + 2 if is_ctx else t0 + 6
                        S.op("act", lambda e, pb=pb, n=n, off=off: e.activation(
                            out=PRE[:, off:off + n], in_=ps[pb][:, :n], func=AF.Copy), [psb[pb]], [b_PRE])
                if kind[0] == "z":
                    continue
                NW_ = T + 4
                S.op("dve", lambda e, f=f: e.tensor_scalar(out=CONVO[:, 0:NW_], in0=PRE[:, 0:NW_], scalar1=CONVW[:, f, 0:1],
                                                          scalar2=None, op0=ALU.mult), [b_PRE, b_par], [b_CONVO])
                for tap in range(1, 5):
                    S.op("dve", lambda e, f=f, tap=tap: e.scalar_tensor_tensor(
                        out=CONVO[:, 0:NW_], in0=PRE[:, tap:tap + NW_], scalar=CONVW[:, f, tap:tap + 1], in1=CONVO[:, 0:NW_],
                        op0=ALU.mult, op1=ALU.add), [b_PRE, b_CONVO, b_par], [b_CONVO])
                if kind[0] == "v":
                    hvl = int(kind[1])
                    S.op("act", lambda e: e.activation(out=VT[:, 0:T_CTX], in_=CONVO[:, 0:T_CTX], func=AF.Silu), [b_CONVO], [b_VT])
                    S.op("act", lambda e: e.activation(out=VT[:, T_CTX:T], in_=CONVO[:, T_CTX + 4:T + 4], func=AF.Silu), [b_CONVO], [b_VT])
                    for g4 in range(0, 18, 6):
                        pb = 2 + ((g4 // 6) % 2)

                        def ftr(pe, g4=g4, pb=pb):
                            ins = None
                            for i in range(6):
                                ins = pe.transpose(C.psbf[pb][:, i * 128:(i + 1) * 128], VT[:, (g4 + i) * 128:(g4 + i + 1) * 128], C.identb[:])
                            return ins
                        S.op("pe", ftr, [b_VT, C.ident_b], [psb[pb]])
                        S.op("dve", lambda e, g4=g4, pb=pb, hvl=hvl: e.tensor_copy(
                            out=Vtok[:, g4:g4 + 6, hvl, :], in_=C.psbf[pb][:, 0:768].rearrange("p (a b) -> p a b", a=6)), [psb[pb]], [b_Vtok])
                else:
                    S.op("act", lambda e: e.activation(out=CONVO[:, 0:T_CTX], in_=CONVO[:, 0:T_CTX], func=AF.Silu), [b_CONVO], [b_CONVO])
                    S.op("act", lambda e: e.activation(out=CONVO[:, T_CTX + 4:T + 4], in_=CONVO[:, T_CTX + 4:T + 4], func=AF.Silu), [b_CONVO], [b_CONVO])
                    dst, bdst = (QT, b_QT) if kind == "q" else (KTt, b_KT)
                    qscale = 128.0 ** -0.5 if kind == "q" else 1.0
                    for c, (t0, n, is_ctx) in enumerate(CHUNKS):
                        off = t0 if is_ctx else t0 + 4
                        tf, tb = next_tmpf(C)
                        S.op("act", lambda e, tf=tf, n=n, off=off: e.activation(out=tf[:, :n], in_=CONVO[:, off:off + n], func=AF.Square), [b_CONVO], [tb])
                        S.op("pe", lambda e, tf=tf, n=n: e.matmul(ps[7][:, :n], C.ones[:], tf[:, :n], start=True, stop=True), [tb, C.ones_b], [psb[7]])
                        S.op("act", lambda e, n=n: e.activation(out=C.rs[:, :n], in_=ps[7][:, :n], func=AF.Sqrt, bias=1e-6), [psb[7]], [C.rs_b])
                        S.op("dve", lambda e, n=n: e.reciprocal(out=C.rs[:, :n], in_=C.rs[:, :n]), [C.rs_b], [C.rs_b])
                        S.op("dve", lambda e, n=n, off=off, t0=t0, dst=dst, qscale=qscale: e.scalar_tensor_tensor(
                            out=dst[:, t0:t0 + n], in0=CONVO[:, off:off + n], scalar=qscale, in1=C.rs[:, :n], op0=ALU.mult, op1=ALU.mult),
                            [b_CONVO, C.rs_b], [bdst])
                    if kind == "k":
                        for g4 in range(0, 18, 6):
                            pb = 2 + ((g4 // 6) % 2)

                            def ftr(pe, g4=g4, pb=pb):
                                ins = None
                                for i in range(6):
                                    ins = pe.transpose(C.psbf[pb][:, i * 128:(i + 1) * 128], KTt[:, (g4 + i) * 128:(g4 + i + 1) * 128], C.identb[:])
                                return ins
                            S.op("pe", ftr, [b_KT, C.ident_b], [psb[pb]])
                            S.op("dve", lambda e, g4=g4, pb=pb: e.tensor_copy(
                                out=Ktok[:, g4:g4 + 6, :], in_=C.psbf[pb][:, 0:768].rearrange("p (a b) -> p a b", a=6)), [psb[pb]], [b_Ktok])
            S.barrier()
        if C.stop == 'P':
            return
        gdn_recurrence(C, l, j, s, hg, GCALL, b_GC, TOKP, b_TOKP, MASK, GNW, b_par, QT, KTt, Vtok, Ktok, SZ,
                       [b_QT, b_KT, b_Vtok, b_Ktok, b_SZ])
        S.barrier()


def run_lockstep(gens):
    live = list(gens)
    while live:
        nxt = []
        for g in live:
            try:
                next(g)
                nxt.append(g)
            except StopIteration:
                pass
        live = nxt


def gdn_recurrence(C, l, j, s, hg, GCALL, b_GC, TOKP, b_TOKP, MASK, GNW, b_par, QT, KTt, Vtok, Ktok, SZ, inb):
    nc, S = C.nc, C.S
    from contextlib import ExitStack
    ps, psb = C.ps, C.ps_b
    b_QT, b_KT, b_Vtok, b_Ktok, b_SZ = inb
    order1 = [1, 0] + list(range(17, 1, -1))
    pos1 = {tl: i for i, tl in enumerate(order1)}
    with ExitStack() as st:
        def sbt(name, shape, dt):
            return st.enter_context(nc.sbuf_tensor(un(name), shape, dt))
        OACC = sbt("OACC", [128, 18, 2, 128], F32)
        b_OACC = [[Buf() for _ in range(2)] for _ in range(18)]
        fin_it = [0]

        def chain(h, d):
            c = 2 * d + h
            hv = 2 * hg + h
            idx = d * 16 + hv
            rc = d * 32 + hv
            rx = 64 + rc
            pR, pGm, pX, pO, pS = c // 2, 2, 2, 3, 4 + c
            rb0 = (c % 2) * 256
            RB = ps[pR][:, rb0:rb0 + 256]
            ocol = slice(c * 128, (c + 1) * 128)

            def tb(name, shape, dt):
                return sbt("%s_c%d" % (name, c), shape, dt), Buf()

            def tb2(name, shape, dt):
                return [tb(name + "a", shape, dt), tb(name + "b", shape, dt)]
            Hf, b_H = tb("Hf", [128, 128], F32)
            Hb, _ = tb("Hb", [128, 128], BF16)
            GM, b_GM = tb("GM", [128, 256], F32)
            E2 = tb2("ERB", [128, 256], F32)
            BR2 = tb2("BR", [128, 2, 128], BF16)
            KA2 = tb2("KA", [128, 2, 128], BF16)
            ar, bar = tb("ARG", [128, 512], F32)
            DD2 = tb2("DD", [128, 512], BF16)
            ut, but = tb("UT", [128, 128], F32)
            pp, bpp = tb("PP", [128, 2, 256], F32)
            xx, bxx = tb("XX", [128, 2, 128], F32)
            u32, bu32 = tb("U32", [128, 128], F32)
            TT2 = tb2("TTB", [128, 128], BF16)
            x1, bx1 = tb("X1", [128, 128], BF16)
            ne, bne = tb("NE", [128, 128], BF16)
            fos, bfos = tb("FOS", [128, 128], F32)
            fon, bfon = tb("FON", [128, 128], BF16)
            fyt, bfyt = tb("FYT", [128, 128], BF16)
            fsq, bfsq = tb("FSQ", [128, 4], F32)
            order = list(range(18)) if d == 0 else order1
            if C.stop and C.stop.startswith('T'):
                order = order[:int(C.stop[1:])]
            prog = {"prep": 0, "state": 0}

            def prep():
                for i, tl in enumerate(order):
                    while i - prog["state"] >= 2:
                        yield
                    k = i % 2
                    (e_, be_), (br, bbr), (ka, bka), (dd, bdd), (ttb, bttb) = E2[k], BR2[k], KA2[k], DD2[k], TT2[k]
                    tt = slice(tl * 128, (tl + 1) * 128)

                    def fg(pe):
                        _m0 = pe.matmul(ps[pGm][:, 256:384], KTt[:, tt], KTt[:, tt], start=True, stop=True)
                        _m1 = pe.matmul(ps[pGm][:, 384:512], KTt[:, tt], QT[:, tt], start=True, stop=True)
                        return (_m0, _m1)
                    S.op("pe", fg, [b_KT, b_QT], [psb[pGm]])
                    S.op("dve", lambda e: e.tensor_tensor(out=GM[:], in0=ps[pGm][:, 256:512], in1=MASK[:, d, :], op=ALU.mult), [psb[pGm], b_par], [b_GM])

                    def frb(pe):
                        _m0 = pe.matmul(RB[:, 0:128], C.ident[:, rx:rx + 1].to_broadcast([128, 128]), GCALL[:, tt], start=True, stop=True)
                        _m1 = pe.matmul(RB[:, 128:256], C.ident[:, rc:rc + 1].to_broadcast([128, 128]), GCALL[:, tt], start=True, stop=True)
                        return (_m0, _m1)
                    S.op("pe", frb, [b_GC, C.ident_b], [psb[pR]])
                    yield
                    S.op("act", lambda e: e.activation(out=e_[:], in_=RB, func=AF.Exp), [psb[pR]], [be_])
                    in0 = RB.rearrange("p (a t) -> p a t", a=2).unsqueeze(2).to_broadcast([128, 2, 2, 128])
                    in1 = TOKP[:, tl, 0:4, idx:idx + 1].rearrange("p (a b) o -> p a b o", a=2).to_broadcast([128, 2, 2, 128])
                    ar4 = ar[:].rearrange("p (a b t) -> p a b t", a=2, b=2)
                    S.op("dve", lambda e: e.tensor_tensor(out=ar4, in0=in0, in1=in1, op=ALU.subtract), [psb[pR], b_TOKP], [bar])
                    yield
                    S.op("pool", lambda e: e.tensor_scalar(out=ar[:], in0=ar[:], scalar1=0.0, scalar2=None, op0=ALU.min), [bar], [bar])
                    S.op("pool", lambda e: e.tensor_tensor(out=br[:, 0, :], in0=KTt[:, tt], in1=e_[:, 0:128], op=ALU.mult), [b_KT, be_], [bbr])
                    S.op("pool", lambda e: e.tensor_tensor(out=br[:, 1, :], in0=QT[:, tt], in1=e_[:, 128:256], op=ALU.mult), [b_QT, be_], [bbr])
                    yield
                    S.op("act", lambda e: e.activation(out=ar[:], in_=ar[:], func=AF.Exp), [bar], [bar])
                    S.op("pool", lambda e: e.tensor_scalar(out=ka[:, 0, :], in0=Ktok[:, tl, :], scalar1=TOKP[:, tl, 4, idx:idx + 1], scalar2=None, op0=ALU.mult), [b_Ktok, b_TOKP], [bka])
                    S.op("pool", lambda e: e.tensor_scalar(out=ka[:, 1, :], in0=Ktok[:, tl, :], scalar1=TOKP[:, tl, 5, idx:idx + 1], scalar2=None, op0=ALU.mult), [b_Ktok, b_TOKP], [bka])
                    yield
                    gm4 = GM[:].rearrange("p (a t) -> p a t", a=2).unsqueeze(2).to_broadcast([128, 2, 2, 128])
                    dd4 = dd[:].rearrange("p (a b t) -> p a b t", a=2, b=2)
                    S.op("dve", lambda e: e.tensor_tensor(out=u32[:], in0=ar[:, 0:128], in1=GM[:, 0:128], op=ALU.mult), [bar, b_GM], [bu32])
                    S.op("dve", lambda e: e.tensor_tensor(out=dd4, in0=ar4, in1=gm4, op=ALU.mult), [bar, b_GM], [bdd])
                    yield
                    yield from solve_fp32(C, u32[:], bu32, pS, ut, but, pp, bpp, xx, bxx, ttb, bttb)
                    prog["prep"] = i + 1

            def state():
                S.op("pool", lambda e: e.memset(Hf[:], 0.0), (), [b_H])
                S.op("pool", lambda e: e.memset(Hb[:], 0.0), (), [b_H])
                yield
                for i, tl in enumerate(order):
                    while prog["prep"] <= i:
                        yield
                    k = i % 2
                    (e_, be_), (br, bbr), (ka, bka), (dd, bdd), (ttb, bttb) = E2[k], BR2[k], KA2[k], DD2[k], TT2[k]
                    tt = slice(tl * 128, (tl + 1) * 128)
                    TT = ttb[:]
                    for cb in ((0, 1) if d == 0 else (1, 0)):
                        pp_ = slice(cb * 64, cb * 64 + 64)
                        cc = pp_
                        pccol = 128 + (cb * 64 + 63 if d == 0 else cb * 64)

                        def fx1(pe):
                            _m0 = pe.matmul(ps[pX][pp_, 0:128], br[:, 0, cc], Hb[:], start=True, stop=False)
                            _m1 = pe.matmul(ps[pX][pp_, 0:128], dd[pp_, 128 + cc.start:128 + cc.start + 64], Vtok[pp_, tl, h, :], start=False, stop=True)
                            return (_m0, _m1)
                        S.op("pe", fx1, [bbr, bdd, b_H, b_Vtok], [psb[pX]])
                        S.op("act", lambda e: e.activation(out=x1[pp_, :], in_=ps[pX][pp_, 0:128], func=AF.Copy), [psb[pX]], [bx1])
                        yield
                        S.op("pe", lambda e: e.matmul(ps[pX][pp_, 0:128], TT[pp_, cc], x1[pp_, :], start=True, stop=True), [bttb, bx1], [psb[pX]])
                        S.op("dve", lambda e: e.tensor_scalar(out=ne[pp_, :], in0=ps[pX][pp_, 0:128], scalar1=-1.0, scalar2=None, op0=ALU.mult), [psb[pX]], [bne])
                        yield

                        def fo(pe):
                            _m0 = pe.matmul(ps[pO][pp_, ocol], br[:, 1, cc], Hb[:], start=True, stop=False)
                            _m1 = pe.matmul(ps[pO][pp_, ocol], dd[pp_, 256 + cc.start:256 + cc.start + 64], Vtok[pp_, tl, h, :], start=False, stop=False)
                            _m2 = pe.matmul(ps[pO][pp_, ocol], dd[pp_, 384 + cc.start:384 + cc.start + 64], ne[pp_, :], start=False, stop=True)
                            return (_m0, _m2)
                        S.op("pe", fo, [bbr, bdd, bne, b_H, b_Vtok], [psb[pO]])

                        def fh(pe):
                            _m0 = pe.matmul(ps[pX][:, 128:256], ka[pp_, 0, :], Vtok[pp_, tl, h, :], start=True, stop=False)
                            _m1 = pe.matmul(ps[pX][:, 128:256], ka[pp_, 1, :], ne[pp_, :], start=False, stop=True)
                            return (_m0, _m1)
                        S.op("pe", fh, [bka, bne, b_Vtok], [psb[pX]])
                        S.op("dve", lambda e: e.scalar_tensor_tensor(
                            out=Hf[:], in0=Hf[:], scalar=e_[:, pccol:pccol + 1], in1=ps[pX][:, 128:256], op0=ALU.mult, op1=ALU.add),
                            [psb[pX], be_, b_H], [b_H])
                        yield
                        S.op("act", lambda e: e.activation(out=Hb[:], in_=Hf[:], func=AF.Copy), [b_H], [b_H])
                        yield
                    first = (tl < pos1[tl]) if d == 0 else (pos1[tl] < tl)
                    if C.stop and C.stop.startswith('T'):
                        first = (d == 0)
                    if first:
                        S.op("act", lambda e: e.activation(out=OACC[:, tl, h, :], in_=ps[pO][:, ocol], func=AF.Copy), [psb[pO]], [b_OACC[tl][h]])
                        yield
                    else:
                        os_, bos, on, bon, yt, byt, sq, bsq = fos, bfos, fon, bfon, fyt, bfyt, fsq, bfsq
                        S.op("dve", lambda e: e.tensor_tensor(out=os_[:], in0=ps[pO][:, ocol], in1=OACC[:, tl, h, :], op=ALU.add), [psb[pO], b_OACC[tl][h]], [bos])
                        yield
                        tf, tbf = next_tmpf(C)
                        S.op("act", lambda e: e.activation(out=tf[:, 0:128], in_=os_[:], func=AF.Square, accum_out=sq[:, 0:1]), [bos], [tbf, bsq])
                        S.op("act", lambda e: e.activation(out=sq[:, 1:2], in_=sq[:, 0:1], func=AF.Sqrt, scale=1.0 / 128, bias=EPS), [bsq], [bsq])
                        yield
                        S.op("dve", lambda e: e.reciprocal(out=sq[:, 2:3], in_=sq[:, 1:2]), [bsq], [bsq])
                        S.op("dve", lambda e: e.tensor_scalar(out=on[:], in0=os_[:], scalar1=sq[:, 2:3], scalar2=None, op0=ALU.mult), [bos, bsq], [bon])
                        yield
                        S.op("pe", lambda e: e.transpose(C.psbf[pS][:, 896:1024], on[:], C.identb[:]), [bon, C.ident_b], [psb[pS]])
                        S.op("dve", lambda e: e.scalar_tensor_tensor(
                            out=yt[:], in0=C.psbf[pS][:, 896:1024], scalar=GNW[:, 0:1], in1=SZ[:, h, tt], op0=ALU.mult, op1=ALU.mult),
                            [psb[pS], b_par, b_SZ], [byt])
                        yield
                        S.dma("sp", C.Yscr[hv, :, tt], yt[:], [byt], [C.Yscr_b])
                        yield
                    prog["state"] = i + 1
            return [prep(), state()]
        gens = []
        for d in range(2):
            for h in range(C.nh):
                gens += chain(h, d)
        run_lockstep(gens)


def solve_fp32(C, U32, bU, pS, UT, bUT, PP, bPP, XX, bXX, TTb, bTT):
    S = C.S
    ps, psb = C.ps, C.ps_b
    S.op("pe", lambda e: e.transpose(ps[pS][:, 384:512], U32, C.ident[:]), [bU, C.ident_b], [psb[pS]])
    S.op("pool", lambda e: e.tensor_tensor(out=XX[:, 0, :], in0=C.ident[:], in1=U32, op=ALU.subtract), [bU, C.ident_b], [bXX])
    yield
    S.op("act", lambda e: e.activation(out=UT[:], in_=ps[pS][:, 384:512], func=AF.Copy), [psb[pS]], [bUT])
    yield
    Pk, PTk = U32, UT[:]
    pdeps = [bU, bUT]
    for k in range(1, 6):
        cur = k % 2
        last = (k == 5)

        def fp(pe, Pk=Pk, PTk=PTk, last=last):
            m1 = None
            if not last:
                m1 = pe.matmul(ps[pS][:, 0:128], PTk, Pk, start=True, stop=True)
            m2 = pe.matmul(ps[pS][:, 128:256], Pk, PTk, start=True, stop=True)
            return (m1 if m1 is not None else m2, m2)
        S.op("pe", fp, pdeps, [psb[pS]])
        yield
        S.op("act", lambda e, cur=cur: e.activation(out=PP[:, cur, :], in_=ps[pS][:, 0:256], func=AF.Copy), [psb[pS]], [bPP])
        yield
        Pk, PTk = PP[:, cur, 0:128], PP[:, cur, 128:256]
        pdeps = [bPP]
        xprev = XX[:, (k - 1) % 2, :]
        S.op("pe", lambda e, PTk=PTk, xprev=xprev: e.matmul(ps[pS][:, 256:384], PTk, xprev, start=True, stop=True), [bPP, bXX], [psb[pS]])
        yield
        if last:
            S.op("dve", lambda e, xprev=xprev: e.tensor_tensor(out=TTb[:], in0=ps[pS][:, 256:384], in1=xprev, op=ALU.add), [psb[pS], bXX], [bTT])
        else:
            xcur = XX[:, k % 2, :]
            S.op("dve", lambda e, xcur=xcur, xprev=xprev: e.tensor_tensor(out=xcur, in0=ps[pS][:, 256:384], in1=xprev, op=ALU.add), [psb[pS], bXX], [bXX])
        yield

def gdn_out_proj(C, l, j, s):
    nc, S = C.nc, C.S
    from contextlib import ExitStack
    ps, psb = C.ps, C.ps_b
    S.barrier()
    with ExitStack() as st:
        def sbt(name, shape, dt):
            return st.enter_context(nc.sbuf_tensor(un(name), shape, dt))
        WO = sbt("WO", [128, KT, 16, 128], BF16)
        YB = [sbt("YB%d" % i, [128, 16, 512], BF16) for i in range(2)]
        b_WO = Buf()
        b_YB = [Buf(), Buf()]
        for o in range(KT):
            S.dma("pool", WO[:, o], C.g_wout[j, o], (), [b_WO])
        it = 0
        for c, (t0, n, is_ctx) in enumerate(CHUNKS):
            col = NSEQ if is_ctx else s
            yb, byb = YB[c % 2], b_YB[c % 2]
            S.dma("sp", yb[:, :, 0:n], C.Yscr[:, :, t0:t0 + n].rearrange("f p t -> p f t"), [C.Yscr_b], [byb])
            for o in range(KT):
                po = it % 2
                it += 1
                mm(S, ps[po][:, :n], [(WO[:, o, f, :], yb[:, f, 0:n]) for f in range(16)], [b_WO, byb], [psb[po]])
                S.op("dve", lambda e, po=po, n=n, o=o, t0=t0, col=col: e.scalar_tensor_tensor(
                    out=C.R[:, o, t0:t0 + n], in0=ps[po][:, :n], scalar=C.MODS[:, l, 2 * 8 + o, col:col + 1],
                    in1=C.R[:, o, t0:t0 + n], op0=ALU.mult, op1=ALU.add),
                    [psb[po], C.MODS_b, C.Rb[o][c]], [C.Rb[o][c]])
        S.barrier()


NCH = T // 8
NCC = T_CTX // 8
I32 = mybir.dt.int32


def s5_declare(C):
    nc = C.nc

    def din(name, shape, dt=F32):
        return nc.dram_tensor(name, list(shape), dt, kind="ExternalInput").ap()
    C.s5_lam = din("s5_lam", [64, 2, 128])
    C.s5_dt = din("s5_dt", [64, 128])
    C.s5_B = din("s5_B", [64, 2, 128, 16])
    C.s5_C = din("s5_C", [64, 2, 128, 16])
    C.s5_dskip = din("s5_dskip", [128, KT])
    C.s5_wglu = din("s5_wglu_t", [16, 128, KT, 128])
    C.s5_bglu = din("s5_bglu_t", [128, 16])
    C.c_sel4 = din("c_sel4", [128, 4, 8, 128])
    C.c_maskz = din("c_maskz", [2, 128, 128])
    C.s5_Tz = nc.dram_tensor("s5_Tz", [128, 128, 128], BF16, kind="Internal").ap()
    C.s5_W = nc.dram_tensor("s5_W", [128, 128, 2, 64], BF16, kind="Internal").ap()
    C.s5_Vi = nc.dram_tensor("s5_Vi", [128, 64, 2, 128], BF16, kind="Internal").ap()
    C.A8 = nc.alloc_sbuf_tensor("A8", [64, 2, 128], F32)
    C.A8_b = Buf("A8")
    C.MOscr = nc.dram_tensor("MOscr", [KT, 128, T], F32, kind="Internal").ap()


def s5_setup(C):
    nc, S = C.nc, C.S
    from contextlib import ExitStack
    ps, psb = C.ps, C.ps_b
    S.barrier()
    with ExitStack() as st:
        def sbt(name, shape, dt=F32):
            return st.enter_context(nc.sbuf_tensor(un(name), shape, dt))
        LAM = sbt("LAM", [64, 2, 128]); DT = sbt("DT", [64, 128])
        Bt = sbt("Bt", [64, 2, 128, 16]); Ct = sbt("Ct", [64, 2, 128, 16])
        MZ = sbt("MZ", [128, 2, 128])
        AA = sbt("AA", [64, 2, 128]); AI = sbt("AI", [64, 2, 128]); FF = sbt("FF", [64, 2, 128])
        ZT = sbt("ZT", [64, 9, 2, 128]); QT_ = sbt("QT", [64, 15, 2, 128])
        W1 = [sbt("s5w%d" % i, [64, 128]) for i in range(6)]
        KI = sbt("KI", [64, 128], I32)
        b = Buf("s5setup")
        S.dma("sp", LAM[:], C.s5_lam, (), [b]); S.dma("sp", DT[:], C.s5_dt, (), [b])
        S.dma("sp", Bt[:], C.s5_B, (), [b]); S.dma("sp", Ct[:], C.s5_C, (), [b])
        S.dma("sp", MZ[:, 0, :], C.c_maskz[0], (), [b]); S.dma("sp", MZ[:, 1, :], C.c_maskz[1], (), [b])

        def V(fn):
            S.op("dve", fn, [b], [b])

        def A(fn):
            S.op("act", fn, [b], [b])
        lr, li = LAM[:, 0, :], LAM[:, 1, :]
        t0_, t1_, t2_, t3_, t4_, t5_ = [w[:] for w in W1]
        A(lambda e: e.activation(out=DT[:], in_=DT[:], func=AF.Exp))
        V(lambda e: e.tensor_tensor(out=t0_, in0=DT[:], in1=lr, op=ALU.mult))
        A(lambda e: e.activation(out=t0_, in_=t0_, func=AF.Exp))
        V(lambda e: e.tensor_tensor(out=t1_, in0=DT[:], in1=li, op=ALU.mult))

        def sincos(dst, shift):
            V(lambda e: e.tensor_scalar(out=t2_, in0=t1_, scalar1=1.0 / (2 * np.pi), scalar2=64.0 + shift, op0=ALU.mult, op1=ALU.add))
            V(lambda e: e.tensor_copy(out=KI[:], in_=t2_))
            V(lambda e: e.tensor_copy(out=t3_, in_=KI[:]))
            V(lambda e: e.tensor_tensor(out=t2_, in0=t2_, in1=t3_, op=ALU.subtract))
            A(lambda e: e.activation(out=dst, in_=t2_, func=AF.Sin, scale=2 * np.pi))
        sincos(t4_, 0.0)
        sincos(t5_, 0.25)
        V(lambda e: e.tensor_tensor(out=AA[:, 0, :], in0=t0_, in1=t5_, op=ALU.mult))
        V(lambda e: e.tensor_tensor(out=AA[:, 1, :], in0=t0_, in1=t4_, op=ALU.mult))
        V(lambda e: e.tensor_tensor(out=t1_, in0=t0_, in1=t0_, op=ALU.mult))
        V(lambda e: e.reciprocal(out=t1_, in_=t1_))
        V(lambda e: e.tensor_tensor(out=AI[:, 0, :], in0=AA[:, 0, :], in1=t1_, op=ALU.mult))
        V(lambda e: e.scalar_tensor_tensor(out=AI[:, 1, :], in0=AA[:, 1, :], scalar=-1.0, in1=t1_, op0=ALU.mult, op1=ALU.mult))
        V(lambda e: e.tensor_tensor(out=t1_, in0=lr, in1=lr, op=ALU.mult))
        V(lambda e: e.tensor_tensor(out=t2_, in0=li, in1=li, op=ALU.mult))
        V(lambda e: e.tensor_tensor(out=t1_, in0=t1_, in1=t2_, op=ALU.add))
        V(lambda e: e.reciprocal(out=t1_, in_=t1_))
        V(lambda e: e.tensor_scalar(out=t2_, in0=AA[:, 0, :], scalar1=-1.0, scalar2=None, op0=ALU.add))
        V(lambda e: e.tensor_tensor(out=t3_, in0=t2_, in1=lr, op=ALU.mult))
        V(lambda e: e.tensor_tensor(out=t4_, in0=AA[:, 1, :], in1=li, op=ALU.mult))
        V(lambda e: e.tensor_tensor(out=t3_, in0=t3_, in1=t4_, op=ALU.add))
        V(lambda e: e.tensor_tensor(out=FF[:, 0, :], in0=t3_, in1=t1_, op=ALU.mult))
        V(lambda e: e.tensor_tensor(out=t3_, in0=AA[:, 1, :], in1=lr, op=ALU.mult))
        V(lambda e: e.tensor_tensor(out=t4_, in0=t2_, in1=li, op=ALU.mult))
        V(lambda e: e.tensor_tensor(out=t3_, in0=t3_, in1=t4_, op=ALU.subtract))
        V(lambda e: e.tensor_tensor(out=FF[:, 1, :], in0=t3_, in1=t1_, op=ALU.mult))

        def cmul(dst, x, y):
            V(lambda e: e.tensor_tensor(out=t0_, in0=x[:, 0, :], in1=y[:, 0, :], op=ALU.mult))
            V(lambda e: e.tensor_tensor(out=t1_, in0=x[:, 1, :], in1=y[:, 1, :], op=ALU.mult))
            V(lambda e: e.tensor_tensor(out=dst[:, 0, :], in0=t0_, in1=t1_, op=ALU.subtract))
            V(lambda e: e.tensor_tensor(out=t0_, in0=x[:, 0, :], in1=y[:, 1, :], op=ALU.mult))
            V(lambda e: e.tensor_tensor(out=t1_, in0=x[:, 1, :], in1=y[:, 0, :], op=ALU.mult))
            V(lambda e: e.tensor_tensor(out=dst[:, 1, :], in0=t0_, in1=t1_, op=ALU.add))
        V(lambda e: e.memset(ZT[:, 0, 0, :], 1.0))
        V(lambda e: e.memset(ZT[:, 0, 1, :], 0.0))
        for e_ in range(1, 9):
            cmul(ZT[:, e_], ZT[:, e_ - 1], AA[:])
        V(lambda e: e.tensor_copy(out=QT_[:, 7], in_=FF[:]))
        for e_ in range(1, 8):
            cmul(QT_[:, 7 + e_], QT_[:, 7 + e_ - 1], AA[:])
            cmul(QT_[:, 7 - e_], QT_[:, 7 - e_ + 1], AI[:])
        S.op("dve", lambda e: e.tensor_copy(out=C.A8[:], in_=ZT[:, 8]), [b], [C.A8_b])

        VP = sbt("VP", [64, 16, 2, 8, 16]); VI = sbt("VI", [64, 16, 2, 8, 16])
        WA = sbt("WA", [64, 16, 2, 8, 16]); WB = sbt("WB", [64, 16, 2, 8, 16])
        TM = sbt("TMs5", [64, 16, 16])
        TZs = sbt("TZs", [128, 16, 128], BF16); WSs = sbt("WSs", [128, 16, 2, 64], BF16); VIs = sbt("VIs", [64, 16, 2, 128], BF16)
        b_st = Buf("s5stage")
        for d in range(2):
            ea = (lambda j: 7 - j) if d == 0 else (lambda j: j)
            eb = (lambda j: -j) if d == 0 else (lambda j: j - 7)
            ev = (lambda t: t) if d == 0 else (lambda t: 7 - t)
            ei = (lambda t: t + 1) if d == 0 else (lambda t: 8 - t)
            for gb in range(4):
                dg = slice(d * 64 + gb * 16, d * 64 + gb * 16 + 16)
                Cr, Ci = Ct[:, 0, dg, :], Ct[:, 1, dg, :]
                Br, Bi = Bt[:, 0, dg, :], Bt[:, 1, dg, :]

                def bc(ap_):
                    return ap_.unsqueeze(2).to_broadcast([64, 16, 16])
                for t in range(8):
                    for (tab, ee) in ((VP, ev(t)), (VI, ei(t))):
                        zr, zi = bc(ZT[:, ee, 0, dg]), bc(ZT[:, ee, 1, dg])
                        V(lambda e, tab=tab, t=t, zr=zr: e.tensor_tensor(out=tab[:, :, 0, t, :], in0=Cr, in1=zr, op=ALU.mult))
                        V(lambda e, zi=zi: e.tensor_tensor(out=TM[:], in0=Ci, in1=zi, op=ALU.mult))
                        V(lambda e, tab=tab, t=t: e.tensor_tensor(out=tab[:, :, 0, t, :], in0=tab[:, :, 0, t, :], in1=TM[:], op=ALU.subtract))
                        V(lambda e, tab=tab, t=t, zi=zi: e.tensor_tensor(out=tab[:, :, 1, t, :], in0=Cr, in1=zi, op=ALU.mult))
                        V(lambda e, zr=zr: e.tensor_tensor(out=TM[:], in0=Ci, in1=zr, op=ALU.mult))
                        V(lambda e, tab=tab, t=t: e.scalar_tensor_tensor(out=tab[:, :, 1, t, :], in0=tab[:, :, 1, t, :], scalar=-1.0, in1=TM[:],
                                                                       op0=ALU.mult, op1=ALU.subtract))
                    for (tab, ee) in ((WA, ea(t)), (WB, eb(t))):
                        qr, qi = bc(QT_[:, 7 + ee, 0, dg]), bc(QT_[:, 7 + ee, 1, dg])
                        V(lambda e, tab=tab, t=t, qr=qr: e.tensor_tensor(out=tab[:, :, 0, t, :], in0=Br, in1=qr, op=ALU.mult))
                        V(lambda e, qi=qi: e.tensor_tensor(out=TM[:], in0=Bi, in1=qi, op=ALU.mult))
                        V(lambda e, tab=tab, t=t: e.tensor_tensor(out=tab[:, :, 0, t, :], in0=tab[:, :, 0, t, :], in1=TM[:], op=ALU.subtract))
                        V(lambda e, tab=tab, t=t, qr=qr: e.tensor_tensor(out=tab[:, :, 1, t, :], in0=Bi, in1=qr, op=ALU.mult))
                        V(lambda e, qi=qi: e.tensor_tensor(out=TM[:], in0=Br, in1=qi, op=ALU.mult))
                        V(lambda e, tab=tab, t=t: e.tensor_tensor(out=tab[:, :, 1, t, :], in0=tab[:, :, 1, t, :], in1=TM[:], op=ALU.add))
                for gl in range(16):
                    pb = gl % 2

                    def flat(tab, ri, gl=gl):
                        return tab[:, gl, ri].rearrange("p t c -> p (t c)")
                    mm(S, ps[pb][:, 0:128], [(flat(WB, 0), flat(VP, 0)), (flat(WB, 1), flat(VP, 1))], [b], [psb[pb]])
                    S.op("dve", lambda e, pb=pb, gl=gl: e.tensor_tensor(out=TZs[:, gl, :], in0=ps[pb][:, 0:128], in1=MZ[:, d, :], op=ALU.mult),
                         [psb[pb], b], [b_st])

                    def ftr(pe, pb=pb, flat=flat):
                        pe.transpose(ps[2 + pb][:, 0:64], flat(WA, 0), C.ident[0:64, 0:64])
                        return pe.transpose(ps[2 + pb][:, 64:128], flat(WA, 1), C.ident[0:64, 0:64])
                    S.op("pe", ftr, [b, C.ident_b], [psb[2 + pb]])
                    S.op("act", lambda e, pb=pb, gl=gl: e.activation(out=WSs[:, gl].rearrange("p r q -> p (r q)"), in_=ps[2 + pb][:, 0:128], func=AF.Copy),
                         [psb[2 + pb]], [b_st])
                S.op("act", lambda e: e.activation(out=VIs[:].rearrange("p g r q -> p (g r q)"), in_=VI[:].rearrange("p g r t c -> p (g r t c)"), func=AF.Copy),
                     [b], [b_st])
                S.dma("sp", C.s5_Tz[dg].rearrange("g p q -> p g q"), TZs[:], [b_st], ())
                S.dma("sp", C.s5_W[dg].rearrange("g p r q -> p g r q"), WSs[:], [b_st], ())
                S.dma("sp", C.s5_Vi[dg].rearrange("g p r q -> p g r q"), VIs[:], [b_st], ())
        S.barrier()


def s5_mixer(C, l, s):
    nc, S = C.nc, C.S
    from contextlib import ExitStack
    ps, psb = C.ps, C.ps_b
    GB = 8
    S.barrier()
    with ExitStack() as st:
        def sbt(name, shape, dt=F32):
            return st.enter_context(nc.sbuf_tensor(un(name), shape, dt))
        ZG = sbt("ZG", [128, KT, T], BF16)
        b_ZG = Buf("ZG")
        DSK = sbt("DSK", [128, KT]); BGL = sbt("BGL", [128, 16])
        b_c = Buf("s5c")
        S.dma("sp", DSK[:], C.s5_dskip, (), [b_c])
        S.dma("sp", BGL[:], C.s5_bglu, (), [b_c])
        with ExitStack() as st2:
            def sb2(name, shape, dt=F32):
                return st2.enter_context(nc.sbuf_tensor(un(name), shape, dt))
            SEL4 = sb2("SEL4", [128, 4, 8, 128], BF16)
            S.dma("pool", SEL4[:], C.c_sel4, (), [b_c])
            Ublk = sb2("Ublk", [128, GB, NCH], BF16); b_U = [Buf() for _ in range(GB)]
            Yblk = sb2("Yblk", [128, GB, NCH], BF16); b_Y = [Buf() for _ in range(GB)]
            SS = sb2("SS", [64, 2 * GB, 2, NCH + 1]); b_SS = Buf("SS")
            TZ = sb2("TZ", [128, 2 * GB, 128], BF16); WW = sb2("WW", [128, 2 * GB, 2, 64], BF16); VV = sb2("VV", [64, 2 * GB, 2, 128], BF16)
            b_P = Buf("s5par")
            T1 = sb2("T1", [64, 2 * GB, 2]); T2 = sb2("T2", [64, 2 * GB, 2]); b_T1 = Buf("s5T1"); b_T2 = Buf("s5T2")
            AB_ = sb2("ABlk", [64, 2, 2 * GB]); b_AB = Buf()
            SPB = sb2("SPB", [64, 2, 2, 2, NCH], BF16); b_SPB = [Buf(), Buf()]
            ZT_ = [sb2("zt%d" % i, [128, NCH]) for i in range(2)]; b_zt = [Buf(), Buf()]
            for blk in range(64 // GB):
                tile = blk
                for d in range(2):
                    dg = slice(d * 64 + blk * GB, d * 64 + blk * GB + GB)
                    S.dma("sp", TZ[:, d * GB:(d + 1) * GB, :], C.s5_Tz[dg].rearrange("g p q -> p g q"), (), [b_P])
                    S.dma("sp", WW[:, d * GB:(d + 1) * GB], C.s5_W[dg].rearrange("g p r q -> p g r q"), (), [b_P])
                    S.dma("sp", VV[:, d * GB:(d + 1) * GB], C.s5_Vi[dg].rearrange("g p r q -> p g r q"), (), [b_P])
                    S.op("dve", lambda e, d=d, dg=dg: e.tensor_copy(out=AB_[:, :, d * GB:(d + 1) * GB], in_=C.A8[:, :, dg]), [C.A8_b], [b_AB])
                hreads = [C.Hb[tile][c] for c in range(5)]
                for gl in range(GB):
                    b2, q = gl // 4, gl % 4
                    pb = gl % 2
                    rr = slice(64 * b2, 64 * b2 + 64)
                    mm(S, ps[pb][:, 0:NCH], [(SEL4[rr, q, j, :], C.H[rr, tile, j::8]) for j in range(8)], [b_c] + hreads, [psb[pb]])
                    S.op("act", lambda e, pb=pb, gl=gl: e.activation(out=Ublk[:, gl, :], in_=ps[pb][:, 0:NCH], func=AF.Copy), [psb[pb]], [b_U[gl]])
                S.op("pool", lambda e: e.memset(SS[:, :, :, 0:1], 0.0), (), [b_SS])
                for gl in range(GB):
                    for d in range(2):
                        ix = d * GB + gl
                        for ri in range(2):
                            pb = 2 + ((gl * 4 + d * 2 + ri) % 4)
                            S.op("pe", lambda e, pb=pb, ix=ix, ri=ri, gl=gl: e.matmul(ps[pb][0:64, 0:NCH], WW[:, ix, ri, :], Ublk[:, gl, :], start=True, stop=True),
                                 [b_P, b_U[gl]], [psb[pb]])
                            eng = "act" if ri == 0 else "dve"

                            def fcp(e, pb=pb, ix=ix, ri=ri, eng=eng, d=d):
                                if d == 0:
                                    segs = [(SS[:, ix, ri, 1:NCH + 1], ps[pb][0:64, 0:NCH])]
                                else:
                                    segs = [(SS[:, ix, ri, 1:NCC + 1], ps[pb][0:64, NCC - 1::-1]),
                                            (SS[:, ix, ri, NCC + 1:NCH + 1], ps[pb][0:64, NCH - 1:NCC - 1:-1])]
                                ins = None
                                for (o_, i_) in segs:
                                    ins = e.activation(out=o_, in_=i_, func=AF.Copy) if eng == "act" else e.tensor_copy(out=o_, in_=i_)
                                return ins
                            S.op(eng, fcp, [psb[pb]], [b_SS])
                Ar = AB_[:, 0, :].unsqueeze(2).to_broadcast([64, 2 * GB, 2])
                Ai = AB_[:, 1, :].unsqueeze(2).to_broadcast([64, 2 * GB, 2])
                for k in range(1, NCH):
                    S.op("dve", lambda e, k=k: e.tensor_tensor(out=T1[:], in0=SS[:, :, :, k], in1=Ar, op=ALU.mult), [b_SS, b_AB], [b_T1])
                    S.op("pool", lambda e, k=k: e.tensor_tensor(out=T2[:], in0=SS[:, :, :, k], in1=Ai, op=ALU.mult), [b_SS, b_AB], [b_T2])
                    S.op("dve", lambda e, k=k: e.tensor_tensor(out=SS[:, :, :, k + 1], in0=SS[:, :, :, k + 1], in1=T1[:], op=ALU.add), [b_T1, b_SS], [b_SS])
                    S.op("dve", lambda e, k=k: e.tensor_tensor(out=SS[:, :, 0, k + 1], in0=SS[:, :, 0, k + 1], in1=T2[:, :, 1], op=ALU.subtract), [b_T2, b_SS], [b_SS])
                    S.op("dve", lambda e, k=k: e.tensor_tensor(out=SS[:, :, 1, k + 1], in0=SS[:, :, 1, k + 1], in1=T2[:, :, 0], op=ALU.add), [b_T2, b_SS], [b_SS])
                for gl in range(GB):
                    pb = gl % 2
                    rot = gl % 2
                    S.op("act", lambda e, gl=gl, rot=rot: e.activation(out=SPB[:, rot, 0], in_=SS[:, gl, :, 0:NCH], func=AF.Copy), [b_SS], [b_SPB[rot]])

                    def fsp(e, gl=gl, rot=rot):
                        e.tensor_copy(out=SPB[:, rot, 1, :, 0:NCC], in_=SS[:, GB + gl, :, NCC - 1::-1])
                        return e.tensor_copy(out=SPB[:, rot, 1, :, NCC:NCH], in_=SS[:, GB + gl, :, NCH - 1:NCC - 1:-1])
                    S.op("dve", fsp, [b_SS], [b_SPB[rot]])

                    def fy(pe, pb=pb, gl=gl, rot=rot):
                        ins = None
                        for (c0, c1) in ((0, NCC), (NCC, NCH)):
                            terms = [(TZ[:, gl, :], Ublk[:, gl, c0:c1]), (TZ[:, GB + gl, :], Ublk[:, gl, c0:c1])]
                            for ri in range(2):
                                terms.append((VV[:, gl, ri, :], SPB[:, rot, 0, ri, c0:c1]))
                                terms.append((VV[:, GB + gl, ri, :], SPB[:, rot, 1, ri, c0:c1]))
                            for i, (l_, r_) in enumerate(terms):
                                ins = pe.matmul(ps[pb][:, c0:c1], l_, r_, start=(i == 0), stop=(i == len(terms) - 1))
                        return ins
                    S.op("pe", fy, [b_P, b_U[gl], b_SPB[rot]], [psb[pb]])
                    S.op("act", lambda e, pb=pb, gl=gl: e.activation(out=Yblk[:, gl, :], in_=ps[pb][:, 0:NCH], func=AF.Copy), [psb[pb]], [b_Y[gl]])
                for t in range(8):
                    b2, q = t // 4, t % 4
                    rr = slice(64 * b2, 64 * b2 + 64)
                    pb = 4 + (t % 2)
                    mm(S, ps[pb][:, 0:NCH], [(SEL4[rr, q, g8, :], Yblk[rr, g8, :]) for g8 in range(8)], [b_c] + b_Y, [psb[pb]])
                    z, bz = ZT_[t % 2], b_zt[t % 2]
                    S.op("dve", lambda e, z=z, tile=tile, t=t, pb=pb: e.scalar_tensor_tensor(
                        out=z[:], in0=C.H[:, tile, t::8], scalar=DSK[:, tile:tile + 1], in1=ps[pb][:, 0:NCH], op0=ALU.mult, op1=ALU.add),
                        [psb[pb], b_c] + hreads, [bz])
                    gelu_tanh(C, ZG[:, tile, t::8], z, bz, [b_ZG])
            S.barrier()
        WG = sbt("WG", [128, 16, KT, 128], BF16); b_WG = Buf()
        SGT = [sbt("sgt%d" % i, [128, 512]) for i in range(3)]; b_sg = [Buf(), Buf(), Buf()]
        for f in range(16):
            S.dma("pool", WG[:, f], C.s5_wglu[f], (), [b_WG])
        it = 0
        for c, (t0, n, is_ctx) in enumerate(CHUNKS):
            for o in range(KT):
                pv, pg = (it % 3) * 2, (it % 3) * 2 + 1
                sg, bsg = SGT[it % 3], b_sg[it % 3]
                it += 1
                mm(S, ps[pv][:, :n], [(WG[:, o, k, :], ZG[:, k, t0:t0 + n]) for k in range(KT)], [b_WG, b_ZG], [psb[pv]])
                mm(S, ps[pg][:, :n], [(WG[:, 8 + o, k, :], ZG[:, k, t0:t0 + n]) for k in range(KT)], [b_WG, b_ZG], [psb[pg]])
                S.op("act", lambda e, pg=pg, sg=sg, n=n, o=o: e.activation(out=sg[:, :n], in_=ps[pg][:, :n], func=AF.Sigmoid, bias=BGL[:, 8 + o:9 + o]), [psb[pg], b_c], [bsg])
                S.op("dve", lambda e, pv=pv, sg=sg, n=n, o=o: e.scalar_tensor_tensor(
                    out=sg[:, :n], in0=ps[pv][:, :n], scalar=BGL[:, o:o + 1], in1=sg[:, :n], op0=ALU.add, op1=ALU.mult), [psb[pv], bsg, b_c], [bsg])
                S.dma("sp", C.MOscr[o, :, t0:t0 + n], sg[:, :n], [bsg], ())
        S.barrier()


def apply_mixer_out(C, l, s):
    nc, S = C.nc, C.S
    S.barrier()
    with (nc.sbuf_tensor(un("mo0"), [128, 512], F32) as mo0, nc.sbuf_tensor(un("mo1"), [128, 512], F32) as mo1,
          nc.sbuf_tensor(un("mo2"), [128, 512], F32) as mo2):
        mo, bmo = [mo0, mo1, mo2], [Buf(), Buf(), Buf()]
        it = 0
        for c, (t0, n, is_ctx) in enumerate(CHUNKS):
            col = NSEQ if is_ctx else s
            for o in range(KT):
                m_, bm = mo[it % 3], bmo[it % 3]
                it += 1
                S.dma("sp" if it % 2 == 0 else "act", m_[:, :n], C.MOscr[o, :, t0:t0 + n], (), [bm])
                S.op("dve", lambda e, m_=m_, n=n, o=o, t0=t0, col=col: e.scalar_tensor_tensor(
                    out=C.R[:, o, t0:t0 + n], in0=m_[:, :n], scalar=C.MODS[:, l, 2 * 8 + o, col:col + 1], in1=C.R[:, o, t0:t0 + n],
                    op0=ALU.mult, op1=ALU.add), [bm, C.MODS_b, C.Rb[o][c]], [C.Rb[o][c]])
        S.barrier()


def gelu_tanh(C, out_ap, z, bz, wbufs):
    S = C.S
    tf, tb = next_tmpf(C)
    n = z.shape[1]
    S.op("dve", lambda e: e.tensor_tensor(out=tf[:, :n], in0=z[:], in1=z[:], op=ALU.mult), [bz], [tb])
    S.op("dve", lambda e: e.tensor_scalar(out=tf[:, :n], in0=tf[:, :n], scalar1=0.044715, scalar2=1.0, op0=ALU.mult, op1=ALU.add), [tb], [tb])
    S.op("dve", lambda e: e.tensor_tensor(out=tf[:, :n], in0=tf[:, :n], in1=z[:], op=ALU.mult), [tb, bz], [tb])
    S.op("act", lambda e: e.activation(out=tf[:, :n], in_=tf[:, :n], func=AF.Tanh, scale=0.7978845608028654), [tb], [tb])
    S.op("dve", lambda e: e.tensor_scalar(out=tf[:, :n], in0=tf[:, :n], scalar1=1.0, scalar2=0.5, op0=ALU.add, op1=ALU.mult), [tb], [tb])
    S.op("dve", lambda e: e.tensor_tensor(out=out_ap, in0=tf[:, :n], in1=z[:], op=ALU.mult), [tb, bz], wbufs)


RW_NV = 9


def rwkv_declare(C):
    nc = C.nc

    def din(name, shape, dt=F32):
        return nc.dram_tensor(name, list(shape), dt, kind="ExternalInput").ap()
    C.rw_mu = din("rw_mu", [128, 6, KT])
    C.rw_wrkv = din("rw_wrkv_t", [3, KT, 128, KT, 128])
    C.rw_l1 = din("rw_l1_t", [3, 128, KT, 128])
    C.rw_l2 = din("rw_l2", [3, 128, D])
    C.rw_vecs = din("rw_vecs", [128, RW_NV, KT])
    C.rw_wout = din("rw_wout_t", [KT, 128, KT, 128])
    C.c_mask4 = din("c_mask4", [2, 128, 512])


def rwkv_mixer(C, l, s):
    nc, S = C.nc, C.S
    from contextlib import ExitStack
    ps, psb = C.ps, C.ps_b
    S.barrier()
    with ExitStack() as st:
        def sbt(name, shape, dt=F32):
            return st.enter_context(nc.sbuf_tensor(un(name), shape, dt))
        SH = sbt("SH", [128, KT, T], BF16); b_SH = Buf("SH")
        MU = sbt("MU", [128, 2, 6, KT]); VEC = sbt("VEC", [128, RW_NV, KT]); b_par = Buf("rwpar")
        L2 = sbt("L2", [128, 3, D], BF16)
        LH = sbt("LH", [128, 3, T], BF16); b_LH = Buf("LH")
        BD = sbt("BD", [128, 128]); MASK4 = sbt("MASK4", [128, 2, 512], BF16)
        S.dma("sp", MU[:, 0], C.rw_mu, (), [b_par])
        S.dma("sp", VEC[:], C.rw_vecs, (), [b_par])
        S.dma("pool", MASK4[:, 0, :], C.c_mask4[0], (), [b_par])
        S.dma("pool", MASK4[:, 1, :], C.c_mask4[1], (), [b_par])
        S.dma("pool", L2[:], C.rw_l2.rearrange("a p n -> p a n"), (), [b_par])
        S.op("dve", lambda e: e.tensor_scalar(out=MU[:, 1], in0=MU[:, 0], scalar1=-1.0, scalar2=1.0, op0=ALU.mult, op1=ALU.add), [b_par], [b_par])
        S.op("pool", lambda e: e.memset(BD[:], 0.0), (), [b_par])
        S.op("pool", lambda e: e.memset(BD[0:64, 0:64], 1.0), [b_par], [b_par])
        S.op("pool", lambda e: e.memset(BD[64:128, 64:128], 1.0), [b_par], [b_par])
        S.op("pool", lambda e: e.memset(SH[:], 0.0), (), [b_SH])
        allH = [C.Hb[k][c] for k in range(KT) for c in range(5)]
        for k in range(KT):
            eng = ("dve", "pool", "act")[k % 3]

            def cp(e, o_, i_, eng=eng):
                return e.activation(out=o_, in_=i_, func=AF.Copy) if eng == "act" else e.tensor_copy(out=o_, in_=i_)
            xs = slice(T_CTX, T)
            if k < 4:
                S.op(eng, lambda e, k=k, cp=cp: cp(e, SH[:, k, 1:T_CTX], C.H[:, k, 0:T_CTX - 1]), allH, [b_SH])
            else:
                S.op(eng, lambda e, k=k, cp=cp: cp(e, SH[:, k, 0:T_CTX - 1], C.H[:, k, 1:T_CTX]), allH, [b_SH])
            hx = C.H[:, k, xs].rearrange("p (r c) -> p r c", c=64)
            sx = SH[:, k, xs].rearrange("p (r c) -> p r c", c=64)
            if k < 2:
                S.op(eng, lambda e, cp=cp, sx=sx, hx=hx: cp(e, sx[:, :, 1:64], hx[:, :, 0:63]), allH, [b_SH])
            elif k < 4:
                S.op(eng, lambda e, cp=cp, sx=sx, hx=hx: cp(e, sx[:, :, 0:63], hx[:, :, 1:64]), allH, [b_SH])
            elif k < 6:
                S.op(eng, lambda e, k=k, cp=cp: cp(e, SH[:, k, T_CTX + 64:T], C.H[:, k, T_CTX:T - 64]), allH, [b_SH])
            else:
                S.op(eng, lambda e, k=k, cp=cp: cp(e, SH[:, k, T_CTX:T - 64], C.H[:, k, T_CTX + 64:T]), allH, [b_SH])

        cnt = [0]
        wb = {}

        class wscope:
            def __enter__(self_):
                self_.st = ExitStack()
                wb["WF"] = [self_.st.enter_context(nc.sbuf_tensor(un("rwWF"), [128, KT, 128], F32)) for i in range(2)]
                wb["WP"] = [self_.st.enter_context(nc.sbuf_tensor(un("rwWP"), [128, 2, KT, 128], BF16)) for i in range(2)]
                wb["bWF"] = [Buf(), Buf()]
                wb["bWP"] = [Buf(), Buf()]
                return self_

            def __exit__(self_, *a):
                S.barrier()
                self_.st.close()
                return False

        def proj2(w_ap, mu_idx, out_fn):
            i = cnt[0] % 2
            cnt[0] += 1
            wf, bwf, wp, bwp = wb["WF"][i], wb["bWF"][i], wb["WP"][i], wb["bWP"][i]
            S.dma("sp" if i == 0 else "act", wf[:], w_ap, (), [bwf])
            S.op("dve", lambda e: e.tensor_tensor(out=wp[:, 0], in0=wf[:], in1=MU[:, 1, mu_idx, :].unsqueeze(2).to_broadcast([128, KT, 128]), op=ALU.mult), [bwf, b_par], [bwp])
            S.op("pool", lambda e: e.tensor_tensor(out=wp[:, 1], in0=wf[:], in1=MU[:, 0, mu_idx, :].unsqueeze(2).to_broadcast([128, KT, 128]), op=ALU.mult), [bwf, b_par], [bwp])
            for c, (t0, n, _) in enumerate(CHUNKS):
                pb = c % 2
                pairs = [(wp[:, 0, k, :], C.H[:, k, t0:t0 + n]) for k in range(KT)] + [(wp[:, 1, k, :], SH[:, k, t0:t0 + n]) for k in range(KT)]
                mm(S, ps[pb][:, :n], pairs, [bwp, b_SH] + [C.Hb[k][c] for k in range(KT)], [psb[pb]])
                out_fn(c, t0, n, ps[pb][:, :n], psb[pb])
        ws_ = wscope()
        ws_.__enter__()
        proj2(C.rw_l1[0], 1, lambda c, t0, n, p_, pb_: S.op("act", lambda e: e.activation(out=LH[:, 0, t0:t0 + n], in_=p_, func=AF.Tanh), [pb_], [b_LH]))
        proj2(C.rw_l1[1], 4, lambda c, t0, n, p_, pb_: S.op("act", lambda e: e.activation(out=LH[:, 1, t0:t0 + n], in_=p_, func=AF.Copy), [pb_], [b_LH]))
        proj2(C.rw_l1[2], 5, lambda c, t0, n, p_, pb_: S.op("act", lambda e: e.activation(out=LH[:, 2, t0:t0 + n], in_=p_, func=AF.Sigmoid), [pb_], [b_LH]))

        ws_.__exit__(None, None, None)
        for o in range(KT if not C.stop else 1):
            rwkv_tile(C, l, s, o, SH, b_SH, MU, VEC, b_par, L2, LH, b_LH, BD, MASK4, proj2, wscope)
        S.barrier()


def rwkv_tile(C, l, s, o, SH, b_SH, MU, VEC, b_par, L2, LH, b_LH, BD, MASK4, proj2, wscope):
    nc, S = C.nc, C.S
    from contextlib import ExitStack
    ps, psb = C.ps, C.ps_b
    ocol = slice(o * 128, (o + 1) * 128)
    with ExitStack() as st:
        def sbt(name, shape, dt=F32):
            return st.enter_context(nc.sbuf_tensor(un(name), shape, dt))
        Rr = sbt("Rr", [128, T], BF16); Kk = sbt("Kk", [128, T], BF16); Vv = sbt("Vv", [128, T], BF16)
        KK = sbt("KK", [128, T], BF16); Gg = sbt("Gg", [128, T], BF16); BON = sbt("BON", [128, T], BF16)
        Vtok = sbt("rVtok", [128, 18, 128], BF16)
        OACC = sbt("rOACC", [128, 18, 128]); b_OACC = [Buf() for _ in range(18)]
        b_R, b_K, b_V, b_KK, b_G, b_BON, b_Vtok = [Buf() for _ in range(7)]
        ws_ = wscope()
        ws_.__enter__()
        proj2(C.rw_wrkv[0, o], 0, lambda c, t0, n, p_, pb_: S.op("act", lambda e: e.activation(out=Rr[:, t0:t0 + n], in_=p_, func=AF.Copy), [pb_], [b_R]))
        proj2(C.rw_wrkv[1, o], 2, lambda c, t0, n, p_, pb_: S.op("act", lambda e: e.activation(out=Kk[:, t0:t0 + n], in_=p_, func=AF.Copy), [pb_], [b_K]))
        proj2(C.rw_wrkv[2, o], 3, lambda c, t0, n, p_, pb_: S.op("act", lambda e: e.activation(out=Vv[:, t0:t0 + n], in_=p_, func=AF.Copy), [pb_], [b_V]))
        ws_.__exit__(None, None, None)
        S.op("pool", lambda e: e.memset(BON[:], 0.0), (), [b_BON])
        for c, (t0, n, _) in enumerate(CHUNKS):
            pb = 2 + c % 2
            S.op("pe", lambda e, pb=pb, t0=t0, n=n: e.matmul(ps[pb][:, :n], L2[:, 2, ocol], LH[:, 2, t0:t0 + n], start=True, stop=True), [b_par, b_LH], [psb[pb]])
            S.op("act", lambda e, pb=pb, t0=t0, n=n: e.activation(out=Gg[:, t0:t0 + n], in_=ps[pb][:, :n], func=AF.Copy), [psb[pb]], [b_G])
            tf, tb = next_tmpf(C)
            tg, tgb = next_tmpf(C)
            S.op("dve", lambda e, tf=tf, t0=t0, n=n: e.tensor_scalar(out=tf[:, :n], in0=Kk[:, t0:t0 + n], scalar1=VEC[:, 4, o:o + 1], scalar2=None, op0=ALU.mult), [b_K, b_par], [tb])
            S.op("act", lambda e, tf=tf, tg=tg, n=n: e.activation(out=tg[:, :n], in_=tf[:, :n], func=AF.Square), [tb], [tgb])
            S.op("pe", lambda e, tg=tg, n=n: e.matmul(ps[7][:, :n], BD[:], tg[:, :n], start=True, stop=True), [tgb, b_par], [psb[7]])
            S.op("act", lambda e, n=n: e.activation(out=C.rs[:, :n], in_=ps[7][:, :n], func=AF.Sqrt, bias=1e-6), [psb[7]], [C.rs_b])
            S.op("dve", lambda e, n=n: e.reciprocal(out=C.rs[:, :n], in_=C.rs[:, :n]), [C.rs_b], [C.rs_b])
            S.op("dve", lambda e, tf=tf, t0=t0, n=n: e.tensor_tensor(out=KK[:, t0:t0 + n], in0=tf[:, :n], in1=C.rs[:, :n], op=ALU.mult), [tb, C.rs_b], [b_KK])
        for g6 in range(0, 18, 6):
            pb = 2 + ((g6 // 6) % 2)

            def ftr(pe, g6=g6, pb=pb):
                ins = None
                for i in range(6):
                    ins = pe.transpose(C.psbf[pb][:, i * 128:(i + 1) * 128], Vv[:, (g6 + i) * 128:(g6 + i + 1) * 128], C.identb[:])
                return ins
            S.op("pe", ftr, [b_V, C.ident_b], [psb[pb]])
            S.op("dve", lambda e, g6=g6, pb=pb: e.tensor_copy(out=Vtok[:, g6:g6 + 6, :], in_=C.psbf[pb][:, 0:768].rearrange("p (a b) -> p a b", a=6)), [psb[pb]], [b_Vtok])

        for d in range(2):
            with ExitStack() as sd:
                def sbd(name, shape, dt=F32):
                    return sd.enter_context(nc.sbuf_tensor(un(name), shape, dt))
                LC = sbd("LC", [128, T]); KD = sbd("KD", [128, T], BF16); AK = sbd("AK", [128, T], BF16)
                b_LC, b_LW, b_KD, b_AK = Buf(), Buf(), Buf(), Buf()
                lwg = nc.sbuf_tensor(un("LW"), [128, T], F32)
                LW = lwg.__enter__()
                hs = slice(d * 64, d * 64 + 64)
                for c, (t0, n, _) in enumerate(CHUNKS):
                    pw, pa = 2 + c % 2, 4 + c % 2
                    S.op("pe", lambda e, pw=pw, t0=t0, n=n: e.matmul(ps[pw][:, :n], L2[hs, 0, ocol], LH[hs, 0, t0:t0 + n], start=True, stop=True), [b_par, b_LH], [psb[pw]])
                    S.op("act", lambda e, pw=pw, t0=t0, n=n: e.activation(out=LW[:, t0:t0 + n], in_=ps[pw][:, :n], func=AF.Sigmoid, bias=VEC[:, 0 + d, o:o + 1]), [psb[pw], b_par], [b_LW])
                    tf, tb = next_tmpf(C)
                    S.op("pe", lambda e, pa=pa, t0=t0, n=n: e.matmul(ps[pa][:, :n], L2[hs, 1, ocol], LH[hs, 1, t0:t0 + n], start=True, stop=True), [b_par, b_LH], [psb[pa]])
                    S.op("act", lambda e, pa=pa, tf=tf, n=n: e.activation(out=tf[:, :n], in_=ps[pa][:, :n], func=AF.Sigmoid, bias=VEC[:, 2 + d, o:o + 1]), [psb[pa], b_par], [tb])
                    S.op("dve", lambda e, tf=tf, t0=t0, n=n: e.tensor_tensor(out=AK[:, t0:t0 + n], in0=tf[:, :n], in1=KK[:, t0:t0 + n], op=ALU.mult), [tb, b_KK], [b_AK])
                    S.op("dve", lambda e, tf=tf, n=n: e.tensor_scalar(out=tf[:, :n], in0=tf[:, :n], scalar1=-1.0, scalar2=VEC[:, 5, o:o + 1], op0=ALU.add, op1=ALU.mult), [tb, b_par], [tb])
                    S.op("dve", lambda e, tf=tf, t0=t0, n=n: e.scalar_tensor_tensor(out=KD[:, t0:t0 + n], in0=tf[:, :n], scalar=1.0, in1=Kk[:, t0:t0 + n], op0=ALU.add, op1=ALU.mult), [tb, b_K], [b_KD])
                    tg, tgb = next_tmpf(C)
                    S.op("dve", lambda e, tg=tg, t0=t0, n=n: e.scalar_tensor_tensor(out=tg[:, :n], in0=KD[:, t0:t0 + n], scalar=VEC[:, 6, o:o + 1], in1=Rr[:, t0:t0 + n], op0=ALU.mult, op1=ALU.mult), [b_KD, b_R, b_par], [tgb])
                    S.op("pe", lambda e, tg=tg, n=n: e.matmul(ps[6][:, :n], BD[:], tg[:, :n], start=True, stop=True), [tgb, b_par], [psb[6]])
                    S.op("dve", lambda e, tg=tg, t0=t0, n=n: e.tensor_tensor(out=tg[:, :n], in0=ps[6][:, :n], in1=Vv[:, t0:t0 + n], op=ALU.mult), [psb[6], b_V], [tgb])
                    S.op("pool", lambda e, tg=tg, t0=t0, n=n: e.tensor_tensor(out=BON[:, t0:t0 + n], in0=BON[:, t0:t0 + n], in1=tg[:, :n], op=ALU.add), [tgb, b_BON], [b_BON])
                S.op("dve", lambda e: e.tensor_scalar(out=LW[:], in0=LW[:], scalar1=-float(np.exp(-0.5)), scalar2=None, op0=ALU.mult), [b_LW], [b_LW])
                for n_ in range(T // 64):
                    cs = slice(n_ * 64, (n_ + 1) * 64)
                    if d == 0:
                        S.op("dve", lambda e, cs=cs: e.tensor_tensor_scan(out=LC[:, cs], data0=C.ones[:, 0:64], data1=LW[:, cs], initial=0.0, op0=ALU.mult, op1=ALU.add), [b_LW, C.ones_b], [b_LC])
                    else:
                        rs_ = slice(n_ * 64 + 63, (n_ * 64 - 1) if n_ > 0 else None, -1)
                        S.op("dve", lambda e, rs_=rs_: e.tensor_tensor_scan(out=LC[:, rs_], data0=C.ones[:, 0:64], data1=LW[:, rs_], initial=0.0, op0=ALU.mult, op1=ALU.add), [b_LW, C.ones_b], [b_LC])
                S.barrier()
                lwg.__exit__(None, None, None)
                LW = None
                if C.stop == 'RP':
                    continue
                rwkv_recurrence(C, o, d, Rr, KK, KD, AK, LC, LW, Vtok, OACC, b_OACC, MASK4,
                                [b_R, b_KK, b_KD, b_AK, b_LC, b_LW, b_Vtok, b_par], Gg, b_G, BON, b_BON, VEC)
                S.barrier()


def rwkv_recurrence(C, o, d, Rr, KK, KD, AK, LC, LW, Vtok, OACC, b_OACC, MASK4, inb, Gg, b_G, BON, b_BON, VEC):
    nc, S = C.nc, C.S
    from contextlib import ExitStack
    ps, psb = C.ps, C.ps_b
    b_R, b_KK, b_KD, b_AK, b_LC, b_LW, b_Vtok, b_par = inb
    with ExitStack() as st:
        def sbt(name, shape, dt=F32):
            return st.enter_context(nc.sbuf_tensor(un(name), shape, dt))
        Hf = sbt("rHf", [128, 64]); HbP = [sbt("rHb%d" % h, [128, 64], BF16) for h in range(2)]; b_H = [Buf(), Buf()]
        NR = 2

        def rot(name, shape, dt, nh=1, nr=NR):
            return [[sbt("%s%d_%d" % (name, h, i), shape, dt) for i in range(nr)] for h in range(nh)], [[Buf() for i in range(nr)] for h in range(nh)]
        EX, b_EX = rot("rEX", [128, 4, 128], F32)
        FM, b_FM = rot("rFM", [128, 6, 128], BF16)
        KAt, b_KAt = rot("rKAt", [128, 2, 128], BF16)
        DD, b_DD = rot("rDD", [128, 512], BF16, 2)
        UT, b_UT = rot("rUT", [128, 128], F32, 2, 1)
        PP, b_PP = rot("rPP", [128, 2, 256], F32, 2, 1)
        XX, b_XX = rot("rXX", [128, 2, 128], F32, 2, 1)
        U32, b_U32 = rot("rU32", [128, 128], F32, 2, 1)
        TTB, b_TTB = rot("rTTB", [128, 128], BF16, 2)
        X1, b_X1 = rot("rX1", [128, 64], BF16, 2)
        NE, b_NE = rot("rNE", [128, 64], BF16, 2)
        FIN, b_FIN = rot("rFIN", [128, 128], F32)
        FNb, b_FNb = rot("rFNb", [128, 128], BF16)
        YT, b_YT = rot("rYT", [128, 128], BF16)
        ST, b_ST = rot("rST", [128, 2, 8], F32)
        S.op("pool", lambda e: e.memset(Hf[:], 0.0), (), b_H)
        for h_ in range(2):
            S.op("pool", lambda e, h_=h_: e.memset(HbP[h_][:], 0.0), (), [b_H[h_]])
        order = list(range(18)) if d == 0 else [1, 0] + list(range(17, 1, -1))
        if C.stop and C.stop.startswith('T'):
            order = order[:int(C.stop[1:])]
        def tileprep(i):
            tl = order[i]
            tt = slice(tl * 128, (tl + 1) * 128)
            ri = i % NR
            ex, bex = EX[0][ri], b_EX[0][ri]
            fm, bfm = FM[0][ri], b_FM[0][ri]
            LC3 = LC[:, tt].rearrange("p (a c) -> p a c", a=2)
            endc = 63 if d == 0 else 0
            ex03 = ex[:, 0, :].rearrange("p (a c) -> p a c", a=2)
            if d == 0:
                S.op("dve", lambda e, ex03=ex03, LC3=LC3: e.tensor_copy(out=ex03[:, :, 1:64], in_=LC3[:, :, 0:63]), [b_LC], [bex])
                S.op("pool", lambda e, ex03=ex03: e.memset(ex03[:, :, 0:1], 0.0), (), [bex])
            else:
                S.op("dve", lambda e, ex03=ex03, LC3=LC3: e.tensor_copy(out=ex03[:, :, 0:63], in_=LC3[:, :, 1:64]), [b_LC], [bex])
                S.op("pool", lambda e, ex03=ex03: e.memset(ex03[:, :, 63:64], 0.0), (), [bex])
            S.op("pool", lambda e, ex=ex, tt=tt: e.tensor_copy(out=ex[:, 1, :], in_=LC[:, tt]), [b_LC], [bex])
            S.op("pool", lambda e, ex=ex, tt=tt: e.tensor_scalar(out=ex[:, 2, :], in0=LC[:, tt], scalar1=-1.0, scalar2=None, op0=ALU.mult), [b_LC], [bex])
            S.op("dve", lambda e, ex=ex, LC3=LC3: e.tensor_tensor(out=ex[:, 3, :].rearrange("p (a c) -> p a c", a=2), in0=LC3[:, :, endc:endc + 1].to_broadcast([128, 2, 64]), in1=LC3, op=ALU.subtract), [b_LC], [bex])
            S.op("act", lambda e, ex=ex: e.activation(out=ex[:], in_=ex[:], func=AF.Exp), [bex], [bex])
            S.op("dve", lambda e, fm=fm, ex=ex, tt=tt: e.tensor_tensor(out=fm[:, 0, :], in0=KK[:, tt], in1=ex[:, 0, :], op=ALU.mult), [b_KK, bex], [bfm])
            S.op("pool", lambda e, fm=fm, ex=ex, tt=tt: e.tensor_tensor(out=fm[:, 1, :], in0=Rr[:, tt], in1=ex[:, 1, :], op=ALU.mult), [b_R, bex], [bfm])
            S.op("dve", lambda e, fm=fm, ex=ex, tt=tt: e.tensor_tensor(out=fm[:, 2, :], in0=KD[:, tt], in1=ex[:, 2, :], op=ALU.mult), [b_KD, bex], [bfm])
            S.op("pool", lambda e, fm=fm, ex=ex, tt=tt: e.tensor_tensor(out=fm[:, 3, :], in0=AK[:, tt], in1=ex[:, 2, :], op=ALU.mult), [b_AK, bex], [bfm])
            S.op("dve", lambda e, fm=fm, ex=ex, tt=tt: e.tensor_tensor(out=fm[:, 4, :], in0=KD[:, tt], in1=ex[:, 3, :], op=ALU.mult), [b_KD, bex], [bfm])
            S.op("pool", lambda e, fm=fm, ex=ex, tt=tt: e.tensor_tensor(out=fm[:, 5, :], in0=AK[:, tt], in1=ex[:, 3, :], op=ALU.mult), [b_AK, bex], [bfm])
            kat, bkat = KAt[0][ri], b_KAt[0][ri]

            def ftk(pe, fm=fm):
                pe.transpose(C.psbf[3][:, 0:128], fm[:, 4, :], C.identb[:])
                return pe.transpose(C.psbf[3][:, 128:256], fm[:, 5, :], C.identb[:])
            S.op("pe", ftk, [bfm, C.ident_b], [psb[3]])
            S.op("act", lambda e, kat=kat: e.activation(out=kat[:].rearrange("p a b -> p (a b)"), in_=C.psbf[3][:, 0:256], func=AF.Copy), [psb[3]], [bkat])
            return dict(tl=tl, tt=tt, ri=ri, ex=ex, bex=bex, fm=fm, bfm=bfm, kat=kat, bkat=bkat)

        if True:
            def prepgen(h, P):
                tl, tt, ri, ex, bex, fm, bfm, kat, bkat = [P[k_] for k_ in ("tl", "tt", "ri", "ex", "bex", "fm", "bfm", "kat", "bkat")]
                hp = slice(h * 64, h * 64 + 64)
                hc = hp
                dd, bdd = DD[h][ri], b_DD[h][ri]
                pG = h

                def fgm(pe, fm=fm, pG=pG, hp=hp):
                    _m0 = pe.matmul(ps[pG][:, 0:128], fm[hp, 3, :], fm[hp, 0, :], start=True, stop=True)
                    _m1 = pe.matmul(ps[pG][:, 128:256], fm[hp, 2, :], fm[hp, 0, :], start=True, stop=True)
                    _m2 = pe.matmul(ps[pG][:, 256:384], fm[hp, 2, :], fm[hp, 1, :], start=True, stop=True)
                    _m3 = pe.matmul(ps[pG][:, 384:512], fm[hp, 3, :], fm[hp, 1, :], start=True, stop=True)
                    return (_m0, _m3)
                S.op("pe", fgm, [bfm], [psb[pG]])
                yield
                S.op("dve", lambda e, dd=dd, pG=pG: e.tensor_tensor(out=dd[:], in0=ps[pG][:, 0:512], in1=MASK4[:, d, :], op=ALU.mult), [psb[pG], b_par], [bdd])
                u32, bu32 = U32[h][0], b_U32[h][0]
                S.op("dve", lambda e, u32=u32, pG=pG: e.tensor_tensor(out=u32[:], in0=ps[pG][:, 0:128], in1=MASK4[:, d, 0:128], op=ALU.mult), [psb[pG], b_par], [bu32])
                ut, but = UT[h][0], b_UT[h][0]
                pS = 4 + h
                xx, bxx = XX[h][0], b_XX[h][0]
                pp, bpp = PP[h][0], b_PP[h][0]
                ttb, bttb = TTB[h][ri], b_TTB[h][ri]
                yield
                yield from solve_fp32(C, u32[:], bu32, pS, ut, but, pp, bpp, xx, bxx, ttb, bttb)

            def stategen(h, P):
                tl, tt, ri, ex, bex, fm, bfm, kat, bkat = [P[k_] for k_ in ("tl", "tt", "ri", "ex", "bex", "fm", "bfm", "kat", "bkat")]
                hp = slice(h * 64, h * 64 + 64)
                hc = hp
                dd, bdd = DD[h][ri], b_DD[h][ri]
                ttb, bttb = TTB[h][ri], b_TTB[h][ri]
                bxx = bttb
                TT = ttb[:]
                x1, bx1 = X1[h][ri], b_X1[h][ri]
                ne, bne = NE[h][ri], b_NE[h][ri]
                pX, pO, pH = 6, 2, 7
                for cb in ((0, 1) if d == 0 else (1, 0)):
                    pp_ = slice(cb * 64, cb * 64 + 64)
                    cc = pp_
                    pccol = cb * 64 + (63 if d == 0 else 0)

                    def fx1(pe, pp_=pp_, cc=cc, fm=fm, dd=dd, hp=hp, hc=hc, tl=tl, h=h):
                        _m0 = pe.matmul(ps[pX][pp_, h * 128:h * 128 + 64], fm[:, 0, cc], HbP[h][:, :], start=True, stop=False)
                        _m1 = pe.matmul(ps[pX][pp_, h * 128:h * 128 + 64], dd[pp_, 128 + cc.start:128 + cc.start + 64], Vtok[pp_, tl, hc], start=False, stop=True)
                        return (_m0, _m1)
                    S.op("pe", fx1, [bfm, bdd, b_H[h], b_Vtok], [psb[pX]])
                    yield
                    S.op("act", lambda e, pp_=pp_, x1=x1: e.activation(out=x1[pp_, :], in_=ps[pX][pp_, h * 128:h * 128 + 64], func=AF.Copy), [psb[pX]], [bx1])
                    yield
                    S.op("pe", lambda e, pp_=pp_, cc=cc, TT=TT, x1=x1: e.matmul(ps[pX][pp_, h * 128 + 64:h * 128 + 128], TT[pp_, cc], x1[pp_, :], start=True, stop=True), [bxx, bx1], [psb[pX]])
                    yield
                    S.op("dve", lambda e, pp_=pp_, ne=ne: e.tensor_scalar(out=ne[pp_, :], in0=ps[pX][pp_, h * 128 + 64:h * 128 + 128], scalar1=-1.0, scalar2=None, op0=ALU.mult), [psb[pX]], [bne])
                    yield

                    def fo(pe, pp_=pp_, cc=cc, fm=fm, dd=dd, ne=ne, hp=hp, hc=hc, tl=tl, h=h):
                        _m0 = pe.matmul(ps[pO][pp_, hc], fm[:, 1, cc], HbP[h][:, :], start=True, stop=False)
                        _m1 = pe.matmul(ps[pO][pp_, hc], dd[pp_, 256 + cc.start:256 + cc.start + 64], Vtok[pp_, tl, hc], start=False, stop=False)
                        _m2 = pe.matmul(ps[pO][pp_, hc], dd[pp_, 384 + cc.start:384 + cc.start + 64], ne[pp_, :], start=False, stop=True)
                        return (_m0, _m2)
                    S.op("pe", fo, [bfm, bdd, bne, b_H[h], b_Vtok], [psb[pO]])

                    def fh(pe, pp_=pp_, kat=kat, ne=ne, hp=hp, hc=hc, tl=tl):
                        _m0 = pe.matmul(ps[pH][hp, 0:64], kat[pp_, 0, hc], Vtok[pp_, tl, hc], start=True, stop=False)
                        _m1 = pe.matmul(ps[pH][hp, 0:64], kat[pp_, 1, hc], ne[pp_, :], start=False, stop=True)
                        return (_m0, _m1)
                    S.op("pe", fh, [bkat, bne, b_Vtok], [psb[pH]])
                    yield
                    S.op("dve", lambda e, hp=hp, ex=ex, pccol=pccol: e.scalar_tensor_tensor(
                        out=Hf[hp, :], in0=Hf[hp, :], scalar=ex[hp, 1, pccol:pccol + 1], in1=ps[pH][hp, 0:64], op0=ALU.mult, op1=ALU.add),
                        [psb[pH], bex, b_H[h]], [b_H[h]])
                    yield
                    S.op("act", lambda e, hp=hp, h=h: e.activation(out=HbP[h][hp, :], in_=Hf[hp, :], func=AF.Copy), [b_H[h]], [b_H[h]])
        def outputs(P):
            tl, tt, ri = P["tl"], P["tt"], P["ri"]
            pO = 2
            if d == 0:
                S.op("act", lambda e, tl=tl: e.activation(out=OACC[:, tl, :], in_=ps[pO][:, 0:128], func=AF.Copy), [psb[pO]], [b_OACC[tl]])
            else:
                fin, bfin = FIN[0][ri], b_FIN[0][ri]
                fnb, bfnb = FNb[0][ri], b_FNb[0][ri]
                yt, byt = YT[0][ri], b_YT[0][ri]
                stt, bst = ST[0][ri], b_ST[0][ri]
                S.op("dve", lambda e, fin=fin, tl=tl: e.tensor_tensor(out=fin[:], in0=ps[pO][:, 0:128], in1=OACC[:, tl, :], op=ALU.add), [psb[pO], b_OACC[tl]], [bfin])
                for h in range(2):
                    hc = slice(h * 64, h * 64 + 64)
                    tf, tb = next_tmpf(C)
                    S.op("act", lambda e, tf=tf, fin=fin, hc=hc, stt=stt, h=h: e.activation(out=tf[:, 0:64], in_=fin[:, hc], func=AF.Identity, accum_out=stt[:, h, 0:1]), [bfin], [tb, bst])
                    S.op("act", lambda e, tf=tf, fin=fin, hc=hc, stt=stt, h=h: e.activation(out=tf[:, 64:128], in_=fin[:, hc], func=AF.Square, accum_out=stt[:, h, 1:2]), [bfin], [tb, bst])
                S.op("dve", lambda e, stt=stt: e.tensor_scalar(out=stt[:, :, 2:3], in0=stt[:, :, 0:1], scalar1=1.0 / 64, scalar2=None, op0=ALU.mult), [bst], [bst])
                S.op("dve", lambda e, stt=stt: e.tensor_tensor(out=stt[:, :, 3:4], in0=stt[:, :, 2:3], in1=stt[:, :, 2:3], op=ALU.mult), [bst], [bst])
                S.op("dve", lambda e, stt=stt: e.scalar_tensor_tensor(out=stt[:, :, 4:5], in0=stt[:, :, 1:2], scalar=1.0 / 64, in1=stt[:, :, 3:4], op0=ALU.mult, op1=ALU.subtract), [bst], [bst])
                S.op("act", lambda e, stt=stt: e.activation(out=stt[:, :, 5:6], in_=stt[:, :, 4:5], func=AF.Sqrt, bias=64e-5), [bst], [bst])
                S.op("dve", lambda e, stt=stt: e.reciprocal(out=stt[:, :, 6:7], in_=stt[:, :, 5:6]), [bst], [bst])
                for h in range(2):
                    hc = slice(h * 64, h * 64 + 64)
                    S.op("dve", lambda e, fnb=fnb, fin=fin, hc=hc, stt=stt, h=h: e.tensor_scalar(out=fnb[:, hc], in0=fin[:, hc], scalar1=stt[:, h, 2:3], scalar2=stt[:, h, 6:7], op0=ALU.subtract, op1=ALU.mult), [bfin, bst], [bfnb])
                S.op("pe", lambda e, fnb=fnb: e.transpose(C.psbf[3][:, 512:640], fnb[:], C.identb[:]), [bfnb, C.ident_b], [psb[3]])
                tf, tb = next_tmpf(C)
                S.op("dve", lambda e, tf=tf: e.tensor_scalar(out=tf[:, 0:128], in0=C.psbf[3][:, 512:640], scalar1=VEC[:, 7, o:o + 1], scalar2=VEC[:, 8, o:o + 1], op0=ALU.mult, op1=ALU.add), [psb[3], b_par], [tb])
                S.op("pool", lambda e, tf=tf, tt=tt: e.tensor_tensor(out=tf[:, 0:128], in0=tf[:, 0:128], in1=BON[:, tt], op=ALU.add), [tb, b_BON], [tb])
                S.op("dve", lambda e, tf=tf, yt=yt, tt=tt: e.tensor_tensor(out=yt[:], in0=tf[:, 0:128], in1=Gg[:, tt], op=ALU.mult), [tb, b_G], [byt])
                S.dma("sp", C.Yscr[o, :, tt], yt[:], [byt], [C.Yscr_b])

        nt = len(order)
        Pn = tileprep(0)
        run_lockstep([prepgen(0, Pn), prepgen(1, Pn)])
        for i in range(nt):
            Pc = Pn
            gens = [stategen(0, Pc), stategen(1, Pc)]
            if i + 1 < nt:
                Pn = tileprep(i + 1)
                gens += [prepgen(0, Pn), prepgen(1, Pn)]
            run_lockstep(gens)
            outputs(Pc)


def rwkv_out_proj(C, l, s):
    nc, S = C.nc, C.S
    from contextlib import ExitStack
    ps, psb = C.ps, C.ps_b
    S.barrier()
    with ExitStack() as st:
        def sbt(name, shape, dt):
            return st.enter_context(nc.sbuf_tensor(un(name), shape, dt))
        WO = sbt("rWO", [128, KT, KT, 128], BF16)
        YB = [sbt("rYB%d" % i, [128, KT, 512], BF16) for i in range(2)]
        b_WO = Buf()
        b_YB = [Buf(), Buf()]
        for o in range(KT):
            S.dma("pool", WO[:, o], C.rw_wout[o], (), [b_WO])
        it = 0
        for c, (t0, n, is_ctx) in enumerate(CHUNKS):
            col = NSEQ if is_ctx else s
            yb, byb = YB[c % 2], b_YB[c % 2]
            S.dma("sp", yb[:, :, 0:n], C.Yscr[0:KT, :, t0:t0 + n].rearrange("f p t -> p f t"), [C.Yscr_b], [byb])
            for o in range(KT):
                po = it % 2
                it += 1
                mm(S, ps[po][:, :n], [(WO[:, o, f, :], yb[:, f, 0:n]) for f in range(KT)], [b_WO, byb], [psb[po]])
                S.op("dve", lambda e, po=po, n=n, o=o, t0=t0, col=col: e.scalar_tensor_tensor(
                    out=C.R[:, o, t0:t0 + n], in0=ps[po][:, :n], scalar=C.MODS[:, l, 2 * 8 + o, col:col + 1],
                    in1=C.R[:, o, t0:t0 + n], op0=ALU.mult, op1=ALU.add),
                    [psb[po], C.MODS_b, C.Rb[o][c]], [C.Rb[o][c]])
        S.barrier()


def tile_w(w, ncol_tiles=None):
    K, N = w.shape
    return np.ascontiguousarray(w.reshape(K // 128, 128, N // 128, 128).transpose(2, 1, 0, 3))


def vec_t(v):
    sh = v.shape
    n = sh[-1] // 128
    a = v.reshape(sh[:-1] + (n, 128))
    return np.ascontiguousarray(np.moveaxis(a, -1, 0))


def host_shared(inp):
    sh = {}
    sh["mod_w_t"] = np.stack([tile_w(inp["mod_w"][l]) for l in range(DEPTH)])
    sh["mod_b_t"] = vec_t(inp["mod_b"])
    sh["norm_mix_t"] = vec_t(inp["norm_mix"])
    sh["norm_ffn_t"] = vec_t(inp["norm_ffn"])
    sh["final_norm_t"] = vec_t(inp["final_norm"])
    sh["ffn_w_in_t"] = np.stack([tile_w(inp["ffn_w_in"][l]) for l in range(DEPTH)])
    sh["ffn_w_out_t"] = np.stack([tile_w(inp["ffn_w_out"][l]) for l in range(DEPTH)])
    sh.update(host_gdn(inp))
    sh.update(host_s5(inp))
    sh.update(host_rwkv(inp))
    return sh


def host_rwkv(inp):
    sh = {}
    sh["rw_mu"] = vec_t(inp["rwkv_mu"][0])
    sh["rw_wrkv_t"] = np.stack([tile_w(inp["rwkv_w_rkv"][0, i]) for i in range(3)])
    w1cat = np.concatenate([inp["rwkv_w1"][0, 0], inp["rwkv_w1"][0, 1]], axis=1)
    a1cat = np.concatenate([inp["rwkv_a1"][0, 0], inp["rwkv_a1"][0, 1]], axis=1)
    sh["rw_l1_t"] = np.stack([tile_w(w1cat)[0], tile_w(a1cat)[0], tile_w(inp["rwkv_g1"][0])[0]])
    w2cat = np.concatenate([inp["rwkv_w2"][0, 0], inp["rwkv_w2"][0, 1]], axis=0)
    a2cat = np.concatenate([inp["rwkv_a2"][0, 0], inp["rwkv_a2"][0, 1]], axis=0)
    sh["rw_l2"] = np.ascontiguousarray(np.stack([w2cat, a2cat, inp["rwkv_g2"][0]]))
    vecs = [inp["rwkv_w0"][0, 0], inp["rwkv_w0"][0, 1], inp["rwkv_a0"][0, 0], inp["rwkv_a0"][0, 1],
            inp["rwkv_k_k"][0], inp["rwkv_k_a"][0], inp["rwkv_r_k"][0].reshape(-1), inp["rwkv_ln_w"][0], inp["rwkv_ln_b"][0]]
    sh["rw_vecs"] = vec_t(np.stack(vecs))
    sh["rw_wout_t"] = tile_w(inp["rwkv_w_out"][0])
    jj = np.arange(128)[:, None]
    tt = np.arange(128)[None, :]
    same = (jj // 64) == (tt // 64)
    m = np.zeros((2, 128, 512), np.float32)
    for d, (st_, inc_) in enumerate((((jj < tt), (jj <= tt)), ((jj > tt), (jj >= tt)))):
        m[d, :, 0:128] = same & st_
        m[d, :, 128:256] = same & st_
        m[d, :, 256:384] = same & inc_
        m[d, :, 384:512] = same & inc_
    sh["c_mask4"] = m
    return sh


def host_s5(inp):
    sh = {}
    lr, li = inp["s5_lambda_re"][0], inp["s5_lambda_im"][0]
    lam = np.stack([lr, li], 0)
    sh["s5_lam"] = np.ascontiguousarray(lam.transpose(3, 0, 1, 2).reshape(64, 2, 128))
    sh["s5_dt"] = np.ascontiguousarray(np.broadcast_to(inp["s5_log_dt"][0].reshape(1, 128), (64, 128)))
    B = np.stack([inp["s5_b_re"][0], inp["s5_b_im"][0]], 0)
    sh["s5_B"] = np.ascontiguousarray(B.transpose(3, 0, 1, 2, 4).reshape(64, 2, 128, 16))
    Cm = np.stack([inp["s5_c_re"][0], inp["s5_c_im"][0]], 0)
    sh["s5_C"] = np.ascontiguousarray(Cm.transpose(4, 0, 1, 2, 3).reshape(64, 2, 128, 16))
    sh["s5_dskip"] = vec_t(inp["s5_d"][0])
    sh["s5_wglu_t"] = tile_w(inp["s5_w_glu"][0])
    sh["s5_bglu_t"] = vec_t(inp["s5_b_glu"][0])
    sel = np.zeros((128, 4, 8, 128), np.float32)
    for r in range(128):
        q, c = (r % 64) // 16, r % 16
        for j in range(8):
            sel[r, q, j, 16 * j + c] = 1.0
    sh["c_sel4"] = sel
    jj = np.arange(128)[:, None] // 16
    tt = np.arange(128)[None, :] // 16
    sh["c_maskz"] = np.stack([(jj <= tt), (jj >= tt)]).astype(np.float32)
    return sh


def host_gdn(inp):
    sh = {}
    NG = inp["gdn_w_qkvz"].shape[0]
    sh["g_wqkvz_t"] = np.stack([tile_w(inp["gdn_w_qkvz"][j]) for j in range(NG)])
    cv = inp["gdn_conv"].reshape(NG, 5, 32, 128)
    sh["g_conv_t"] = np.ascontiguousarray(cv.transpose(0, 3, 2, 1))
    wab = inp["gdn_w_ab"]
    cat = np.concatenate([wab[:, 0], wab[:, 1]], axis=-1)
    cat = np.concatenate([cat, cat], axis=-1)
    sh["g_wab_t"] = np.ascontiguousarray(cat.reshape(NG, KT, 128, 128).transpose(0, 2, 1, 3))
    par = np.zeros((NG, 128, 8), np.float32)
    for c in range(128):
        d = (c % 64) // 32
        kind = ((c % 64) % 32) // 16
        hv = c % 16
        par[:, c, 0] = 1.0 if kind == 0 else -1.0
        if kind == 0:
            par[:, c, 1] = inp["gdn_dt_bias"][:, d, hv]
            par[:, c, 2] = inp["gdn_a_log"][:, d, hv]
        par[:, c, 3] = 1.0 if d == 1 else 0.0
        par[:, c, 4] = -1.0 if c >= 64 else 0.0
    sh["g_par"] = par
    sh["g_norm_t"] = np.ascontiguousarray(inp["gdn_norm"][:, :, None])
    sh["g_wout_t"] = np.stack([tile_w(inp["gdn_w_out"][j]) for j in range(NG)])
    sh["c_ident"] = np.eye(128, dtype=np.float32)
    jj = np.arange(128)[:, None]
    tt = np.arange(128)[None, :]
    same = (jj // 64) == (tt // 64)
    m = np.zeros((2, 128, 256), np.float32)
    m[0, :, 0:128] = same & (jj < tt)
    m[0, :, 128:256] = same & (jj <= tt)
    m[1, :, 0:128] = same & (jj > tt)
    m[1, :, 128:256] = same & (jj >= tt)
    sh["c_mask"] = m
    return sh


def host_core(inp, core, nseq=NSEQ):
    b0 = core * NSEQ
    m = {}
    xs = []
    for s in range(nseq):
        full = np.concatenate([inp["ctx"][b0 + s], inp["x"][b0 + s]], axis=0)
        xs.append(full.T.reshape(KT, 128, T))
    m["xT"] = np.ascontiguousarray(np.stack(xs))
    cc = np.concatenate([inp["c"][b0:b0 + NSEQ], inp["c_ctx"][None, :]], axis=0)
    m["cT"] = np.ascontiguousarray(cc.T.reshape(KT, 128, NSEQ + 1).transpose(1, 0, 2))
    return m


_CACHE = {}


def kernel(**inp):
    inp = {k: np.asarray(v, dtype=np.float32) for k, v in inp.items()}
    if "nc" not in _CACHE:
        probe = build_program()
        _UID[0] = 0
        _CACHE["nc"] = build_program(single=probe._sched_record)
    nc = _CACHE["nc"]
    shared = host_shared(inp)
    in_maps = []
    for core in range(NCORES):
        m = dict(shared)
        m.update(host_core(inp, core))
        in_maps.append(m)
    res = run_bass_kernel_spmd(nc, in_maps, core_ids=list(range(NCORES)))
    out = np.empty((NCORES * NSEQ, T_X, D), np.float32)
    for core in range(NCORES):
        o = res.results[core]["outT"]
        for s in range(NSEQ):
            out[core * NSEQ + s] = o[s].reshape(D, T_X).T
    return out
```
